# Optimizing a Trainium2 kernel written in Bass

```python
import math
import jax, jax.numpy as jnp
from jax import lax
import numpy as np

D_MODEL = 2048
BATCH = 1
SEQ = 8192
DEPTH = 1
DEC_BATCH = 16
DEC_SEQ = 2048
PAST_LEN = 128

MIX_WIDTH = D_MODEL
MLA_HEADS = 8
MLA_V_DIM = 128
MLA_NOPE_DIM = 128
MLA_ROPE_DIM = 64
Q_LORA = 512
KV_LORA = 512
MLA_WIDTH = MLA_HEADS * MLA_V_DIM
GMLP_GROUPS = 8
GMLP_GROUP_DIM = 128
GMLP_WIDTH = GMLP_GROUPS * GMLP_GROUP_DIM
CHUNK = 128
IN_WIDTH = Q_LORA + KV_LORA + MLA_ROPE_DIM + 2 * GMLP_WIDTH
D_FF = 5632
Q_BLOCK = 128
ROPE_THETA = 10000.0
EPS = 1e-6

kernel_name = "hybrid_mla_gmlp_macaron_encoder"


def rms_norm(x, g):
    xf = x.astype(jnp.float32)
    y = xf * lax.rsqrt(jnp.mean(xf * xf, axis=-1, keepdims=True) + EPS)
    return (y * g.astype(jnp.float32)).astype(x.dtype)


def swiglu(h, w_gate, w_up, w_down):
    return (jax.nn.silu(h @ w_gate) * (h @ w_up)) @ w_down


def rope_tables(seq):
    inv = 1.0 / (ROPE_THETA ** (jnp.arange(0, MLA_ROPE_DIM, 2, dtype=jnp.float32) / MLA_ROPE_DIM))
    ang = jnp.arange(seq, dtype=jnp.float32)[:, None] * inv[None, :]
    return jnp.cos(ang), jnp.sin(ang)


def apply_rope(x, cos, sin):
    half = x.shape[-1] // 2
    x1 = x[..., :half].astype(jnp.float32)
    x2 = x[..., half:].astype(jnp.float32)
    out = jnp.concatenate([x1 * cos - x2 * sin, x1 * sin + x2 * cos], axis=-1)
    return out.astype(x.dtype)


def mla_attention(q_nope, q_rope, k_nope, k_rope, v):
    B, S, H, _ = q_nope.shape
    nblk = S // Q_BLOCK
    scale = 1.0 / math.sqrt(MLA_NOPE_DIM + MLA_ROPE_DIM)
    qn = jnp.moveaxis(q_nope.reshape(B, nblk, Q_BLOCK, H, MLA_NOPE_DIM), 1, 0)
    qr = jnp.moveaxis(q_rope.reshape(B, nblk, Q_BLOCK, H, MLA_ROPE_DIM), 1, 0)

    def block(args):
        qn_b, qr_b = args
        s = (jnp.einsum('bqhd,bkhd->bhqk', qn_b, k_nope, preferred_element_type=jnp.float32)
             + jnp.einsum('bqhd,bkd->bhqk', qr_b, k_rope, preferred_element_type=jnp.float32))
        p = jax.nn.softmax(s * scale, axis=-1).astype(v.dtype)
        return jnp.einsum('bhqk,bkhd->bqhd', p, v)

    o = lax.map(block, (qn, qr))
    return jnp.moveaxis(o, 0, 1).reshape(B, S, H * MLA_V_DIM)


def chunked_spatial_gating(u, v, g_v, w_s, b_s):
    B, S, _ = u.shape
    n = S // CHUNK
    vn = rms_norm(v.reshape(B, S, GMLP_GROUPS, GMLP_GROUP_DIM),
                  g_v.reshape(GMLP_GROUPS, GMLP_GROUP_DIM))
    vn = vn.reshape(B, n, CHUNK, GMLP_GROUPS, GMLP_GROUP_DIM)
    s = jnp.einsum('gij,bnjgc->bnigc', w_s, vn) + jnp.transpose(b_s)[None, None, :, :, None]
    return u * s.reshape(B, S, GMLP_WIDTH)


def encoder_layer(x, g_ffn1, w1_gate, w1_up, w1_down, g_mix, w_in, g_q, w_q_b, g_kv, w_kv_b,
                  g_v, w_s, b_s, g_out_attn, g_out_gmlp, w_out, g_ffn2, w2_gate, w2_up, w2_down):
    B, S, _ = x.shape
    x = x + 0.5 * swiglu(rms_norm(x, g_ffn1), w1_gate, w1_up, w1_down)

    z = rms_norm(x, g_mix) @ w_in
    o1 = Q_LORA
    o2 = o1 + KV_LORA
    o3 = o2 + MLA_ROPE_DIM
    o4 = o3 + GMLP_WIDTH
    c_q, c_kv, k_rope, u, v = z[..., :o1], z[..., o1:o2], z[..., o2:o3], z[..., o3:o4], z[..., o4:]

    cos, sin = rope_tables(S)
    q = (rms_norm(c_q, g_q) @ w_q_b).reshape(B, S, MLA_HEADS, MLA_NOPE_DIM + MLA_ROPE_DIM)
    q_nope = q[..., :MLA_NOPE_DIM]
    q_rope = apply_rope(q[..., MLA_NOPE_DIM:], cos[None, :, None, :], sin[None, :, None, :])
    kv = (rms_norm(c_kv, g_kv) @ w_kv_b).reshape(B, S, MLA_HEADS, MLA_NOPE_DIM + MLA_V_DIM)
    k_nope = kv[..., :MLA_NOPE_DIM]
    v_att = kv[..., MLA_NOPE_DIM:]
    k_rope = apply_rope(k_rope, cos[None], sin[None])
    o_attn = mla_attention(q_nope, q_rope, k_nope, k_rope, v_att)

    o_gmlp = chunked_spatial_gating(jax.nn.gelu(u), jax.nn.gelu(v), g_v, w_s, b_s)

    o = jnp.concatenate([rms_norm(o_attn, g_out_attn), rms_norm(o_gmlp, g_out_gmlp)], axis=-1)
    x = x + o @ w_out

    x = x + 0.5 * swiglu(rms_norm(x, g_ffn2), w2_gate, w2_up, w2_down)
    return x


def setup_inputs(seed: int = 0) -> dict:
    key = jax.random.key(seed)
    ks = iter(jax.random.split(key, 32))
    f32 = jnp.float32

    def nrm(shape, fan_in):
        return jax.random.normal(next(ks), shape, f32) * (fan_in ** -0.5)

    def gain(shape):
        return 1.0 + 0.01 * jax.random.normal(next(ks), shape, f32)

    L = DEPTH
    H = MLA_HEADS
    return {
        "x_prompt": jax.random.normal(next(ks), (BATCH, SEQ, D_MODEL), f32),
        "x_sample": jax.random.normal(next(ks), (DEC_BATCH, DEC_SEQ, D_MODEL), f32),
        "g_ffn1": gain((L, D_MODEL)),
        "w1_gate": nrm((L, D_MODEL, D_FF), D_MODEL),
        "w1_up": nrm((L, D_MODEL, D_FF), D_MODEL),
        "w1_down": nrm((L, D_FF, D_MODEL), D_FF),
        "g_mix": gain((L, D_MODEL)),
        "w_in": nrm((L, D_MODEL, IN_WIDTH), D_MODEL),
        "g_q": gain((L, Q_LORA)),
        "w_q_b": nrm((L, Q_LORA, H * (MLA_NOPE_DIM + MLA_ROPE_DIM)), Q_LORA),
        "g_kv": gain((L, KV_LORA)),
        "w_kv_b": nrm((L, KV_LORA, H * (MLA_NOPE_DIM + MLA_V_DIM)), KV_LORA),
        "g_v": gain((L, GMLP_WIDTH)),
        "w_s": nrm((L, GMLP_GROUPS, CHUNK, CHUNK), CHUNK),
        "b_s": 1.0 + 0.01 * jax.random.normal(next(ks), (L, GMLP_GROUPS, CHUNK), f32),
        "g_out_attn": gain((L, MLA_WIDTH)),
        "g_out_gmlp": gain((L, GMLP_WIDTH)),
        "w_out": nrm((L, MIX_WIDTH, D_MODEL), MIX_WIDTH),
        "g_ffn2": gain((L, D_MODEL)),
        "w2_gate": nrm((L, D_MODEL, D_FF), D_MODEL),
        "w2_up": nrm((L, D_MODEL, D_FF), D_MODEL),
        "w2_down": nrm((L, D_FF, D_MODEL), D_FF),
        "g_final": gain((D_MODEL,)),
    }


def _trunk(x, g_ffn1, w1_gate, w1_up, w1_down, g_mix, w_in, g_q, w_q_b, g_kv, w_kv_b,
           g_v, w_s, b_s, g_out_attn, g_out_gmlp, w_out, g_ffn2, w2_gate, w2_up, w2_down, g_final):
    for l in range(DEPTH):
        x = encoder_layer(x, g_ffn1[l], w1_gate[l], w1_up[l], w1_down[l], g_mix[l], w_in[l],
                          g_q[l], w_q_b[l], g_kv[l], w_kv_b[l], g_v[l], w_s[l], b_s[l],
                          g_out_attn[l], g_out_gmlp[l], w_out[l], g_ffn2[l],
                          w2_gate[l], w2_up[l], w2_down[l])
    return rms_norm(x, g_final)


def reference(x_prompt, x_sample, g_ffn1, w1_gate, w1_up, w1_down, g_mix, w_in, g_q, w_q_b,
              g_kv, w_kv_b, g_v, w_s, b_s, g_out_attn, g_out_gmlp, w_out, g_ffn2,
              w2_gate, w2_up, w2_down, g_final):
    y_prompt = _trunk(x_prompt, g_ffn1, w1_gate, w1_up, w1_down, g_mix, w_in, g_q, w_q_b,
                      g_kv, w_kv_b, g_v, w_s, b_s, g_out_attn, g_out_gmlp, w_out, g_ffn2,
                      w2_gate, w2_up, w2_down, g_final)
    y_sample = _trunk(x_sample, g_ffn1, w1_gate, w1_up, w1_down, g_mix, w_in, g_q, w_q_b,
                      g_kv, w_kv_b, g_v, w_s, b_s, g_out_attn, g_out_gmlp, w_out, g_ffn2,
                      w2_gate, w2_up, w2_down, g_final)
    return (y_prompt, y_sample)
```

```python
import math
from contextlib import ExitStack
import numpy as np
import concourse.bass as bass
import concourse.mybir as mybir
from concourse.bass_utils import run_bass_kernel_spmd

AF = mybir.ActivationFunctionType
ALU = mybir.AluOpType
AX = mybir.AxisListType
F32 = mybir.dt.float32
BF16 = mybir.dt.bfloat16
EPS = 1e-6
ROPE_THETA = 10000.0
SEM_LIMIT = 30000


class Sem:
    def __init__(self, handle, name, kind="dma"):
        self.h = handle
        self.name = name
        self.count = 0
        self.kind = kind


class Buf:
    __slots__ = ("name", "w", "r")

    def __init__(self, name):
        self.name = name
        self.w = None
        self.r = {}


class Eng:
    def __init__(self, name, sem):
        self.name = name
        self.sem = sem
        self.ops = []
        self.waited = {}
        self.is_pe = name == "pe"


class Prog:
    def __init__(self, nc, stack):
        self.nc = nc
        self.stack = stack
        self.engs = {}
        self.all_sems = []
        self.uid = 0
        for n in ("pe", "act", "dve", "pool", "sp"):
            self.engs[n] = Eng(n, self.new_sem("c_" + n, "eng"))
        self.nops = 0

    def new_sem(self, name, kind="dma"):
        self.uid += 1
        h = self.stack.enter_context(self.nc.semaphore(f"{name}_{self.uid}"))
        s = Sem(h, name, kind)
        self.all_sems.append(s)
        return s

    def sbuf(self, name, shape, dtype, stack=None):
        st = stack if stack is not None else self.stack
        return st.enter_context(self.nc.sbuf_tensor("sb_" + name, list(shape), dtype))

    def psum(self, name, shape, dtype=F32):
        return self.stack.enter_context(self.nc.psum_tensor(name, list(shape), dtype))

    def _needs(self, eng, reads, writes, is_dma):
        need = {}

        def add(tok):
            s, v = tok
            cur = need.get(s)
            if cur is None or cur < v:
                need[s] = v

        own = None if is_dma else eng.sem
        for b in reads:
            if b.w is not None:
                if b.w[0] is own and eng.is_pe:
                    continue
                add(b.w)
        for b in writes:
            if b.w is not None and b.w[0] is not own:
                add(b.w)
            for tok in b.r.values():
                if tok[0] is not own:
                    add(tok)
        out = []
        for s, v in need.items():
            assert v <= s.count, f"wait on a not-yet-emitted op ({s.name} {v}>{s.count}) on {eng.name}: deadlock risk"
            if s.kind == "dma":
                v = max(v, s.count)
            if eng.waited.get(s, 0) >= v:
                continue
            eng.waited[s] = v
            out.append((s, v))
        return out

    @staticmethod
    def _commit(reads, writes, tok):
        for b in writes:
            b.w = tok
            b.r = {}
        s = tok[0]
        for b in reads:
            cur = b.r.get(s)
            if cur is None or cur[1] < tok[1]:
                b.r[s] = tok

    def op(self, engname, fn, reads=(), writes=(), signal=True):
        eng = self.engs[engname]
        if eng.sem.count >= SEM_LIMIT:
            eng.sem = self.new_sem("c_" + engname, "eng")
        waits = self._needs(eng, reads, writes, False)
        sem = eng.sem
        if signal:
            sem.count += 1
            tok = (sem, sem.count)
        else:
            tok = (sem, sem.count + 1)
        self._commit(reads, writes if signal else (), tok)

        def run(e, fn=fn, waits=waits, signal=signal, sem=sem):
            for s, v in waits:
                e.wait_ge(s.h, v)
            ins = fn(e)
            if signal:
                ins.then_inc(sem.h, 1)

        eng.ops.append(run)
        self.nops += 1

    def dma(self, engname, out_ap, in_ap, sem, reads=(), writes=()):
        eng = self.engs[engname]
        waits = self._needs(eng, reads, writes, True)
        sem.count += 16
        tok = (sem, sem.count)
        self._commit(reads, writes, tok)

        def run(e, waits=waits, sem=sem, out_ap=out_ap, in_ap=in_ap):
            for s, v in waits:
                e.wait_ge(s.h, v)
            e.dma_start(out=out_ap, in_=in_ap).then_inc(sem.h, 16)

        eng.ops.append(run)
        self.nops += 1

    def barrier(self):
        for eng in self.engs.values():
            waits = []
            for s in self.all_sems:
                if s.count > 0 and eng.waited.get(s, 0) < s.count:
                    eng.waited[s] = s.count
                    if s is eng.sem:
                        continue
                    waits.append((s, s.count))

            def run(e, waits=waits):
                for s, v in waits:
                    e.wait_ge(s.h, v)

            eng.ops.append(run)

    def emit(self):
        with self.nc.Block() as block:
            @block.tensor
            def _(e):
                for f in self.engs["pe"].ops:
                    f(e)

            @block.scalar
            def _(e):
                for f in self.engs["act"].ops:
                    f(e)

            @block.vector
            def _(e):
                for f in self.engs["dve"].ops:
                    f(e)

            @block.gpsimd
            def _(e):
                for f in self.engs["pool"].ops:
                    f(e)

            @block.sync
            def _(e):
                for f in self.engs["sp"].ops:
                    f(e)


class Cfg:
    def __init__(self, D=2048, DFF=5632, QL=512, KL=512, H=8, G=8, SA=2048, SP=8192, NCORE=8, QG=1024):
        self.D, self.DFF, self.QL, self.KL, self.H, self.G = D, DFF, QL, KL, H, G
        self.SA, self.SP, self.NCORE, self.QG = SA, SP, NCORE, QG
        self.ND = D // 128
        self.NF = DFF // 128
        self.NQ = QL // 128
        self.NK = KL // 128
        self.GW = G * 128
        self.AW = H * 128
        self.MIXW = self.AW + self.GW
        self.MC = self.MIXW // 128
        self.IN_W = QL + KL + 64 + 2 * self.GW
        self.OWNP = SP // NCORE
        self.NOWN = 2 * SA + self.OWNP
        self.NTOK = 2 * SA + SP
        self.NBD = max(1, D // 512)
        self.WD = D // self.NBD
        self.NFT = self.NF // 4
        self.NQC = self.NF // 4
        self.QW = H * 192
        self.o_ckv = QL
        self.o_kr = QL + KL
        self.o_u = QL + KL + 64
        self.o_v = self.o_u + self.GW
        assert self.NOWN % 512 == 0 and self.NTOK % 512 == 0 and self.MC == self.ND
        assert SA % QG == 0 and self.OWNP % QG == 0 and QG % 512 == 0


CFG_FULL = Cfg()


def _split(start, width, step=512):
    out = []
    o = 0
    while o < width:
        w = min(step, width - o)
        out.append((start + o, w))
        o += w
    return out


def build(cfg):
    c = cfg
    D, ND, NF, NQ, NK, H, G = c.D, c.ND, c.NF, c.NQ, c.NK, c.H, c.G
    nc = bass.Bass("TRN2", target_bir_lowering=False)

    def din(name, shape, dt=F32):
        return nc.dram_tensor(name, list(shape), dt, kind="ExternalInput").ap()

    xall = din("xall", [c.NTOK, D])
    w_g = [din("w1g", [D, c.DFF]), din("w2g", [D, c.DFF])]
    w_u = [din("w1u", [D, c.DFF]), din("w2u", [D, c.DFF])]
    w_d = [din("w1d", [c.DFF, D]), din("w2d", [c.DFF, D])]
    w_in = din("win", [D, c.IN_W])
    w_qb = din("wqb", [c.QL, c.QW])
    w_kvb = din("wkvb", [c.KL, H * 256])
    w_o = din("wout", [c.MIXW, D])
    w_s = din("ws", [G, 128, 128])
    NG = 3 * ND + NQ + NK + c.MC
    gfm_d = din("gfm", [128, NG])
    gv_d = din("gvrow", [1, c.GW])
    gfin_d = din("gfinrow", [1, D])
    bsT_d = din("bsT", [128, G])
    ropeC_d = din("ropeC", [64, c.NTOK])
    ropeS_d = din("ropeS", [64, c.NTOK])
    ident_d = din("ident", [128, 128])
    y = nc.dram_tensor("y", [c.NOWN, D], F32, kind="ExternalOutput").ap()

    wg_s = [nc.dram_tensor(f"wgs{l}", [c.NFT, 128, ND, 512], BF16) for l in range(2)]
    wu_s = [nc.dram_tensor(f"wus{l}", [c.NFT, 128, ND, 512], BF16) for l in range(2)]
    wd_s = [nc.dram_tensor(f"wds{l}", [c.NBD, 4, 128, c.NQC, c.WD], BF16) for l in range(2)]
    win_s = nc.dram_tensor("wins", [128, ND, c.IN_W + 64], BF16)
    wq_s = nc.dram_tensor("wqs", [128, NQ, c.QW + H * 64], BF16)
    wk_s = nc.dram_tensor("wks", [128, NK, c.AW], BF16)
    wv_s = nc.dram_tensor("wvs", [128, NK, c.AW], BF16)
    wo_s = nc.dram_tensor("wos", [c.NBD, 128, c.MC, c.WD], BF16)
    x1_s = nc.dram_tensor("x1s", [c.NOWN, D], F32)
    h2_s = nc.dram_tensor("h2s", [c.NTOK, D], BF16)
    cqT_s = nc.dram_tensor("cqTs", [128, NQ, c.NOWN], BF16)
    ckT_s = nc.dram_tensor("ckTs", [128, NK, c.NTOK], BF16)
    krT_s = nc.dram_tensor("krTs", [64, c.NTOK], BF16)
    hgT_s = nc.dram_tensor("hgTs", [128, G, c.NOWN], BF16)
    hoT_s = nc.dram_tensor("hoTs", [128, H, c.NOWN], BF16)
    b_wscr = Buf("wscr")
    b_x1s, b_h2s, b_cqTs, b_ckTs, b_krTs, b_hgTs, b_hoTs, b_y = (Buf(n) for n in (
        "x1s", "h2s", "cqTs", "ckTs", "krTs", "hgTs", "hoTs", "y"))

    with ExitStack() as st:
        P = Prog(nc, st)
        banks = [P.psum(f"bk{i}", [128, 512], F32) for i in range(8)]
        bb = [Buf(f"bk{i}") for i in range(8)]

        def bf(i):
            return banks[i][:].bitcast(BF16)

        ident_f = P.sbuf("ident_f", [128, 128], F32)
        ident = P.sbuf("ident_b", [128, 128], BF16)
        ones_b = P.sbuf("ones_b", [128, 128], BF16)
        epsT = P.sbuf("epsT", [128, 1], F32)
        gfm = P.sbuf("gfm", [128, NG], F32)
        b_const = Buf("const")
        s_const = P.new_sem("s_const")
        P.dma("sp", ident_f[:], ident_d, s_const, writes=[b_const])
        P.dma("sp", gfm[:], gfm_d, s_const, writes=[b_const])
        b_ident = Buf("ident")
        P.op("dve", lambda e: e.tensor_copy(ident[:], ident_f[:]), reads=[b_const], writes=[b_ident])
        P.op("dve", lambda e: e.memset(ones_b[:], 1.0), writes=[b_ident])
        P.op("dve", lambda e: e.memset(epsT[:], EPS), writes=[b_ident])
        o_g1, o_gm, o_gq, o_gk, o_go, o_g2 = 0, ND, 2 * ND, 2 * ND + NQ, 2 * ND + NQ + NK, 2 * ND + NQ + NK + c.MC

        def copy(engname, out, in_, reads, writes):
            if engname == "act":
                P.op("act", lambda e: e.copy(out, in_), reads=reads, writes=writes)
            else:
                P.op(engname, lambda e: e.tensor_copy(out, in_), reads=reads, writes=writes)

        rr = {"n": 0}

        def alt(*names):
            rr["n"] += 1
            return names[rr["n"] % len(names)]

        STG = 6144
        STGB = 4096
        cur = {"stg": STG}

        def conv_jobs(src, dst, A, B, goff=None):
            a_step = max(1, cur["stg"] // B)
            out = []
            for a0 in range(0, A, a_step):
                an = min(a_step, A - a0)
                out.append((src[:, a0:a0 + an, :], dst[:, a0:a0 + an, :], an, B, None if goff is None else goff + a0))
            return out

        def fm(w, c0, cw):
            return w[:, c0:c0 + cw].rearrange("(k p) f -> p k f", p=128)

        def ffn_jobs(l):
            goff = o_g1 if l == 0 else o_g2
            jobs = []
            for j in range(c.NFT):
                jobs += conv_jobs(fm(w_g[l], j * 512, 512), wg_s[l][j], ND, 512, goff)
                jobs += conv_jobs(fm(w_u[l], j * 512, 512), wu_s[l][j], ND, 512, goff)
            for n in range(c.NBD):
                for q in range(4):
                    src = w_d[l][:, n * c.WD:(n + 1) * c.WD].rearrange("(c p) f -> p c f", p=128)
                    jobs += conv_jobs(src[:, q * c.NQC:(q + 1) * c.NQC, :], wd_s[l][n, q], c.NQC, c.WD)
            return jobs

        def other_jobs():
            jobs = []
            jobs += conv_jobs(fm(w_in, 0, c.IN_W), win_s[:, :, 0:c.IN_W], ND, c.IN_W, o_gm)
            jobs += conv_jobs(fm(w_in, c.o_kr + 32, 32), win_s[:, :, c.IN_W:c.IN_W + 32], ND, 32, o_gm)
            jobs += conv_jobs(fm(w_in, c.o_kr, 32), win_s[:, :, c.IN_W + 32:c.IN_W + 64], ND, 32, o_gm)
            jobs += conv_jobs(fm(w_qb, 0, c.QW), wq_s[:, :, 0:c.QW], NQ, c.QW, o_gq)
            for h in range(H):
                jobs += conv_jobs(fm(w_qb, h * 192 + 160, 32), wq_s[:, :, c.QW + h * 64:c.QW + h * 64 + 32], NQ, 32, o_gq)
                jobs += conv_jobs(fm(w_qb, h * 192 + 128, 32), wq_s[:, :, c.QW + h * 64 + 32:c.QW + h * 64 + 64], NQ, 32, o_gq)
                jobs += conv_jobs(fm(w_kvb, h * 256, 128), wk_s[:, :, h * 128:(h + 1) * 128], NK, 128, o_gk)
                jobs += conv_jobs(fm(w_kvb, h * 256 + 128, 128), wv_s[:, :, h * 128:(h + 1) * 128], NK, 128, o_gk)
            for n in range(c.NBD):
                jobs += conv_jobs(fm(w_o, n * c.WD, c.WD), wo_s[n], c.MC, c.WD, o_go)
            return jobs

        def run_job(job, stg_t, cvo_t, b_s, b_c, s_s, s_c, ldq, stq, eng):
            src, dst, an, B, goff = job
            n = an * B
            sv = stg_t[:, 0:n].rearrange("p (a b) -> p a b", a=an)
            ov = cvo_t[:, 0:n].rearrange("p (a b) -> p a b", a=an)
            P.dma(ldq, sv, src, s_s, writes=[b_s])
            if goff is not None:
                gb = gfm[:, goff:goff + an].unsqueeze(2).broadcast_to([128, an, B])
                P.op(eng, lambda e, ov=ov, sv=sv, gb=gb: e.tensor_tensor(ov, sv, gb, ALU.mult),
                     reads=[b_s, b_const], writes=[b_c])
            else:
                P.op(eng, lambda e, ov=ov, sv=sv: e.tensor_copy(ov, sv), reads=[b_s], writes=[b_c])
            P.dma(stq, dst, ov, s_c, reads=[b_c], writes=[b_wscr])

        with ExitStack() as ph:
            stg = [P.sbuf(f"stg{i}", [128, STG], F32, ph) for i in range(2)]
            cvo = [P.sbuf(f"cvo{i}", [128, STG], BF16, ph) for i in range(2)]
            b_stg = [Buf(f"stg{i}") for i in range(2)]
            b_cvo = [Buf(f"cvo{i}") for i in range(2)]
            s_stg = [P.new_sem(f"s_stg{i}") for i in range(2)]
            s_cvo = [P.new_sem(f"s_cvo{i}") for i in range(2)]
            fg_jobs = ffn_jobs(0)
            if getattr(cfg, 'bg_convert', True):
                cur["stg"] = STGB
                bg_jobs = other_jobs() + ffn_jobs(1)
                cur["stg"] = STG
            else:
                fg_jobs = fg_jobs + other_jobs() + ffn_jobs(1)
                bg_jobs = []
            for ji, job in enumerate(fg_jobs):
                i = ji % 2
                run_job(job, stg[i], cvo[i], b_stg[i], b_cvo[i], s_stg[i], s_cvo[i], "sp", "act",
                        "dve" if ji % 3 else "pool")
        P.barrier()
        if getattr(cfg, 'stop', None) == 'P0':
            P.emit()
            return nc

        def ffn_phase(ph, which):
            NS = 4
            slots = [P.sbuf(f"slot{which}{i}", [128, 8192], BF16, ph) for i in range(NS)]
            b_slot = [Buf(f"slot{i}") for i in range(NS)]
            s_slot = [P.new_sem(f"s_slot{i}") for i in range(NS)]
            sl = {"i": 0}
            hT = P.sbuf(f"hT{which}", [128, ND, 512], BF16, ph)
            b_hT = [Buf(f"hT{t}") for t in range(4)]
            aT = P.sbuf(f"aT{which}", [128, NF, 512], BF16, ph)
            b_aT = [Buf(f"aT{f}") for f in range(NF)]
            xt = [P.sbuf(f"xt{which}{t}", [128, D], F32, ph) for t in range(4)]
            b_xt = [Buf(f"xt{t}") for t in range(4)]
            s_xt = [P.new_sem(f"s_xt{t}") for t in range(4)]
            hn = [P.sbuf(f"hn{which}{i}", [128, D], BF16, ph) for i in range(2)]
            b_hn = [Buf(f"hn{i}") for i in range(2)]
            s_hn = [P.new_sem(f"s_hn{i}") for i in range(2)]
            junk = P.sbuf(f"junk{which}", [128, D], BF16, ph)
            b_junk = Buf("junk")
            sg = [P.sbuf(f"sg{which}{i}", [128, 512], F32, ph) for i in range(2)]
            b_sg = [Buf(f"sg{i}") for i in range(2)]
            st_ = [P.sbuf(f"stat{which}{i}", [128, 4], F32, ph) for i in range(4)]
            b_st = [Buf(f"stat{i}") for i in range(4)]
            cnt = {"hn": 0, "st": 0, "sg": 0, "g": 0, "u": 0, "tp": 0}

            def next_slot():
                i = sl["i"] % NS
                sl["i"] += 1
                return i

            def rstd_of(src_ap, src_buf, width):
                i = cnt["st"] % 4
                cnt["st"] += 1
                s = st_[i]
                P.op("act", lambda e: e.activation(junk[:, 0:width], src_ap, AF.Square, accum_out=s[:, 0:1]),
                     reads=[src_buf], writes=[b_junk, b_st[i]])
                P.op("act", lambda e: e.activation(s[:, 1:2], s[:, 0:1], AF.Sqrt, bias=epsT[:, 0:1], scale=1.0 / width),
                     reads=[b_st[i], b_ident], writes=[b_st[i]])
                P.op("dve", lambda e: e.reciprocal(s[:, 2:3], s[:, 1:2]), reads=[b_st[i]], writes=[b_st[i]])
                return s[:, 2:3], b_st[i]

            def norm_tile(t, store_h2=None):
                rstd, b_r = rstd_of(xt[t][:], b_xt[t], D)
                i = cnt["hn"] % 2
                cnt["hn"] += 1
                P.op("dve", lambda e: e.tensor_scalar_mul(hn[i][:], xt[t][:], rstd),
                     reads=[b_xt[t], b_r], writes=[b_hn[i]])
                if store_h2 is not None:
                    P.dma("pool", store_h2, hn[i][:], s_hn[i], reads=[b_hn[i]], writes=[b_h2s])
                    return
                transpose_into(hn[i], b_hn[i], ND, hT, b_hT[t], t)

            def transpose_into(src, b_src, nch, dstT, b_dst, t):
                for k0 in range(0, nch, 4):
                    kn = min(4, nch - k0)
                    bi = 4 + cnt["tp"] % 4
                    cnt["tp"] += 1
                    for j in range(kn):
                        P.op("pe", lambda e, j=j, bi=bi, k0=k0: e.transpose(
                            bf(bi)[:, j * 128:(j + 1) * 128], src[:, (k0 + j) * 128:(k0 + j + 1) * 128], ident[:]),
                            reads=[b_src, b_ident], writes=[bb[bi]], signal=(j == kn - 1))
                    ov = dstT[:, k0:k0 + kn, t * 128:(t + 1) * 128]
                    iv = bf(bi)[:, 0:kn * 128].rearrange("p (a b) -> p a b", a=kn)
                    copy(alt("dve", "act"), ov, iv, [bb[bi]], [b_dst])

            def ffn(l):
                for t in range(4):
                    norm_tile(t)
                for j in range(c.NFT):
                    sg_i = next_slot()
                    su_i = next_slot()
                    gv_ = slots[sg_i][:, 0:ND * 512].rearrange("p (k f) -> p k f", k=ND)
                    uv_ = slots[su_i][:, 0:ND * 512].rearrange("p (k f) -> p k f", k=ND)
                    P.dma("sp", gv_, wg_s[l][j], s_slot[sg_i], reads=[b_wscr], writes=[b_slot[sg_i]])
                    P.dma("sp", uv_, wu_s[l][j], s_slot[su_i], reads=[b_wscr], writes=[b_slot[su_i]])
                    for cc in range(4):
                        f = j * 4 + cc
                        bg = cnt["g"] % 2
                        cnt["g"] += 1
                        bu = 2 + cnt["u"] % 2
                        cnt["u"] += 1
                        for k in range(ND):
                            P.op("pe", lambda e, k=k, cc=cc, bg=bg, gv_=gv_: e.matmul(
                                banks[bg][:], gv_[:, k, cc * 128:(cc + 1) * 128], hT[:, k, :],
                                start=(k == 0), stop=(k == ND - 1)),
                                reads=[b_slot[sg_i]] + b_hT, writes=[bb[bg]], signal=(k == ND - 1))
                        for k in range(ND):
                            P.op("pe", lambda e, k=k, cc=cc, bu=bu, uv_=uv_: e.matmul(
                                banks[bu][:], uv_[:, k, cc * 128:(cc + 1) * 128], hT[:, k, :],
                                start=(k == 0), stop=(k == ND - 1)),
                                reads=[b_slot[su_i]] + b_hT, writes=[bb[bu]], signal=(k == ND - 1))
                        si = cnt["sg"] % 2
                        cnt["sg"] += 1
                        P.op("act", lambda e, si=si, bg=bg: e.activation(sg[si][:], banks[bg][:], AF.Silu),
                             reads=[bb[bg]], writes=[b_sg[si]])
                        P.op("dve", lambda e, si=si, bu=bu, f=f: e.tensor_tensor(
                            aT[:, f, :], banks[bu][:], sg[si][:], ALU.mult),
                            reads=[bb[bu], b_sg[si]], writes=[b_aT[f]])
                for n in range(c.NBD):
                    for q in range(4):
                        s_i = next_slot()
                        dv_ = slots[s_i][:, 0:c.NQC * c.WD].rearrange("p (c f) -> p c f", c=c.NQC)
                        P.dma("sp", dv_, wd_s[l][n, q], s_slot[s_i], reads=[b_wscr], writes=[b_slot[s_i]])
                        for t in range(4):
                            bi = 4 + t
                            for cc in range(c.NQC):
                                f = q * c.NQC + cc
                                first = (q == 0 and cc == 0)
                                last = (q == 3 and cc == c.NQC - 1)
                                P.op("pe", lambda e, t=t, cc=cc, f=f, bi=bi, first=first, last=last, dv_=dv_: e.matmul(
                                    banks[bi][:, 0:c.WD], aT[:, f, t * 128:(t + 1) * 128], dv_[:, cc, :],
                                    start=first, stop=last),
                                    reads=[b_slot[s_i], b_aT[f]], writes=[bb[bi]], signal=(last or cc == c.NQC - 1))
                    for t in range(4):
                        xs = xt[t][:, n * c.WD:(n + 1) * c.WD]
                        P.op("dve", lambda e, t=t, xs=xs: e.scalar_tensor_tensor(
                            xs, banks[4 + t][:, 0:c.WD], 0.5, xs, ALU.mult, ALU.add),
                            reads=[bb[4 + t], b_xt[t]], writes=[b_xt[t]])

            return dict(slots=slots, b_slot=b_slot, s_slot=s_slot, next_slot=next_slot, hT=hT, b_hT=b_hT,
                        xt=xt, b_xt=b_xt, s_xt=s_xt, norm_tile=norm_tile, ffn=ffn, rstd_of=rstd_of,
                        transpose_into=transpose_into, cnt=cnt)

        with ExitStack() as ph:
            F = ffn_phase(ph, "a")
            xt, b_xt, s_xt = F["xt"], F["b_xt"], F["s_xt"]
            if bg_jobs:
                bstg = P.sbuf("bstg", [128, STGB], F32, ph)
                bcvo = P.sbuf("bcvo", [128, STGB], BF16, ph)
                b_bstg, b_bcvo = Buf("bstg"), Buf("bcvo")
                s_bstg, s_bcvo = P.new_sem("s_bstg"), P.new_sem("s_bcvo")
            nblk = c.NTOK // 512
            jpos = 0
            for b in range(nblk):
                tok0 = b * 512
                own = tok0 < c.NOWN
                for t in range(4):
                    P.dma("pool", xt[t][:], xall[tok0 + t * 128:tok0 + (t + 1) * 128, :], s_xt[t], writes=[b_xt[t]])
                nb_left = max(1, nblk - 1 - b)
                take = len(bg_jobs) - jpos if b >= nblk - 2 else -(-(len(bg_jobs) - jpos) // nb_left)
                for job in bg_jobs[jpos:jpos + take]:
                    run_job(job, bstg, bcvo, b_bstg, b_bcvo, s_bstg, s_bcvo, "pool", "pool", "pool")
                jpos += take
                F["ffn"](0)
                for t in range(4):
                    r0 = tok0 + t * 128
                    if own:
                        P.dma("pool", x1_s[r0:r0 + 128, :], xt[t][:], s_xt[t], reads=[b_xt[t]], writes=[b_x1s])
                    F["norm_tile"](t, store_h2=h2_s[r0:r0 + 128, :])
        P.barrier()
        if getattr(cfg, 'stop', None) == 'P1a':
            P.emit()
            return nc

        with ExitStack() as ph:
            winT = P.sbuf("winT", [128, ND, c.IN_W + 64], BF16, ph)
            b_win = Buf("winT")
            s_win = P.new_sem("s_win")
            for k0 in range(0, ND, 4):
                P.dma("sp", winT[:, k0:k0 + 4, :], win_s[:, k0:k0 + 4, :], s_win, reads=[b_wscr], writes=[b_win])
            sqv = P.sbuf("sqv", [128, c.GW], F32, ph)
            b_sqv = Buf("sqv")
            wsf = sqv[:].rearrange("p (g c) -> p g c", g=G)
            wsb = P.sbuf("wsb", [128, G, 128], BF16, ph)
            wsT = P.sbuf("wsT", [128, G, 128], BF16, ph)
            gvB = P.sbuf("gvB", [128, c.GW], F32, ph)
            bsB = P.sbuf("bsB", [128, c.GW], F32, ph)
            bsT = P.sbuf("bsTt", [128, G], F32, ph)
            b_gc = Buf("gconst")
            s_gc = P.new_sem("s_gc")
            P.dma("sp", wsf, w_s.rearrange("g i j -> i g j"), s_gc, writes=[b_gc, b_sqv])
            P.dma("sp", gvB[:], gv_d.partition_broadcast(128), s_gc, writes=[b_gc])
            P.dma("sp", bsT[:], bsT_d, s_gc, writes=[b_gc])
            b_gc2 = Buf("gconst2")
            P.op("dve", lambda e: e.tensor_copy(wsb[:], wsf), reads=[b_gc, b_sqv], writes=[b_gc2])
            P.op("dve", lambda e: e.tensor_copy(bsB[:].rearrange("p (g c) -> p g c", g=G),
                                                bsT[:].unsqueeze(2).broadcast_to([128, G, 128])),
                 reads=[b_gc], writes=[b_gc2])
            b_wsT = Buf("wsT")
            for g in range(G):
                bi = 6 + g % 2
                P.op("pe", lambda e, g=g, bi=bi: e.transpose(bf(bi)[:, 0:128], wsb[:, g, :], ident[:]),
                     reads=[b_gc2, b_ident], writes=[bb[bi]])
                copy("dve", wsT[:, g, :], bf(bi)[:, 0:128], [bb[bi]], [b_wsT])

            h2tm = [P.sbuf(f"h2tm{i}", [128, D], BF16, ph) for i in range(2)]
            b_h2tm = [Buf(f"h2tm{i}") for i in range(2)]
            s_h2tm = [P.new_sem(f"s_h2tm{i}") for i in range(2)]
            h2T = [P.sbuf(f"h2T{i}", [128, ND, 512], BF16, ph) for i in range(1)]
            b_h2T = [[Buf(f"h2T{i}_{t}") for t in range(4)] for i in range(1)]
            junk = P.sbuf("junkb", [128, 1024], BF16, ph)
            b_junk = Buf("junkb")
            stt = [P.sbuf(f"sttb{i}", [128, 4], F32, ph) for i in range(8)]
            b_stt = [Buf(f"sttb{i}") for i in range(8)]
            ln = [P.sbuf(f"lnb{i}", [128, 512], BF16, ph) for i in range(4)]
            b_ln = [Buf(f"lnb{i}") for i in range(4)]
            ckT = [P.sbuf(f"ckTb{i}", [128, NK, 512], BF16, ph) for i in range(2)]
            b_ckT = [Buf(f"ckTb{i}") for i in range(2)]
            s_ckT = [P.new_sem(f"s_ckTb{i}") for i in range(2)]
            cqT = [P.sbuf(f"cqTb{i}", [128, NQ, 512], BF16, ph) for i in range(1)] * 2
            b_cqT = [Buf(f"cqTb{i}") for i in range(1)] * 2
            s_cqT = [P.new_sem(f"s_cqTb{i}") for i in range(1)] * 2
            hgT = [P.sbuf(f"hgTb{i}", [128, G, 512], BF16, ph) for i in range(1)] * 2
            b_hgT = [Buf(f"hgTb{i}") for i in range(1)] * 2
            s_hgT = [P.new_sem(f"s_hgTb{i}") for i in range(1)] * 2
            rc = [P.sbuf(f"rcb{i}", [64, 512], F32, ph) for i in range(1)] * 2
            rs = [P.sbuf(f"rsb{i}", [64, 512], F32, ph) for i in range(1)] * 2
            b_rt = [Buf(f"rtb{i}") for i in range(1)] * 2
            s_rt = [P.new_sem(f"s_rtb{i}") for i in range(1)] * 2
            kt1 = P.sbuf("kt1", [64, 512], F32, ph)
            kt2 = P.sbuf("kt2", [64, 512], F32, ph)
            b_kt = Buf("kt")
            krb = [P.sbuf(f"krb{i}", [64, 512], BF16, ph) for i in range(2)]
            b_krb = [Buf(f"krb{i}") for i in range(2)]
            s_krb = [P.new_sem(f"s_krb{i}") for i in range(2)]
            gu = [P.sbuf(f"gu{i}", [128, c.GW], F32, ph) for i in range(2)]
            gvv = [P.sbuf(f"gvv{i}", [128, c.GW], F32, ph) for i in range(2)]
            b_gu = [Buf(f"gu{i}") for i in range(2)]
            b_gvv = [Buf(f"gvv{i}") for i in range(2)]
            ssg = [P.sbuf(f"ssg{i}", [128, 3 * G], F32, ph) for i in range(2)]
            b_ssg = [Buf(f"ssg{i}") for i in range(2)]
            vn = [P.sbuf(f"vn{i}", [128, c.GW], BF16, ph) for i in range(2)]
            b_vn = [Buf(f"vn{i}") for i in range(2)]
            og = [P.sbuf(f"og{i}", [128, c.GW], F32, ph) for i in range(1)] * 2
            b_og = [Buf(f"og{i}") for i in range(1)] * 2
            hgn = [P.sbuf(f"hgn{i}", [128, c.GW], BF16, ph) for i in range(2)]
            b_hgn = [Buf(f"hgn{i}") for i in range(2)]
            cn = {"bk": 0, "tp": 0, "st": 0, "ln": 0}

            def nbank():
                i = cn["bk"] % 6
                cn["bk"] += 1
                return i

            def tbank():
                i = 6 + cn["tp"] % 2
                cn["tp"] += 1
                return i

            def transp(src, b_src, nch, dstT, b_dst, t):
                for k0 in range(0, nch, 4):
                    kn = min(4, nch - k0)
                    bi = tbank()
                    for j in range(kn):
                        P.op("pe", lambda e, j=j, bi=bi, k0=k0: e.transpose(
                            bf(bi)[:, j * 128:(j + 1) * 128], src[:, (k0 + j) * 128:(k0 + j + 1) * 128], ident[:]),
                            reads=[b_src, b_ident], writes=[bb[bi]], signal=(j == kn - 1))
                    ov = dstT[:, k0:k0 + kn, t * 128:(t + 1) * 128]
                    iv = bf(bi)[:, 0:kn * 128].rearrange("p (a b) -> p a b", a=kn)
                    copy(alt("dve", "act"), ov, iv, [bb[bi]], [b_dst])

            def latent(hi, t, col0, width, nch, dstT, b_dst):
                bi = nbank()
                for k in range(ND):
                    P.op("pe", lambda e, k=k, bi=bi: e.matmul(
                        banks[bi][:, 0:width], h2T[hi][:, k, t * 128:(t + 1) * 128], winT[:, k, col0:col0 + width],
                        start=(k == 0), stop=(k == ND - 1)),
                        reads=[b_h2T[hi][t], b_win], writes=[bb[bi]], signal=(k == ND - 1))
                si = cn["st"] % 8
                cn["st"] += 1
                s = stt[si]
                P.op("act", lambda e: e.activation(junk[:, 0:width], banks[bi][:, 0:width], AF.Square, accum_out=s[:, 0:1]),
                     reads=[bb[bi]], writes=[b_junk, b_stt[si]])
                P.op("act", lambda e: e.activation(s[:, 1:2], s[:, 0:1], AF.Sqrt, bias=epsT[:, 0:1], scale=1.0 / width),
                     reads=[b_stt[si], b_ident], writes=[b_stt[si]])
                P.op("dve", lambda e: e.reciprocal(s[:, 2:3], s[:, 1:2]), reads=[b_stt[si]], writes=[b_stt[si]])
                li = cn["ln"] % 4
                cn["ln"] += 1
                P.op("dve", lambda e: e.tensor_scalar_mul(ln[li][:, 0:width], banks[bi][:, 0:width], s[:, 2:3]),
                     reads=[bb[bi], b_stt[si]], writes=[b_ln[li]])
                return lambda: transp(ln[li], b_ln[li], nch, dstT, b_dst, t)

            def gmlp_stage1(hi, t, gi):
                for (col0, w) in _split(c.o_u, c.GW):
                    bi = nbank()
                    for k in range(ND):
                        P.op("pe", lambda e, k=k, bi=bi, col0=col0, w=w: e.matmul(
                            banks[bi][:, 0:w], h2T[hi][:, k, t * 128:(t + 1) * 128], winT[:, k, col0:col0 + w],
                            start=(k == 0), stop=(k == ND - 1)),
                            reads=[b_h2T[hi][t], b_win], writes=[bb[bi]], signal=(k == ND - 1))
                    o0 = col0 - c.o_u
                    P.op("act", lambda e, bi=bi, o0=o0, w=w: e.activation(gu[gi][:, o0:o0 + w], banks[bi][:, 0:w], AF.Gelu_apprx_tanh),
                         reads=[bb[bi]], writes=[b_gu[gi]])
                for (col0, w) in _split(c.o_v, c.GW):
                    bi = nbank()
                    for k in range(ND):
                        P.op("pe", lambda e, k=k, bi=bi, col0=col0, w=w: e.matmul(
                            banks[bi][:, 0:w], h2T[hi][:, k, t * 128:(t + 1) * 128], winT[:, k, col0:col0 + w],
                            start=(k == 0), stop=(k == ND - 1)),
                            reads=[b_h2T[hi][t], b_win], writes=[bb[bi]], signal=(k == ND - 1))
                    o0 = col0 - c.o_v
                    P.op("act", lambda e, bi=bi, o0=o0, w=w: e.activation(gvv[gi][:, o0:o0 + w], banks[bi][:, 0:w], AF.Gelu_apprx_tanh),
                         reads=[bb[bi]], writes=[b_gvv[gi]])
                s = ssg[gi]
                P.op("dve", lambda e: e.tensor_tensor(sqv[:], gvv[gi][:], gvv[gi][:], ALU.mult),
                     reads=[b_gvv[gi]], writes=[b_sqv])
                P.op("dve", lambda e: e.tensor_reduce(s[:, 0:G], sqv[:].rearrange("p (g c) -> p g c", g=G), AX.X, ALU.add),
                     reads=[b_sqv], writes=[b_ssg[gi]])
                P.op("act", lambda e: e.activation(s[:, G:2 * G], s[:, 0:G], AF.Sqrt, bias=epsT[:, 0:1], scale=1.0 / 128),
                     reads=[b_ssg[gi], b_ident], writes=[b_ssg[gi]])
                P.op("dve", lambda e: e.reciprocal(s[:, 2 * G:3 * G], s[:, G:2 * G]), reads=[b_ssg[gi]], writes=[b_ssg[gi]])
                P.op("dve", lambda e: e.tensor_tensor(
                    vn[gi][:].rearrange("p (g c) -> p g c", g=G), gvv[gi][:].rearrange("p (g c) -> p g c", g=G),
                    s[:, 2 * G:3 * G].unsqueeze(2).broadcast_to([128, G, 128]), ALU.mult),
                    reads=[b_gvv[gi], b_ssg[gi]], writes=[b_vn[gi]])

            def gmlp_stage2(t, gi, oi):
                halves = _split(0, c.GW)
                for (c0, w) in halves:
                    bi = nbank()
                    for g in range(c0 // 128, (c0 + w) // 128):
                        P.op("pe", lambda e, g=g, bi=bi, c0=c0: e.matmul(
                            banks[bi][:, g * 128 - c0:(g + 1) * 128 - c0], wsT[:, g, :], vn[gi][:, g * 128:(g + 1) * 128],
                            start=True, stop=True),
                            reads=[b_wsT, b_vn[gi]], writes=[bb[bi]], signal=(g == (c0 + w) // 128 - 1))
                    P.op("dve", lambda e, bi=bi, c0=c0, w=w: e.tensor_tensor(
                        og[gi][:, c0:c0 + w], banks[bi][:, 0:w], gvB[:, c0:c0 + w], ALU.mult),
                        reads=[bb[bi], b_gc], writes=[b_og[gi]])
                P.op("pool", lambda e: e.tensor_tensor(og[gi][:], og[gi][:], bsB[:], ALU.add),
                     reads=[b_og[gi], b_gc2], writes=[b_og[gi]])
                P.op("pool", lambda e: e.tensor_tensor(og[gi][:], og[gi][:], gu[gi][:], ALU.mult),
                     reads=[b_og[gi], b_gu[gi]], writes=[b_og[gi]])
                si = cn["st"] % 8
                cn["st"] += 1
                s = stt[si]
                P.op("act", lambda e: e.activation(junk[:, 0:c.GW], og[gi][:], AF.Square, accum_out=s[:, 0:1]),
                     reads=[b_og[gi]], writes=[b_junk, b_stt[si]])
                P.op("act", lambda e: e.activation(s[:, 1:2], s[:, 0:1], AF.Sqrt, bias=epsT[:, 0:1], scale=1.0 / c.GW),
                     reads=[b_stt[si], b_ident], writes=[b_stt[si]])
                P.op("dve", lambda e: e.reciprocal(s[:, 2:3], s[:, 1:2]), reads=[b_stt[si]], writes=[b_stt[si]])
                P.op("dve", lambda e: e.tensor_scalar_mul(hgn[gi][:], og[gi][:], s[:, 2:3]),
                     reads=[b_og[gi], b_stt[si]], writes=[b_hgn[gi]])
                return lambda: transp(hgn[gi], b_hgn[gi], G, hgT[oi], b_hgT[oi], t)

            gcount = 0
            for b in range(c.NTOK // 512):
                tok0 = b * 512
                own = tok0 < c.NOWN
                hi = 0
                oi = b % 2
                for t in range(2):
                    r0 = tok0 + t * 128
                    P.dma("pool", h2tm[t][:], h2_s[r0:r0 + 128, :], s_h2tm[t], reads=[b_h2s], writes=[b_h2tm[t]])
                P.dma("pool", rc[oi][:], ropeC_d[:, tok0:tok0 + 512], s_rt[oi], writes=[b_rt[oi]])
                P.dma("pool", rs[oi][:], ropeS_d[:, tok0:tok0 + 512], s_rt[oi], writes=[b_rt[oi]])
                pend = []
                for t in range(4):
                    transp(h2tm[t % 2], b_h2tm[t % 2], ND, h2T[hi], b_h2T[hi][t], t)
                    if t + 2 < 4:
                        r0 = tok0 + (t + 2) * 128
                        P.dma("pool", h2tm[t % 2][:], h2_s[r0:r0 + 128, :], s_h2tm[t % 2], reads=[b_h2s], writes=[b_h2tm[t % 2]])
                    nxt = []
                    nxt.append(latent(hi, t, c.o_ckv, c.KL, NK, ckT[oi], b_ckT[oi]))
                    if own:
                        nxt.append(latent(hi, t, 0, c.QL, NQ, cqT[oi], b_cqT[oi]))
                        gi = gcount % 2
                        gcount += 1
                        gmlp_stage1(hi, t, gi)
                        nxt.append(("g", t, gi))
                    for it in pend:
                        if isinstance(it, tuple):
                            pend2 = gmlp_stage2(it[1], it[2], oi)
                            nxt.append(pend2)
                        else:
                            it()
                    pend = nxt
                while pend:
                    nxt = []
                    for it in pend:
                        if isinstance(it, tuple):
                            nxt.append(gmlp_stage2(it[1], it[2], oi))
                        else:
                            it()
                    pend = nxt
                ba, bbk = nbank(), nbank()
                for (bi, coff) in ((ba, c.o_kr), (bbk, c.IN_W)):
                    for k in range(ND):
                        P.op("pe", lambda e, k=k, bi=bi, coff=coff, hi=hi: e.matmul(
                            banks[bi][0:64, :], winT[:, k, coff:coff + 64], h2T[hi][:, k, :],
                            start=(k == 0), stop=(k == ND - 1)),
                            reads=b_h2T[hi] + [b_win], writes=[bb[bi]], signal=(k == ND - 1))
                P.op("dve", lambda e, ba=ba, oi=oi: e.tensor_tensor(kt1[:], banks[ba][0:64, :], rc[oi][:], ALU.mult),
                     reads=[bb[ba], b_rt[oi]], writes=[b_kt])
                P.op("dve", lambda e, bbk=bbk, oi=oi: e.tensor_tensor(kt2[:], banks[bbk][0:64, :], rs[oi][:], ALU.mult),
                     reads=[bb[bbk], b_rt[oi]], writes=[b_kt])
                P.op("dve", lambda e, oi=oi: e.tensor_tensor(krb[oi][:], kt1[:], kt2[:], ALU.add),
                     reads=[b_kt], writes=[b_krb[oi]])
                P.dma("pool", krT_s[:, tok0:tok0 + 512], krb[oi][:], s_krb[oi], reads=[b_krb[oi]], writes=[b_krTs])
                P.dma("pool", ckT_s[:, :, tok0:tok0 + 512], ckT[oi][:], s_ckT[oi], reads=[b_ckT[oi]], writes=[b_ckTs])
                if own:
                    P.dma("pool", cqT_s[:, :, tok0:tok0 + 512], cqT[oi][:], s_cqT[oi], reads=[b_cqT[oi]], writes=[b_cqTs])
                    P.dma("pool", hgT_s[:, :, tok0:tok0 + 512], hgT[oi][:], s_hgT[oi], reads=[b_hgT[oi]], writes=[b_hgTs])
        P.barrier()
        if getattr(cfg, 'stop', None) == 'P1b':
            P.emit()
            return nc

        with ExitStack() as ph:
            wq = P.sbuf("wq", [128, NQ, c.QW + H * 64], BF16, ph)
            wk = P.sbuf("wk", [128, NK, c.AW], BF16, ph)
            wv = P.sbuf("wv", [128, NK, c.AW], BF16, ph)
            b_aw = Buf("attw")
            s_aw = P.new_sem("s_aw")
            P.dma("sp", wq[:], wq_s[:, :, :], s_aw, reads=[b_wscr], writes=[b_aw])
            P.dma("sp", wk[:], wk_s[:, :, :], s_aw, reads=[b_wscr], writes=[b_aw])
            P.dma("sp", wv[:], wv_s[:, :, :], s_aw, reads=[b_wscr], writes=[b_aw])
            SKMAX = max(c.SA, c.SP)
            QG = c.QG
            krT = P.sbuf("krT", [64, SKMAX], BF16, ph)
            b_krT = Buf("krT")
            s_krT = P.new_sem("s_krT")
            cqg = P.sbuf("cqg", [128, NQ, QG], BF16, ph)
            b_cqg = Buf("cqg")
            s_cqg = P.new_sem("s_cqg")
            rcq = P.sbuf("rcq", [64, QG], F32, ph)
            rsq = P.sbuf("rsq", [64, QG], F32, ph)
            b_rq = Buf("rq")
            s_rq = P.new_sem("s_rq")
            KT = P.sbuf("KT", [128, SKMAX], BF16, ph)
            b_KT = Buf("KT")
            V = P.sbuf("V", [128, SKMAX // 128, 128], BF16, ph)
            b_V = Buf("V")
            ckc = [P.sbuf(f"ckc{i}", [128, NK, 512], BF16, ph) for i in range(2)]
            b_ckc = [Buf(f"ckc{i}") for i in range(2)]
            s_ckc = [P.new_sem(f"s_ckc{i}") for i in range(2)]
            QnT = [P.sbuf(f"QnT{i}", [128, 512], BF16, ph) for i in range(2)]
            QrT = [P.sbuf(f"QrT{i}", [64, 512], BF16, ph) for i in range(2)]
            b_Q = [Buf(f"Q{i}") for i in range(2)]
            qt1 = P.sbuf("qt1", [64, 512], F32, ph)
            qt2 = P.sbuf("qt2", [64, 512], F32, ph)
            b_qt = Buf("qt")
            PT = [P.sbuf(f"PT{i}", [128, 512], BF16, ph) for i in range(3)]
            b_PT = [Buf(f"PT{i}") for i in range(3)]
            OT = P.sbuf("OT", [128, H, QG], F32, ph)
            b_OT = [Buf(f"OT{h}") for h in range(H)]
            rl = P.sbuf("rl", [128, 512], F32, ph)
            b_rl = Buf("rl")
            sqo = [P.sbuf(f"sqo{i}", [128, 512], BF16, ph) for i in range(2)]
            b_sqo = [Buf(f"sqo{i}") for i in range(2)]
            rso = P.sbuf("rso", [128, 512], F32, ph)
            b_rso = Buf("rso")
            hoT = [P.sbuf(f"hoT{i}", [128, H, 512], BF16, ph) for i in range(2)]
            b_hoT = [Buf(f"hoT{i}") for i in range(2)]
            s_hoT = [P.new_sem(f"s_hoT{i}") for i in range(2)]
            scale = 1.0 / math.sqrt(192.0)
            ct = {"ck": 0, "q": 0, "s": 0, "pt": 0, "ol": 0, "ho": 0, "sq": 0, "mb": 0}

            def mbk():
                ct["mb"] += 1
                return (2, 7)[ct["mb"] % 2]

            items = []
            for s_ in range(2):
                for qg in range(c.SA // QG):
                    items.append((s_ * c.SA, c.SA, s_ * c.SA + qg * QG))
            for qg in range(c.OWNP // QG):
                items.append((2 * c.SA, c.SP, 2 * c.SA + qg * QG))
            prev_k0 = None
            for (k0, Sk, q0) in items[:getattr(cfg, 'dbg_items', 99)]:
                if k0 != prev_k0:
                    P.dma("pool", krT[:, 0:Sk], krT_s[:, k0:k0 + Sk], s_krT, reads=[b_krTs], writes=[b_krT])
                    prev_k0 = k0
                P.dma("pool", cqg[:], cqT_s[:, :, q0:q0 + QG], s_cqg, reads=[b_cqTs], writes=[b_cqg])
                P.dma("pool", rcq[:], ropeC_d[:, q0:q0 + QG], s_rq, writes=[b_rq])
                P.dma("pool", rsq[:], ropeS_d[:, q0:q0 + QG], s_rq, writes=[b_rq])
                for h in range(getattr(cfg, 'dbg_heads', H)):
                    for kb in range(Sk // 512):
                        ci = ct["ck"] % 2
                        ct["ck"] += 1
                        P.dma("sp", ckc[ci][:], ckT_s[:, :, k0 + kb * 512:k0 + (kb + 1) * 512], s_ckc[ci],
                              reads=[b_ckTs], writes=[b_ckc[ci]])
                        m = mbk()
                        for k in range(NK):
                            P.op("pe", lambda e, k=k, ci=ci, h=h, m=m: e.matmul(
                                banks[m][:], wk[:, k, h * 128:(h + 1) * 128], ckc[ci][:, k, :],
                                start=(k == 0), stop=(k == NK - 1)),
                                reads=[b_aw, b_ckc[ci]], writes=[bb[m]], signal=(k == NK - 1))
                        copy("dve", KT[:, kb * 512:(kb + 1) * 512], banks[m][:], [bb[m]], [b_KT])
                        m = mbk()
                        for j in range(4):
                            for k in range(NK):
                                P.op("pe", lambda e, k=k, j=j, ci=ci, h=h, m=m: e.matmul(
                                    banks[m][:, j * 128:(j + 1) * 128], ckc[ci][:, k, j * 128:(j + 1) * 128],
                                    wv[:, k, h * 128:(h + 1) * 128], start=(k == 0), stop=(k == NK - 1)),
                                    reads=[b_aw, b_ckc[ci]], writes=[bb[m]], signal=(j == 3 and k == NK - 1))
                        copy("act", V[:, kb * 4:(kb + 1) * 4, :], banks[m][:].rearrange("p (a b) -> p a b", a=4),
                             [bb[m]], [b_V])
                    for qb in range(QG // 512 if getattr(cfg, 'dbg_p2', 9) >= 2 else 0):
                        qi = ct["q"] % 2
                        ct["q"] += 1
                        qs = slice(qb * 512, (qb + 1) * 512)
                        m = mbk()
                        for k in range(NQ):
                            P.op("pe", lambda e, k=k, h=h, qs=qs, m=m: e.matmul(
                                banks[m][:], wq[:, k, h * 192:h * 192 + 128], cqg[:, k, qs],
                                start=(k == 0), stop=(k == NQ - 1)),
                                reads=[b_aw, b_cqg], writes=[bb[m]], signal=(k == NQ - 1))
                        P.op("act", lambda e, qi=qi, m=m: e.mul(QnT[qi][:], banks[m][:], scale), reads=[bb[m]], writes=[b_Q[qi]])
                        for (coff, dst, tab) in ((h * 192 + 128, qt1, rcq), (c.QW + h * 64, qt2, rsq)):
                            m = mbk()
                            for k in range(NQ):
                                P.op("pe", lambda e, k=k, coff=coff, qs=qs, m=m: e.matmul(
                                    banks[m][0:64, :], wq[:, k, coff:coff + 64], cqg[:, k, qs],
                                    start=(k == 0), stop=(k == NQ - 1)),
                                    reads=[b_aw, b_cqg], writes=[bb[m]], signal=(k == NQ - 1))
                            P.op("dve", lambda e, dst=dst, tab=tab, qs=qs, m=m: e.scalar_tensor_tensor(
                                dst[:], banks[m][0:64, :], scale, tab[:, qs], ALU.mult, ALU.mult),
                                reads=[bb[m], b_rq], writes=[b_qt])
                        P.op("dve", lambda e, qi=qi: e.tensor_tensor(QrT[qi][:], qt1[:], qt2[:], ALU.add),
                             reads=[b_qt], writes=[b_Q[qi]])
                        oli = ct["ol"] % 2
                        ct["ol"] += 1
                        bo, bl = 3 + 2 * oli, 4 + 2 * oli
                        nkt = Sk // 128
                        pend = None

                        def ol(kt, pi, bo=bo, bl=bl, nkt=nkt):
                            P.op("pe", lambda e: e.matmul(banks[bo][:], V[:, kt, :], PT[pi][:],
                                                          start=(kt == 0), stop=(kt == nkt - 1)),
                                 reads=[b_V, b_PT[pi]], writes=[bb[bo]], signal=(kt == nkt - 1))
                            P.op("pe", lambda e: e.matmul(banks[bl][:], ones_b[:], PT[pi][:],
                                                          start=(kt == 0), stop=(kt == nkt - 1)),
                                 reads=[b_ident, b_PT[pi]], writes=[bb[bl]], signal=(kt == nkt - 1))

                        if getattr(cfg, 'dbg_p2', 9) < 3:
                            continue
                        for kt in range(nkt):
                            si = ct["s"] % 2
                            ct["s"] += 1
                            P.op("pe", lambda e, kt=kt, si=si, qi=qi: e.matmul(
                                banks[si][:], KT[:, kt * 128:(kt + 1) * 128], QnT[qi][:], start=True, stop=False),
                                reads=[b_KT, b_Q[qi]], writes=[bb[si]], signal=False)
                            P.op("pe", lambda e, kt=kt, si=si, qi=qi: e.matmul(
                                banks[si][:], krT[:, kt * 128:(kt + 1) * 128], QrT[qi][:], start=False, stop=True),
                                reads=[b_krT, b_Q[qi]], writes=[bb[si]], signal=True)
                            pi = ct["pt"] % 3
                            ct["pt"] += 1
                            P.op("act", lambda e, si=si, pi=pi: e.activation(PT[pi][:], banks[si][:], AF.Exp),
                                 reads=[bb[si]], writes=[b_PT[pi]])
                            if pend is not None:
                                ol(*pend)
                            pend = (kt, pi)
                        ol(*pend)
                        P.op("dve", lambda e, bl=bl: e.reciprocal(rl[:], banks[bl][:]), reads=[bb[bl]], writes=[b_rl])
                        P.op("dve", lambda e, bo=bo, h=h, qs=qs: e.tensor_tensor(OT[:, h, qs], banks[bo][:], rl[:], ALU.mult),
                             reads=[bb[bo], b_rl], writes=[b_OT[h]])
                for qb in range(QG // 512 if getattr(cfg, 'dbg_p2', 9) >= 4 else 0):
                    qs = slice(qb * 512, (qb + 1) * 512)
                    m = mbk()
                    for h in range(H):
                        qi2 = ct["sq"] % 2
                        ct["sq"] += 1
                        P.op(alt("dve", "pool"), lambda e, h=h, qi2=qi2, qs=qs: e.tensor_tensor(
                            sqo[qi2][:], OT[:, h, qs], OT[:, h, qs], ALU.mult),
                            reads=[b_OT[h]], writes=[b_sqo[qi2]])
                        P.op("pe", lambda e, h=h, qi2=qi2, m=m: e.matmul(banks[m][:], ones_b[:], sqo[qi2][:],
                                                                         start=(h == 0), stop=(h == H - 1)),
                             reads=[b_ident, b_sqo[qi2]], writes=[bb[m]], signal=True)
                    P.op("act", lambda e, m=m: e.activation(rso[:], banks[m][:], AF.Sqrt, bias=epsT[:, 0:1], scale=1.0 / c.AW),
                         reads=[bb[m], b_ident], writes=[b_rso])
                    P.op("dve", lambda e: e.reciprocal(rso[:], rso[:]), reads=[b_rso], writes=[b_rso])
                    oi = ct["ho"] % 2
                    ct["ho"] += 1
                    for h in range(H):
                        P.op(alt("dve", "pool"), lambda e, h=h, oi=oi, qs=qs: e.tensor_tensor(
                            hoT[oi][:, h, :], OT[:, h, qs], rso[:], ALU.mult),
                            reads=[b_OT[h], b_rso], writes=[b_hoT[oi]])
                    P.dma("pool", hoT_s[:, :, q0 + qb * 512:q0 + (qb + 1) * 512], hoT[oi][:], s_hoT[oi],
                          reads=[b_hoT[oi]], writes=[b_hoTs])
        P.barrier()
        if getattr(cfg, 'stop', None) == 'P2':
            P.emit()
            return nc

        with ExitStack() as ph:
            F = ffn_phase(ph, "c")
            xt, b_xt, s_xt, hT, b_hT = F["xt"], F["b_xt"], F["s_xt"], F["hT"], F["b_hT"]
            slots, b_slot, s_slot = F["slots"], F["b_slot"], F["s_slot"]
            gfin = P.sbuf("gfin", [128, D], F32, ph)
            b_gfin = Buf("gfin")
            s_gfin = P.new_sem("s_gfin")
            P.dma("sp", gfin[:], gfin_d.partition_broadcast(128), s_gfin, writes=[b_gfin])
            yo = [P.sbuf(f"yo{i}", [128, D], F32, ph) for i in range(2)]
            b_yo = [Buf(f"yo{i}") for i in range(2)]
            s_yo = [P.new_sem(f"s_yo{i}") for i in range(2)]
            s_hc = P.new_sem("s_hc")
            yc = 0
            for b in range(c.NOWN // 512):
                tok0 = b * 512
                P.dma("pool", hT[:, 0:H, :], hoT_s[:, :, tok0:tok0 + 512], s_hc, reads=[b_hoTs], writes=b_hT)
                P.dma("pool", hT[:, H:H + G, :], hgT_s[:, :, tok0:tok0 + 512], s_hc, reads=[b_hgTs], writes=b_hT)
                for t in range(4):
                    r0 = tok0 + t * 128
                    P.dma("pool", xt[t][:], x1_s[r0:r0 + 128, :], s_xt[t], reads=[b_x1s], writes=[b_xt[t]])
                for n in range(c.NBD):
                    s_i = F["next_slot"]()
                    ov_ = slots[s_i][:, 0:c.MC * c.WD].rearrange("p (k f) -> p k f", k=c.MC)
                    P.dma("sp", ov_, wo_s[n], s_slot[s_i], reads=[b_wscr], writes=[b_slot[s_i]])
                    for t in range(4):
                        bi = t % 4
                        for k in range(c.MC):
                            P.op("pe", lambda e, k=k, t=t, bi=bi, ov_=ov_: e.matmul(
                                banks[bi][:, 0:c.WD], hT[:, k, t * 128:(t + 1) * 128], ov_[:, k, :],
                                start=(k == 0), stop=(k == c.MC - 1)),
                                reads=[b_hT[t], b_slot[s_i]], writes=[bb[bi]], signal=(k == c.MC - 1))
                        xs = xt[t][:, n * c.WD:(n + 1) * c.WD]
                        P.op("dve", lambda e, bi=bi, xs=xs: e.tensor_tensor(xs, banks[bi][:, 0:c.WD], xs, ALU.add),
                             reads=[bb[bi], b_xt[t]], writes=[b_xt[t]])
                F["ffn"](1)
                for t in range(4):
                    rstd, b_r = F["rstd_of"](xt[t][:], b_xt[t], D)
                    i = yc % 2
                    yc += 1
                    P.op("dve", lambda e, t=t, i=i, rstd=rstd: e.scalar_tensor_tensor(
                        yo[i][:], xt[t][:], rstd, gfin[:], ALU.mult, ALU.mult),
                        reads=[b_xt[t], b_r, b_gfin], writes=[b_yo[i]])
                    r0 = tok0 + t * 128
                    P.dma("pool", y[r0:r0 + 128, :], yo[i][:], s_yo[i], reads=[b_yo[i]], writes=[b_y])
        if getattr(cfg, 'dbg_dump', False):
            s_dd = P.new_sem("s_dd")
            b_dd = Buf("dd")
            for nm, t_, shp, dt_, bsrc in (("d_x1", x1_s, [c.NOWN, D], F32, b_x1s), ("d_h2", h2_s, [c.NTOK, D], BF16, b_h2s),
                                           ("d_cq", cqT_s, [128, NQ, c.NOWN], BF16, b_cqTs), ("d_ck", ckT_s, [128, NK, c.NTOK], BF16, b_ckTs),
                                           ("d_kr", krT_s, [64, c.NTOK], BF16, b_krTs), ("d_hg", hgT_s, [128, G, c.NOWN], BF16, b_hgTs),
                                           ("d_ho", hoT_s, [128, H, c.NOWN], BF16, b_hoTs)):
                od = nc.dram_tensor(nm, shp, dt_, kind="ExternalOutput").ap()
                P.dma("pool", od, t_.ap(), s_dd, reads=[bsrc], writes=[b_dd])
        P.barrier()
        P.emit()
    return nc


def rope_tables_T(pos):
    inv = (1.0 / (ROPE_THETA ** (np.arange(0, 64, 2, dtype=np.float32) / np.float32(64)))).astype(np.float32)
    ang = (pos.astype(np.float32)[:, None] * inv[None, :]).astype(np.float32)
    cos = np.cos(ang).astype(np.float32)
    sin = np.sin(ang).astype(np.float32)
    C2 = np.concatenate([cos, cos], axis=1).T
    S2 = np.concatenate([-sin, sin], axis=1).T
    return np.ascontiguousarray(C2), np.ascontiguousarray(S2)


def fmaj(v):
    v = np.asarray(v, np.float32).reshape(-1)
    return v.reshape(-1, 128).T


def make_in_maps(cfg, inp):
    c = cfg
    f = lambda a: np.ascontiguousarray(np.asarray(a, np.float32))
    gfm = np.concatenate([fmaj(inp["g_ffn1"][0]), fmaj(inp["g_mix"][0]), fmaj(inp["g_q"][0]), fmaj(inp["g_kv"][0]),
                          fmaj(np.concatenate([np.asarray(inp["g_out_attn"][0]), np.asarray(inp["g_out_gmlp"][0])])),
                          fmaj(inp["g_ffn2"][0])], axis=1)
    shared = {
        "w1g": f(inp["w1_gate"][0]), "w1u": f(inp["w1_up"][0]), "w1d": f(inp["w1_down"][0]),
        "w2g": f(inp["w2_gate"][0]), "w2u": f(inp["w2_up"][0]), "w2d": f(inp["w2_down"][0]),
        "win": f(inp["w_in"][0]), "wqb": f(inp["w_q_b"][0]), "wkvb": f(inp["w_kv_b"][0]), "wout": f(inp["w_out"][0]),
        "ws": f(inp["w_s"][0]), "gfm": f(gfm), "gvrow": f(np.asarray(inp["g_v"][0]).reshape(1, -1)),
        "gfinrow": f(np.asarray(inp["g_final"]).reshape(1, -1)), "bsT": f(np.asarray(inp["b_s"][0]).T),
        "ident": np.eye(128, dtype=np.float32),
    }
    xp = np.asarray(inp["x_prompt"], np.float32)[0]
    xs = np.asarray(inp["x_sample"], np.float32)
    maps = []
    for core in range(c.NCORE):
        ppos = (core * c.OWNP + np.arange(c.SP)) % c.SP
        xall = np.concatenate([xs[2 * core], xs[2 * core + 1], xp[ppos]], axis=0)
        pos = np.concatenate([np.arange(c.SA), np.arange(c.SA), ppos])
        C2, S2 = rope_tables_T(pos)
        m = dict(shared)
        m["xall"] = np.ascontiguousarray(xall)
        m["ropeC"] = C2
        m["ropeS"] = S2
        maps.append(m)
    return maps


def assemble(cfg, results):
    c = cfg
    yp = np.zeros((1, c.SP, c.D), np.float32)
    ys = np.zeros((2 * c.NCORE, c.SA, c.D), np.float32)
    for core in range(c.NCORE):
        yy = np.asarray(results[core]["y"], np.float32)
        ys[2 * core] = yy[0:c.SA]
        ys[2 * core + 1] = yy[c.SA:2 * c.SA]
        yp[0, core * c.OWNP:(core + 1) * c.OWNP] = yy[2 * c.SA:]
    return yp, ys


_NC_CACHE = {}


def run_cfg(cfg, inp):
    key = id(cfg)
    if key not in _NC_CACHE:
        _NC_CACHE[key] = build(cfg)
    nc = _NC_CACHE[key]
    maps = make_in_maps(cfg, inp)
    res = run_bass_kernel_spmd(nc, maps, core_ids=list(range(cfg.NCORE)))
    if getattr(cfg, 'dbg_dump', False):
        cfg.dbg_results = res.results
    return assemble(cfg, res.results)


def kernel(**inputs):
    return run_cfg(CFG_FULL, inputs)
```

```python
import math
from contextlib import ExitStack
import numpy as np
import concourse.bass as bass
import concourse.mybir as mybir
from concourse.bass_utils import run_bass_kernel_spmd

AF = mybir.ActivationFunctionType
ALU = mybir.AluOpType
AX = mybir.AxisListType
F32 = mybir.dt.float32
BF16 = mybir.dt.bfloat16
EPS = 1e-6
ROPE_THETA = 10000.0
SEM_LIMIT = 30000


class Sem:
    def __init__(self, handle, name, kind="dma"):
        self.h = handle
        self.name = name
        self.count = 0
        self.kind = kind


class Buf:
    __slots__ = ("name", "w", "r")

    def __init__(self, name):
        self.name = name
        self.w = None
        self.r = {}


class Eng:
    def __init__(self, name, sem):
        self.name = name
        self.sem = sem
        self.ops = []
        self.waited = {}
        self.is_pe = name == "pe"


class Prog:
    def __init__(self, nc, stack):
        self.nc = nc
        self.stack = stack
        self.engs = {}
        self.all_sems = []
        self.uid = 0
        for n in ("pe", "act", "dve", "pool", "sp"):
            self.engs[n] = Eng(n, self.new_sem("c_" + n, "eng"))
        self.nops = 0

    def new_sem(self, name, kind="dma"):
        self.uid += 1
        h = self.stack.enter_context(self.nc.semaphore(f"{name}_{self.uid}"))
        s = Sem(h, name, kind)
        self.all_sems.append(s)
        return s

    def sbuf(self, name, shape, dtype, stack=None):
        st = stack if stack is not None else self.stack
        return st.enter_context(self.nc.sbuf_tensor("sb_" + name, list(shape), dtype))

    def psum(self, name, shape, dtype=F32):
        return self.stack.enter_context(self.nc.psum_tensor(name, list(shape), dtype))

    def _needs(self, eng, reads, writes, is_dma):
        need = {}

        def add(tok):
            s, v = tok
            cur = need.get(s)
            if cur is None or cur < v:
                need[s] = v

        own = None if is_dma else eng.sem
        for b in reads:
            if b.w is not None:
                if b.w[0] is own and eng.is_pe:
                    continue
                add(b.w)
        for b in writes:
            if b.w is not None and b.w[0] is not own:
                add(b.w)
            for tok in b.r.values():
                if tok[0] is not own:
                    add(tok)
        out = []
        for s, v in need.items():
            assert v <= s.count, f"wait on a not-yet-emitted op ({s.name} {v}>{s.count}) on {eng.name}: deadlock risk"
            if s.kind == "dma":
                v = max(v, s.count)
            if eng.waited.get(s, 0) >= v:
                continue
            eng.waited[s] = v
            out.append((s, v))
        return out

    @staticmethod
    def _commit(reads, writes, tok):
        for b in writes:
            b.w = tok
            b.r = {}
        s = tok[0]
        for b in reads:
            cur = b.r.get(s)
            if cur is None or cur[1] < tok[1]:
                b.r[s] = tok

    def op(self, engname, fn, reads=(), writes=(), signal=True):
        eng = self.engs[engname]
        if eng.sem.count >= SEM_LIMIT:
            eng.sem = self.new_sem("c_" + engname, "eng")
        waits = self._needs(eng, reads, writes, False)
        sem = eng.sem
        if signal:
            sem.count += 1
            tok = (sem, sem.count)
        else:
            tok = (sem, sem.count + 1)
        self._commit(reads, writes if signal else (), tok)

        def run(e, fn=fn, waits=waits, signal=signal, sem=sem):
            for s, v in waits:
                e.wait_ge(s.h, v)
            ins = fn(e)
            if signal:
                ins.then_inc(sem.h, 1)

        eng.ops.append(run)
        self.nops += 1

    def dma(self, engname, out_ap, in_ap, sem, reads=(), writes=()):
        eng = self.engs[engname]
        waits = self._needs(eng, reads, writes, True)
        sem.count += 16
        tok = (sem, sem.count)
        self._commit(reads, writes, tok)

        def run(e, waits=waits, sem=sem, out_ap=out_ap, in_ap=in_ap):
            for s, v in waits:
                e.wait_ge(s.h, v)
            e.dma_start(out=out_ap, in_=in_ap).then_inc(sem.h, 16)

        eng.ops.append(run)
        self.nops += 1

    def barrier(self):
        for eng in self.engs.values():
            waits = []
            for s in self.all_sems:
                if s.count > 0 and eng.waited.get(s, 0) < s.count:
                    eng.waited[s] = s.count
                    if s is eng.sem:
                        continue
                    waits.append((s, s.count))

            def run(e, waits=waits):
                for s, v in waits:
                    e.wait_ge(s.h, v)

            eng.ops.append(run)

    def emit(self):
        with self.nc.Block() as block:
            @block.tensor
            def _(e):
                for f in self.engs["pe"].ops:
                    f(e)

            @block.scalar
            def _(e):
                for f in self.engs["act"].ops:
                    f(e)

            @block.vector
            def _(e):
                for f in self.engs["dve"].ops:
                    f(e)

            @block.gpsimd
            def _(e):
                for f in self.engs["pool"].ops:
                    f(e)

            @block.sync
            def _(e):
                for f in self.engs["sp"].ops:
                    f(e)


class Cfg:
    def __init__(self, D=2048, DFF=5632, QL=512, KL=512, H=8, G=8, SA=2048, SP=8192, NCORE=8, QG=1024):
        self.D, self.DFF, self.QL, self.KL, self.H, self.G = D, DFF, QL, KL, H, G
        self.SA, self.SP, self.NCORE, self.QG = SA, SP, NCORE, QG
        self.ND = D // 128
        self.NF = DFF // 128
        self.NQ = QL // 128
        self.NK = KL // 128
        self.GW = G * 128
        self.AW = H * 128
        self.MIXW = self.AW + self.GW
        self.MC = self.MIXW // 128
        self.IN_W = QL + KL + 64 + 2 * self.GW
        self.OWNP = SP // NCORE
        self.NOWN = 2 * SA + self.OWNP
        self.NTOK = 2 * SA + SP
        self.NBD = max(1, D // 512)
        self.WD = D // self.NBD
        self.NFT = self.NF // 4
        self.NQC = self.NF // 4
        self.QW = H * 192
        self.o_ckv = QL
        self.o_kr = QL + KL
        self.o_u = QL + KL + 64
        self.o_v = self.o_u + self.GW
        assert self.NOWN % 512 == 0 and self.NTOK % 512 == 0 and self.MC == self.ND
        assert SA % QG == 0 and self.OWNP % QG == 0 and QG % 512 == 0


CFG_FULL = Cfg()


def _split(start, width, step=512):
    out = []
    o = 0
    while o < width:
        w = min(step, width - o)
        out.append((start + o, w))
        o += w
    return out


def build(cfg):
    c = cfg
    D, ND, NF, NQ, NK, H, G = c.D, c.ND, c.NF, c.NQ, c.NK, c.H, c.G
    nc = bass.Bass("TRN2", target_bir_lowering=False)

    def din(name, shape, dt=F32):
        return nc.dram_tensor(name, list(shape), dt, kind="ExternalInput").ap()

    xall = din("xall", [c.NTOK, D])
    w_g = [din("w1g", [D, c.DFF]), din("w2g", [D, c.DFF])]
    w_u = [din("w1u", [D, c.DFF]), din("w2u", [D, c.DFF])]
    w_d = [din("w1d", [c.DFF, D]), din("w2d", [c.DFF, D])]
    w_in = din("win", [D, c.IN_W])
    w_qb = din("wqb", [c.QL, c.QW])
    w_kvb = din("wkvb", [c.KL, H * 256])
    w_o = din("wout", [c.MIXW, D])
    w_s = din("ws", [G, 128, 128])
    NG = 3 * ND + NQ + NK + c.MC
    gfm_d = din("gfm", [128, NG])
    gv_d = din("gvrow", [1, c.GW])
    gfin_d = din("gfinrow", [1, D])
    bsT_d = din("bsT", [128, G])
    ropeC_d = din("ropeC", [64, c.NTOK])
    ropeS_d = din("ropeS", [64, c.NTOK])
    ident_d = din("ident", [128, 128])
    y = nc.dram_tensor("y", [c.NOWN, D], F32, kind="ExternalOutput").ap()

    wg_s = [nc.dram_tensor(f"wgs{l}", [c.NFT, 128, ND, 512], BF16) for l in range(2)]
    wu_s = [nc.dram_tensor(f"wus{l}", [c.NFT, 128, ND, 512], BF16) for l in range(2)]
    wd_s = [nc.dram_tensor(f"wds{l}", [c.NBD, 4, 128, c.NQC, c.WD], BF16) for l in range(2)]
    win_s = nc.dram_tensor("wins", [128, ND, c.IN_W + 64], BF16)
    wq_s = nc.dram_tensor("wqs", [128, NQ, c.QW + H * 64], BF16)
    wk_s = nc.dram_tensor("wks", [128, NK, c.AW], BF16)
    wv_s = nc.dram_tensor("wvs", [128, NK, c.AW], BF16)
    wo_s = nc.dram_tensor("wos", [c.NBD, 128, c.MC, c.WD], BF16)
    x1_s = nc.dram_tensor("x1s", [c.NOWN, D], F32)
    h2_s = nc.dram_tensor("h2s", [c.NTOK, D], BF16)
    cqT_s = nc.dram_tensor("cqTs", [128, NQ, c.NOWN], BF16)
    ckT_s = nc.dram_tensor("ckTs", [128, NK, c.NTOK], BF16)
    krT_s = nc.dram_tensor("krTs", [64, c.NTOK], BF16)
    hgT_s = nc.dram_tensor("hgTs", [128, G, c.NOWN], BF16)
    hoT_s = nc.dram_tensor("hoTs", [128, H, c.NOWN], BF16)
    b_wscr = Buf("wscr")
    b_x1s, b_h2s, b_cqTs, b_ckTs, b_krTs, b_hgTs, b_hoTs, b_y = (Buf(n) for n in (
        "x1s", "h2s", "cqTs", "ckTs", "krTs", "hgTs", "hoTs", "y"))

    with ExitStack() as st:
        P = Prog(nc, st)
        banks = [P.psum(f"bk{i}", [128, 512], F32) for i in range(8)]
        bb = [Buf(f"bk{i}") for i in range(8)]

        def bf(i):
            return banks[i][:].bitcast(BF16)

        ident_f = P.sbuf("ident_f", [128, 128], F32)
        ident = P.sbuf("ident_b", [128, 128], BF16)
        ones_b = P.sbuf("ones_b", [128, 128], BF16)
        epsT = P.sbuf("epsT", [128, 1], F32)
        gfm = P.sbuf("gfm", [128, NG], F32)
        b_const = Buf("const")
        s_const = P.new_sem("s_const")
        P.dma("sp", ident_f[:], ident_d, s_const, writes=[b_const])
        P.dma("sp", gfm[:], gfm_d, s_const, writes=[b_const])
        b_ident = Buf("ident")
        P.op("dve", lambda e: e.tensor_copy(ident[:], ident_f[:]), reads=[b_const], writes=[b_ident])
        P.op("dve", lambda e: e.memset(ones_b[:], 1.0), writes=[b_ident])
        P.op("dve", lambda e: e.memset(epsT[:], EPS), writes=[b_ident])
        o_g1, o_gm, o_gq, o_gk, o_go, o_g2 = 0, ND, 2 * ND, 2 * ND + NQ, 2 * ND + NQ + NK, 2 * ND + NQ + NK + c.MC

        def copy(engname, out, in_, reads, writes):
            if engname == "act":
                P.op("act", lambda e: e.copy(out, in_), reads=reads, writes=writes)
            else:
                P.op(engname, lambda e: e.tensor_copy(out, in_), reads=reads, writes=writes)

        rr = {"n": 0}

        def alt(*names):
            rr["n"] += 1
            return names[rr["n"] % len(names)]

        STG = 6144
        STGB = 4096
        cur = {"stg": STG}

        def conv_jobs(src, dst, A, B, goff=None):
            a_step = max(1, cur["stg"] // B)
            out = []
            for a0 in range(0, A, a_step):
                an = min(a_step, A - a0)
                out.append((src[:, a0:a0 + an, :], dst[:, a0:a0 + an, :], an, B, None if goff is None else goff + a0))
            return out

        def fm(w, c0, cw):
            return w[:, c0:c0 + cw].rearrange("(k p) f -> p k f", p=128)

        def ffn_jobs(l):
            goff = o_g1 if l == 0 else o_g2
            jobs = []
            for j in range(c.NFT):
                jobs += conv_jobs(fm(w_g[l], j * 512, 512), wg_s[l][j], ND, 512, goff)
                jobs += conv_jobs(fm(w_u[l], j * 512, 512), wu_s[l][j], ND, 512, goff)
            for n in range(c.NBD):
                for q in range(4):
                    src = w_d[l][:, n * c.WD:(n + 1) * c.WD].rearrange("(c p) f -> p c f", p=128)
                    jobs += conv_jobs(src[:, q * c.NQC:(q + 1) * c.NQC, :], wd_s[l][n, q], c.NQC, c.WD)
            return jobs

        def other_jobs():
            jobs = []
            jobs += conv_jobs(fm(w_in, 0, c.IN_W), win_s[:, :, 0:c.IN_W], ND, c.IN_W, o_gm)
            jobs += conv_jobs(fm(w_in, c.o_kr + 32, 32), win_s[:, :, c.IN_W:c.IN_W + 32], ND, 32, o_gm)
            jobs += conv_jobs(fm(w_in, c.o_kr, 32), win_s[:, :, c.IN_W + 32:c.IN_W + 64], ND, 32, o_gm)
            jobs += conv_jobs(fm(w_qb, 0, c.QW), wq_s[:, :, 0:c.QW], NQ, c.QW, o_gq)
            for h in range(H):
                jobs += conv_jobs(fm(w_qb, h * 192 + 160, 32), wq_s[:, :, c.QW + h * 64:c.QW + h * 64 + 32], NQ, 32, o_gq)
                jobs += conv_jobs(fm(w_qb, h * 192 + 128, 32), wq_s[:, :, c.QW + h * 64 + 32:c.QW + h * 64 + 64], NQ, 32, o_gq)
                jobs += conv_jobs(fm(w_kvb, h * 256, 128), wk_s[:, :, h * 128:(h + 1) * 128], NK, 128, o_gk)
                jobs += conv_jobs(fm(w_kvb, h * 256 + 128, 128), wv_s[:, :, h * 128:(h + 1) * 128], NK, 128, o_gk)
            return jobs

        def wout_jobs():
            jobs = []
            for n in range(c.NBD):
                jobs += conv_jobs(fm(w_o, n * c.WD, c.WD), wo_s[n], c.MC, c.WD, o_go)
            return jobs

        def run_job(job, stg_t, cvo_t, b_s, b_c, s_s, s_c, ldq, stq, eng):
            src, dst, an, B, goff = job
            n = an * B
            sv = stg_t[:, 0:n].rearrange("p (a b) -> p a b", a=an)
            ov = cvo_t[:, 0:n].rearrange("p (a b) -> p a b", a=an)
            P.dma(ldq, sv, src, s_s, writes=[b_s])
            if goff is not None:
                gb = gfm[:, goff:goff + an].unsqueeze(2).broadcast_to([128, an, B])
                P.op(eng, lambda e, ov=ov, sv=sv, gb=gb: e.tensor_tensor(ov, sv, gb, ALU.mult),
                     reads=[b_s, b_const], writes=[b_c])
            else:
                P.op(eng, lambda e, ov=ov, sv=sv: e.tensor_copy(ov, sv), reads=[b_s], writes=[b_c])
            P.dma(stq, dst, ov, s_c, reads=[b_c], writes=[b_wscr])

        with ExitStack() as ph:
            stg = [P.sbuf(f"stg{i}", [128, STG], F32, ph) for i in range(2)]
            cvo = [P.sbuf(f"cvo{i}", [128, STG], BF16, ph) for i in range(2)]
            b_stg = [Buf(f"stg{i}") for i in range(2)]
            b_cvo = [Buf(f"cvo{i}") for i in range(2)]
            s_stg = [P.new_sem(f"s_stg{i}") for i in range(2)]
            s_cvo = [P.new_sem(f"s_cvo{i}") for i in range(2)]
            fg_jobs = ffn_jobs(0) + other_jobs()
            if getattr(cfg, 'bg_convert', False):
                cur["stg"] = STGB
                bg_jobs = wout_jobs() + ffn_jobs(1)
                cur["stg"] = STG
            else:
                fg_jobs = fg_jobs + wout_jobs() + ffn_jobs(1)
                bg_jobs = []
            for ji, job in enumerate(fg_jobs):
                i = ji % 2
                run_job(job, stg[i], cvo[i], b_stg[i], b_cvo[i], s_stg[i], s_cvo[i], "sp", "act",
                        "dve" if ji % 3 else "pool")
        P.barrier()
        if getattr(cfg, 'stop', None) == 'P0':
            P.emit()
            return nc

        def ffn_phase(ph, which):
            NS = 4
            slots = [P.sbuf(f"slot{which}{i}", [128, 8192], BF16, ph) for i in range(NS)]
            b_slot = [Buf(f"slot{i}") for i in range(NS)]
            s_slot = [P.new_sem(f"s_slot{i}") for i in range(NS)]
            sl = {"i": 0}
            hT = P.sbuf(f"hT{which}", [128, ND, 512], BF16, ph)
            b_hT = [Buf(f"hT{t}") for t in range(4)]
            aT = P.sbuf(f"aT{which}", [128, NF, 512], BF16, ph)
            b_aT = [Buf(f"aT{f}") for f in range(NF)]
            xt = [P.sbuf(f"xt{which}{t}", [128, D], F32, ph) for t in range(4)]
            b_xt = [Buf(f"xt{t}") for t in range(4)]
            s_xt = [P.new_sem(f"s_xt{t}") for t in range(4)]
            hn = [P.sbuf(f"hn{which}{i}", [128, D], BF16, ph) for i in range(2)]
            b_hn = [Buf(f"hn{i}") for i in range(2)]
            s_hn = [P.new_sem(f"s_hn{i}") for i in range(2)]
            junk = P.sbuf(f"junk{which}", [128, D], BF16, ph)
            b_junk = Buf("junk")
            sg = [P.sbuf(f"sg{which}{i}", [128, 512], F32, ph) for i in range(2)]
            b_sg = [Buf(f"sg{i}") for i in range(2)]
            st_ = [P.sbuf(f"stat{which}{i}", [128, 4], F32, ph) for i in range(4)]
            b_st = [Buf(f"stat{i}") for i in range(4)]
            cnt = {"hn": 0, "st": 0, "sg": 0, "g": 0, "u": 0, "tp": 0}

            def next_slot():
                i = sl["i"] % NS
                sl["i"] += 1
                return i

            def rstd_of(src_ap, src_buf, width):
                i = cnt["st"] % 4
                cnt["st"] += 1
                s = st_[i]
                P.op("act", lambda e: e.activation(junk[:, 0:width], src_ap, AF.Square, accum_out=s[:, 0:1]),
                     reads=[src_buf], writes=[b_junk, b_st[i]])
                P.op("act", lambda e: e.activation(s[:, 1:2], s[:, 0:1], AF.Sqrt, bias=epsT[:, 0:1], scale=1.0 / width),
                     reads=[b_st[i], b_ident], writes=[b_st[i]])
                P.op("dve", lambda e: e.reciprocal(s[:, 2:3], s[:, 1:2]), reads=[b_st[i]], writes=[b_st[i]])
                return s[:, 2:3], b_st[i]

            def norm_tile(t, store_h2=None):
                rstd, b_r = rstd_of(xt[t][:], b_xt[t], D)
                i = cnt["hn"] % 2
                cnt["hn"] += 1
                P.op("dve", lambda e: e.tensor_scalar_mul(hn[i][:], xt[t][:], rstd),
                     reads=[b_xt[t], b_r], writes=[b_hn[i]])
                if store_h2 is not None:
                    P.dma("pool", store_h2, hn[i][:], s_hn[i], reads=[b_hn[i]], writes=[b_h2s])
                    return
                transpose_into(hn[i], b_hn[i], ND, hT, b_hT[t], t)

            def transpose_into(src, b_src, nch, dstT, b_dst, t):
                for k0 in range(0, nch, 4):
                    kn = min(4, nch - k0)
                    bi = 4 + cnt["tp"] % 4
                    cnt["tp"] += 1
                    for j in range(kn):
                        P.op("pe", lambda e, j=j, bi=bi, k0=k0: e.transpose(
                            bf(bi)[:, j * 128:(j + 1) * 128], src[:, (k0 + j) * 128:(k0 + j + 1) * 128], ident[:]),
                            reads=[b_src, b_ident], writes=[bb[bi]], signal=(j == kn - 1))
                    ov = dstT[:, k0:k0 + kn, t * 128:(t + 1) * 128]
                    iv = bf(bi)[:, 0:kn * 128].rearrange("p (a b) -> p a b", a=kn)
                    copy(alt("dve", "act"), ov, iv, [bb[bi]], [b_dst])

            def ffn(l):
                for t in range(4):
                    norm_tile(t)
                for j in range(c.NFT):
                    sg_i = next_slot()
                    su_i = next_slot()
                    gv_ = slots[sg_i][:, 0:ND * 512].rearrange("p (k f) -> p k f", k=ND)
                    uv_ = slots[su_i][:, 0:ND * 512].rearrange("p (k f) -> p k f", k=ND)
                    P.dma("sp", gv_, wg_s[l][j], s_slot[sg_i], reads=[b_wscr], writes=[b_slot[sg_i]])
                    P.dma("sp", uv_, wu_s[l][j], s_slot[su_i], reads=[b_wscr], writes=[b_slot[su_i]])
                    for cc in range(4):
                        f = j * 4 + cc
                        bg = cnt["g"] % 2
                        cnt["g"] += 1
                        bu = 2 + cnt["u"] % 2
                        cnt["u"] += 1
                        for k in range(ND):
                            P.op("pe", lambda e, k=k, cc=cc, bg=bg, gv_=gv_: e.matmul(
                                banks[bg][:], gv_[:, k, cc * 128:(cc + 1) * 128], hT[:, k, :],
                                start=(k == 0), stop=(k == ND - 1)),
                                reads=[b_slot[sg_i]] + b_hT, writes=[bb[bg]], signal=(k == ND - 1))
                        for k in range(ND):
                            P.op("pe", lambda e, k=k, cc=cc, bu=bu, uv_=uv_: e.matmul(
                                banks[bu][:], uv_[:, k, cc * 128:(cc + 1) * 128], hT[:, k, :],
                                start=(k == 0), stop=(k == ND - 1)),
                                reads=[b_slot[su_i]] + b_hT, writes=[bb[bu]], signal=(k == ND - 1))
                        si = cnt["sg"] % 2
                        cnt["sg"] += 1
                        P.op("act", lambda e, si=si, bg=bg: e.activation(sg[si][:], banks[bg][:], AF.Silu),
                             reads=[bb[bg]], writes=[b_sg[si]])
                        P.op("dve", lambda e, si=si, bu=bu, f=f: e.tensor_tensor(
                            aT[:, f, :], banks[bu][:], sg[si][:], ALU.mult),
                            reads=[bb[bu], b_sg[si]], writes=[b_aT[f]])
                for n in range(c.NBD):
                    for q in range(4):
                        s_i = next_slot()
                        dv_ = slots[s_i][:, 0:c.NQC * c.WD].rearrange("p (c f) -> p c f", c=c.NQC)
                        P.dma("sp", dv_, wd_s[l][n, q], s_slot[s_i], reads=[b_wscr], writes=[b_slot[s_i]])
                        for t in range(4):
                            bi = 4 + t
                            for cc in range(c.NQC):
                                f = q * c.NQC + cc
                                first = (q == 0 and cc == 0)
                                last = (q == 3 and cc == c.NQC - 1)
                                P.op("pe", lambda e, t=t, cc=cc, f=f, bi=bi, first=first, last=last, dv_=dv_: e.matmul(
                                    banks[bi][:, 0:c.WD], aT[:, f, t * 128:(t + 1) * 128], dv_[:, cc, :],
                                    start=first, stop=last),
                                    reads=[b_slot[s_i], b_aT[f]], writes=[bb[bi]], signal=(last or cc == c.NQC - 1))
                    for t in range(4):
                        xs = xt[t][:, n * c.WD:(n + 1) * c.WD]
                        P.op("dve", lambda e, t=t, xs=xs: e.scalar_tensor_tensor(
                            xs, banks[4 + t][:, 0:c.WD], 0.5, xs, ALU.mult, ALU.add),
                            reads=[bb[4 + t], b_xt[t]], writes=[b_xt[t]])

            return dict(slots=slots, b_slot=b_slot, s_slot=s_slot, next_slot=next_slot, hT=hT, b_hT=b_hT,
                        xt=xt, b_xt=b_xt, s_xt=s_xt, norm_tile=norm_tile, ffn=ffn, rstd_of=rstd_of,
                        transpose_into=transpose_into, cnt=cnt)

        with ExitStack() as ph:
            F = ffn_phase(ph, "a")
            xt, b_xt, s_xt = F["xt"], F["b_xt"], F["s_xt"]
            if False:
                bstg = P.sbuf("bstg", [128, STGB], F32, ph)
                bcvo = P.sbuf("bcvo", [128, STGB], BF16, ph)
                b_bstg, b_bcvo = Buf("bstg"), Buf("bcvo")
                s_bstg, s_bcvo = P.new_sem("s_bstg"), P.new_sem("s_bcvo")
            nblk = c.NTOK // 512
            jpos = 0
            for b in range(nblk):
                tok0 = b * 512
                own = tok0 < c.NOWN
                for t in range(4):
                    P.dma("pool", xt[t][:], xall[tok0 + t * 128:tok0 + (t + 1) * 128, :], s_xt[t], writes=[b_xt[t]])
                nb_left = max(1, nblk - 1 - b)
                take = len(bg_jobs) - jpos if b >= nblk - 2 else -(-(len(bg_jobs) - jpos) // nb_left)
                take = 0
                F["ffn"](0)
                for t in range(4):
                    r0 = tok0 + t * 128
                    if own:
                        P.dma("pool", x1_s[r0:r0 + 128, :], xt[t][:], s_xt[t], reads=[b_xt[t]], writes=[b_x1s])
                    F["norm_tile"](t, store_h2=h2_s[r0:r0 + 128, :])
        P.barrier()
        if getattr(cfg, 'stop', None) == 'P1a':
            P.emit()
            return nc

        with ExitStack() as ph:
            winT = P.sbuf("winT", [128, ND, c.IN_W + 64], BF16, ph)
            b_win = Buf("winT")
            s_win = P.new_sem("s_win")
            for k0 in range(0, ND, 4):
                P.dma("sp", winT[:, k0:k0 + 4, :], win_s[:, k0:k0 + 4, :], s_win, reads=[b_wscr], writes=[b_win])
            sqv = P.sbuf("sqv", [128, c.GW], F32, ph)
            b_sqv = Buf("sqv")
            wsf = sqv[:].rearrange("p (g c) -> p g c", g=G)
            wsb = P.sbuf("wsb", [128, G, 128], BF16, ph)
            wsT = P.sbuf("wsT", [128, G, 128], BF16, ph)
            gvB = P.sbuf("gvB", [128, c.GW], F32, ph)
            bsB = P.sbuf("bsB", [128, c.GW], F32, ph)
            bsT = P.sbuf("bsTt", [128, G], F32, ph)
            b_gc = Buf("gconst")
            s_gc = P.new_sem("s_gc")
            P.dma("sp", wsf, w_s.rearrange("g i j -> i g j"), s_gc, writes=[b_gc, b_sqv])
            P.dma("sp", gvB[:], gv_d.partition_broadcast(128), s_gc, writes=[b_gc])
            P.dma("sp", bsT[:], bsT_d, s_gc, writes=[b_gc])
            b_gc2 = Buf("gconst2")
            P.op("dve", lambda e: e.tensor_copy(wsb[:], wsf), reads=[b_gc, b_sqv], writes=[b_gc2])
            P.op("dve", lambda e: e.tensor_copy(bsB[:].rearrange("p (g c) -> p g c", g=G),
                                                bsT[:].unsqueeze(2).broadcast_to([128, G, 128])),
                 reads=[b_gc], writes=[b_gc2])
            b_wsT = Buf("wsT")
            for g in range(G):
                bi = 6 + g % 2
                P.op("pe", lambda e, g=g, bi=bi: e.transpose(bf(bi)[:, 0:128], wsb[:, g, :], ident[:]),
                     reads=[b_gc2, b_ident], writes=[bb[bi]])
                copy("dve", wsT[:, g, :], bf(bi)[:, 0:128], [bb[bi]], [b_wsT])

            h2tm = [P.sbuf(f"h2tm{i}", [128, D], BF16, ph) for i in range(2)]
            b_h2tm = [Buf(f"h2tm{i}") for i in range(2)]
            s_h2tm = [P.new_sem(f"s_h2tm{i}") for i in range(2)]
            h2T = [P.sbuf(f"h2T{i}", [128, ND, 512], BF16, ph) for i in range(1)]
            b_h2T = [[Buf(f"h2T{i}_{t}") for t in range(4)] for i in range(1)]
            junk = P.sbuf("junkb", [128, 1024], BF16, ph)
            b_junk = Buf("junkb")
            stt = [P.sbuf(f"sttb{i}", [128, 4], F32, ph) for i in range(8)]
            b_stt = [Buf(f"sttb{i}") for i in range(8)]
            ln = [P.sbuf(f"lnb{i}", [128, 512], BF16, ph) for i in range(4)]
            b_ln = [Buf(f"lnb{i}") for i in range(4)]
            ckT = [P.sbuf(f"ckTb{i}", [128, NK, 512], BF16, ph) for i in range(2)]
            b_ckT = [Buf(f"ckTb{i}") for i in range(2)]
            s_ckT = [P.new_sem(f"s_ckTb{i}") for i in range(2)]
            cqT = [P.sbuf(f"cqTb{i}", [128, NQ, 512], BF16, ph) for i in range(1)] * 2
            b_cqT = [Buf(f"cqTb{i}") for i in range(1)] * 2
            s_cqT = [P.new_sem(f"s_cqTb{i}") for i in range(1)] * 2
            hgT = [P.sbuf(f"hgTb{i}", [128, G, 512], BF16, ph) for i in range(1)] * 2
            b_hgT = [Buf(f"hgTb{i}") for i in range(1)] * 2
            s_hgT = [P.new_sem(f"s_hgTb{i}") for i in range(1)] * 2
            rc = [P.sbuf(f"rcb{i}", [64, 512], F32, ph) for i in range(1)] * 2
            rs = [P.sbuf(f"rsb{i}", [64, 512], F32, ph) for i in range(1)] * 2
            b_rt = [Buf(f"rtb{i}") for i in range(1)] * 2
            s_rt = [P.new_sem(f"s_rtb{i}") for i in range(1)] * 2
            kt1 = P.sbuf("kt1", [64, 512], F32, ph)
            kt2 = P.sbuf("kt2", [64, 512], F32, ph)
            b_kt = Buf("kt")
            krb = [P.sbuf(f"krb{i}", [64, 512], BF16, ph) for i in range(2)]
            b_krb = [Buf(f"krb{i}") for i in range(2)]
            s_krb = [P.new_sem(f"s_krb{i}") for i in range(2)]
            gu = [P.sbuf(f"gu{i}", [128, c.GW], F32, ph) for i in range(2)]
            gvv = [P.sbuf(f"gvv{i}", [128, c.GW], F32, ph) for i in range(2)]
            b_gu = [Buf(f"gu{i}") for i in range(2)]
            b_gvv = [Buf(f"gvv{i}") for i in range(2)]
            ssg = [P.sbuf(f"ssg{i}", [128, 3 * G], F32, ph) for i in range(2)]
            b_ssg = [Buf(f"ssg{i}") for i in range(2)]
            vn = [P.sbuf(f"vn{i}", [128, c.GW], BF16, ph) for i in range(2)]
            b_vn = [Buf(f"vn{i}") for i in range(2)]
            og = [P.sbuf(f"og{i}", [128, c.GW], F32, ph) for i in range(1)] * 2
            b_og = [Buf(f"og{i}") for i in range(1)] * 2
            hgn = [P.sbuf(f"hgn{i}", [128, c.GW], BF16, ph) for i in range(2)]
            b_hgn = [Buf(f"hgn{i}") for i in range(2)]
            cn = {"bk": 0, "tp": 0, "st": 0, "ln": 0}

            def nbank():
                i = cn["bk"] % 6
                cn["bk"] += 1
                return i

            def tbank():
                i = 6 + cn["tp"] % 2
                cn["tp"] += 1
                return i

            def transp(src, b_src, nch, dstT, b_dst, t):
                for k0 in range(0, nch, 4):
                    kn = min(4, nch - k0)
                    bi = tbank()
                    for j in range(kn):
                        P.op("pe", lambda e, j=j, bi=bi, k0=k0: e.transpose(
                            bf(bi)[:, j * 128:(j + 1) * 128], src[:, (k0 + j) * 128:(k0 + j + 1) * 128], ident[:]),
                            reads=[b_src, b_ident], writes=[bb[bi]], signal=(j == kn - 1))
                    ov = dstT[:, k0:k0 + kn, t * 128:(t + 1) * 128]
                    iv = bf(bi)[:, 0:kn * 128].rearrange("p (a b) -> p a b", a=kn)
                    copy(alt("dve", "act"), ov, iv, [bb[bi]], [b_dst])

            def latent(hi, t, col0, width, nch, dstT, b_dst):
                bi = nbank()
                for k in range(ND):
                    P.op("pe", lambda e, k=k, bi=bi: e.matmul(
                        banks[bi][:, 0:width], h2T[hi][:, k, t * 128:(t + 1) * 128], winT[:, k, col0:col0 + width],
                        start=(k == 0), stop=(k == ND - 1)),
                        reads=[b_h2T[hi][t], b_win], writes=[bb[bi]], signal=(k == ND - 1))
                si = cn["st"] % 8
                cn["st"] += 1
                s = stt[si]
                P.op("act", lambda e: e.activation(junk[:, 0:width], banks[bi][:, 0:width], AF.Square, accum_out=s[:, 0:1]),
                     reads=[bb[bi]], writes=[b_junk, b_stt[si]])
                P.op("act", lambda e: e.activation(s[:, 1:2], s[:, 0:1], AF.Sqrt, bias=epsT[:, 0:1], scale=1.0 / width),
                     reads=[b_stt[si], b_ident], writes=[b_stt[si]])
                P.op("dve", lambda e: e.reciprocal(s[:, 2:3], s[:, 1:2]), reads=[b_stt[si]], writes=[b_stt[si]])
                li = cn["ln"] % 4
                cn["ln"] += 1
                P.op("dve", lambda e: e.tensor_scalar_mul(ln[li][:, 0:width], banks[bi][:, 0:width], s[:, 2:3]),
                     reads=[bb[bi], b_stt[si]], writes=[b_ln[li]])
                return lambda: transp(ln[li], b_ln[li], nch, dstT, b_dst, t)

            def gmlp_stage1(hi, t, gi):
                for (col0, w) in _split(c.o_u, c.GW):
                    bi = nbank()
                    for k in range(ND):
                        P.op("pe", lambda e, k=k, bi=bi, col0=col0, w=w: e.matmul(
                            banks[bi][:, 0:w], h2T[hi][:, k, t * 128:(t + 1) * 128], winT[:, k, col0:col0 + w],
                            start=(k == 0), stop=(k == ND - 1)),
                            reads=[b_h2T[hi][t], b_win], writes=[bb[bi]], signal=(k == ND - 1))
                    o0 = col0 - c.o_u
                    P.op("act", lambda e, bi=bi, o0=o0, w=w: e.activation(gu[gi][:, o0:o0 + w], banks[bi][:, 0:w], AF.Gelu_apprx_tanh),
                         reads=[bb[bi]], writes=[b_gu[gi]])
                for (col0, w) in _split(c.o_v, c.GW):
                    bi = nbank()
                    for k in range(ND):
                        P.op("pe", lambda e, k=k, bi=bi, col0=col0, w=w: e.matmul(
                            banks[bi][:, 0:w], h2T[hi][:, k, t * 128:(t + 1) * 128], winT[:, k, col0:col0 + w],
                            start=(k == 0), stop=(k == ND - 1)),
                            reads=[b_h2T[hi][t], b_win], writes=[bb[bi]], signal=(k == ND - 1))
                    o0 = col0 - c.o_v
                    P.op("act", lambda e, bi=bi, o0=o0, w=w: e.activation(gvv[gi][:, o0:o0 + w], banks[bi][:, 0:w], AF.Gelu_apprx_tanh),
                         reads=[bb[bi]], writes=[b_gvv[gi]])
                s = ssg[gi]
                P.op("dve", lambda e: e.tensor_tensor(sqv[:], gvv[gi][:], gvv[gi][:], ALU.mult),
                     reads=[b_gvv[gi]], writes=[b_sqv])
                P.op("dve", lambda e: e.tensor_reduce(s[:, 0:G], sqv[:].rearrange("p (g c) -> p g c", g=G), AX.X, ALU.add),
                     reads=[b_sqv], writes=[b_ssg[gi]])
                P.op("act", lambda e: e.activation(s[:, G:2 * G], s[:, 0:G], AF.Sqrt, bias=epsT[:, 0:1], scale=1.0 / 128),
                     reads=[b_ssg[gi], b_ident], writes=[b_ssg[gi]])
                P.op("dve", lambda e: e.reciprocal(s[:, 2 * G:3 * G], s[:, G:2 * G]), reads=[b_ssg[gi]], writes=[b_ssg[gi]])
                P.op("dve", lambda e: e.tensor_tensor(
                    vn[gi][:].rearrange("p (g c) -> p g c", g=G), gvv[gi][:].rearrange("p (g c) -> p g c", g=G),
                    s[:, 2 * G:3 * G].unsqueeze(2).broadcast_to([128, G, 128]), ALU.mult),
                    reads=[b_gvv[gi], b_ssg[gi]], writes=[b_vn[gi]])

            def gmlp_stage2(t, gi, oi):
                halves = _split(0, c.GW)
                for (c0, w) in halves:
                    bi = nbank()
                    for g in range(c0 // 128, (c0 + w) // 128):
                        P.op("pe", lambda e, g=g, bi=bi, c0=c0: e.matmul(
                            banks[bi][:, g * 128 - c0:(g + 1) * 128 - c0], wsT[:, g, :], vn[gi][:, g * 128:(g + 1) * 128],
                            start=True, stop=True),
                            reads=[b_wsT, b_vn[gi]], writes=[bb[bi]], signal=(g == (c0 + w) // 128 - 1))
                    P.op("dve", lambda e, bi=bi, c0=c0, w=w: e.tensor_tensor(
                        og[gi][:, c0:c0 + w], banks[bi][:, 0:w], gvB[:, c0:c0 + w], ALU.mult),
                        reads=[bb[bi], b_gc], writes=[b_og[gi]])
                P.op("pool", lambda e: e.tensor_tensor(og[gi][:], og[gi][:], bsB[:], ALU.add),
                     reads=[b_og[gi], b_gc2], writes=[b_og[gi]])
                P.op("pool", lambda e: e.tensor_tensor(og[gi][:], og[gi][:], gu[gi][:], ALU.mult),
                     reads=[b_og[gi], b_gu[gi]], writes=[b_og[gi]])
                si = cn["st"] % 8
                cn["st"] += 1
                s = stt[si]
                P.op("act", lambda e: e.activation(junk[:, 0:c.GW], og[gi][:], AF.Square, accum_out=s[:, 0:1]),
                     reads=[b_og[gi]], writes=[b_junk, b_stt[si]])
                P.op("act", lambda e: e.activation(s[:, 1:2], s[:, 0:1], AF.Sqrt, bias=epsT[:, 0:1], scale=1.0 / c.GW),
                     reads=[b_stt[si], b_ident], writes=[b_stt[si]])
                P.op("dve", lambda e: e.reciprocal(s[:, 2:3], s[:, 1:2]), reads=[b_stt[si]], writes=[b_stt[si]])
                P.op("dve", lambda e: e.tensor_scalar_mul(hgn[gi][:], og[gi][:], s[:, 2:3]),
                     reads=[b_og[gi], b_stt[si]], writes=[b_hgn[gi]])
                return lambda: transp(hgn[gi], b_hgn[gi], G, hgT[oi], b_hgT[oi], t)

            gcount = 0
            for b in range(c.NTOK // 512):
                tok0 = b * 512
                own = tok0 < c.NOWN
                hi = 0
                oi = b % 2
                for t in range(2):
                    r0 = tok0 + t * 128
                    P.dma("pool", h2tm[t][:], h2_s[r0:r0 + 128, :], s_h2tm[t], reads=[b_h2s], writes=[b_h2tm[t]])
                P.dma("pool", rc[oi][:], ropeC_d[:, tok0:tok0 + 512], s_rt[oi], writes=[b_rt[oi]])
                P.dma("pool", rs[oi][:], ropeS_d[:, tok0:tok0 + 512], s_rt[oi], writes=[b_rt[oi]])
                pend = []
                for t in range(4):
                    transp(h2tm[t % 2], b_h2tm[t % 2], ND, h2T[hi], b_h2T[hi][t], t)
                    if t + 2 < 4:
                        r0 = tok0 + (t + 2) * 128
                        P.dma("pool", h2tm[t % 2][:], h2_s[r0:r0 + 128, :], s_h2tm[t % 2], reads=[b_h2s], writes=[b_h2tm[t % 2]])
                    nxt = []
                    nxt.append(latent(hi, t, c.o_ckv, c.KL, NK, ckT[oi], b_ckT[oi]))
                    if own:
                        nxt.append(latent(hi, t, 0, c.QL, NQ, cqT[oi], b_cqT[oi]))
                        gi = gcount % 2
                        gcount += 1
                        gmlp_stage1(hi, t, gi)
                        nxt.append(("g", t, gi))
                    for it in pend:
                        if isinstance(it, tuple):
                            pend2 = gmlp_stage2(it[1], it[2], oi)
                            nxt.append(pend2)
                        else:
                            it()
                    pend = nxt
                while pend:
                    nxt = []
                    for it in pend:
                        if isinstance(it, tuple):
                            nxt.append(gmlp_stage2(it[1], it[2], oi))
                        else:
                            it()
                    pend = nxt
                ba, bbk = nbank(), nbank()
                for (bi, coff) in ((ba, c.o_kr), (bbk, c.IN_W)):
                    for k in range(ND):
                        P.op("pe", lambda e, k=k, bi=bi, coff=coff, hi=hi: e.matmul(
                            banks[bi][0:64, :], winT[:, k, coff:coff + 64], h2T[hi][:, k, :],
                            start=(k == 0), stop=(k == ND - 1)),
                            reads=b_h2T[hi] + [b_win], writes=[bb[bi]], signal=(k == ND - 1))
                P.op("dve", lambda e, ba=ba, oi=oi: e.tensor_tensor(kt1[:], banks[ba][0:64, :], rc[oi][:], ALU.mult),
                     reads=[bb[ba], b_rt[oi]], writes=[b_kt])
                P.op("dve", lambda e, bbk=bbk, oi=oi: e.tensor_tensor(kt2[:], banks[bbk][0:64, :], rs[oi][:], ALU.mult),
                     reads=[bb[bbk], b_rt[oi]], writes=[b_kt])
                P.op("dve", lambda e, oi=oi: e.tensor_tensor(krb[oi][:], kt1[:], kt2[:], ALU.add),
                     reads=[b_kt], writes=[b_krb[oi]])
                P.dma("pool", krT_s[:, tok0:tok0 + 512], krb[oi][:], s_krb[oi], reads=[b_krb[oi]], writes=[b_krTs])
                P.dma("pool", ckT_s[:, :, tok0:tok0 + 512], ckT[oi][:], s_ckT[oi], reads=[b_ckT[oi]], writes=[b_ckTs])
                if own:
                    P.dma("pool", cqT_s[:, :, tok0:tok0 + 512], cqT[oi][:], s_cqT[oi], reads=[b_cqT[oi]], writes=[b_cqTs])
                    P.dma("pool", hgT_s[:, :, tok0:tok0 + 512], hgT[oi][:], s_hgT[oi], reads=[b_hgT[oi]], writes=[b_hgTs])
        P.barrier()
        if getattr(cfg, 'stop', None) == 'P1b':
            P.emit()
            return nc

        with ExitStack() as ph:
            wq = P.sbuf("wq", [128, NQ, c.QW + H * 64], BF16, ph)
            wk = P.sbuf("wk", [128, NK, c.AW], BF16, ph)
            wv = P.sbuf("wv", [128, NK, c.AW], BF16, ph)
            b_aw = Buf("attw")
            s_aw = P.new_sem("s_aw")
            P.dma("sp", wq[:], wq_s[:, :, :], s_aw, reads=[b_wscr], writes=[b_aw])
            P.dma("sp", wk[:], wk_s[:, :, :], s_aw, reads=[b_wscr], writes=[b_aw])
            P.dma("sp", wv[:], wv_s[:, :, :], s_aw, reads=[b_wscr], writes=[b_aw])
            SKMAX = max(c.SA, c.SP)
            QG = c.QG
            krT = P.sbuf("krT", [64, SKMAX], BF16, ph)
            b_krT = Buf("krT")
            s_krT = P.new_sem("s_krT")
            cqg = P.sbuf("cqg", [128, NQ, QG], BF16, ph)
            b_cqg = Buf("cqg")
            s_cqg = P.new_sem("s_cqg")
            rcq = P.sbuf("rcq", [64, QG], F32, ph)
            rsq = P.sbuf("rsq", [64, QG], F32, ph)
            b_rq = Buf("rq")
            s_rq = P.new_sem("s_rq")
            KT = P.sbuf("KT", [128, SKMAX], BF16, ph)
            b_KT = Buf("KT")
            V = P.sbuf("V", [128, SKMAX // 128, 128], BF16, ph)
            b_V = Buf("V")
            NCK = 4
            ckc = [P.sbuf(f"ckc{i}", [128, NK, 512], BF16, ph) for i in range(NCK)]
            b_ckc = [Buf(f"ckc{i}") for i in range(NCK)]
            s_ckc = [P.new_sem(f"s_ckc{i}") for i in range(NCK)]
            QnT = [P.sbuf(f"QnT{i}", [128, 512], BF16, ph) for i in range(2)]
            QrT = [P.sbuf(f"QrT{i}", [64, 512], BF16, ph) for i in range(2)]
            b_Q = [Buf(f"Q{i}") for i in range(2)]
            qt1 = P.sbuf("qt1", [64, 512], F32, ph)
            qt2 = P.sbuf("qt2", [64, 512], F32, ph)
            b_qt = Buf("qt")
            PT = [P.sbuf(f"PT{i}", [128, 512], BF16, ph) for i in range(3)]
            b_PT = [Buf(f"PT{i}") for i in range(3)]
            OT = P.sbuf("OT", [128, H, QG], F32, ph)
            b_OT = [Buf(f"OT{h}") for h in range(H)]
            rl = P.sbuf("rl", [128, 512], F32, ph)
            b_rl = Buf("rl")
            sqo = [P.sbuf(f"sqo{i}", [128, 512], BF16, ph) for i in range(2)]
            b_sqo = [Buf(f"sqo{i}") for i in range(2)]
            rso = P.sbuf("rso", [128, 512], F32, ph)
            b_rso = Buf("rso")
            hoT = [P.sbuf(f"hoT{i}", [128, H, 512], BF16, ph) for i in range(2)]
            b_hoT = [Buf(f"hoT{i}") for i in range(2)]
            s_hoT = [P.new_sem(f"s_hoT{i}") for i in range(2)]
            scale = 1.0 / math.sqrt(192.0)
            ct = {"ck": 0, "q": 0, "s": 0, "pt": 0, "ol": 0, "ho": 0, "sq": 0, "mb": 0}
            if bg_jobs:
                bstg = P.sbuf("bstg2", [128, STGB], F32, ph)
                bcvo = P.sbuf("bcvo2", [128, STGB], BF16, ph)
                b_bstg, b_bcvo = Buf("bstg2"), Buf("bcvo2")
                s_bstg, s_bcvo = P.new_sem("s_bstg2"), P.new_sem("s_bcvo2")
            bgp = {"pos": 0, "it": 0}

            def mbk():
                ct["mb"] += 1
                return (2, 7)[ct["mb"] % 2]

            items = []
            for s_ in range(2):
                for qg in range(c.SA // QG):
                    items.append((s_ * c.SA, c.SA, s_ * c.SA + qg * QG))
            for qg in range(c.OWNP // QG):
                items.append((2 * c.SA, c.SP, 2 * c.SA + qg * QG))
            prev_k0 = None
            for (k0, Sk, q0) in items[:getattr(cfg, 'dbg_items', 99)]:
                if k0 != prev_k0:
                    P.dma("pool", krT[:, 0:Sk], krT_s[:, k0:k0 + Sk], s_krT, reads=[b_krTs], writes=[b_krT])
                    prev_k0 = k0
                P.dma("pool", cqg[:], cqT_s[:, :, q0:q0 + QG], s_cqg, reads=[b_cqTs], writes=[b_cqg])
                P.dma("pool", rcq[:], ropeC_d[:, q0:q0 + QG], s_rq, writes=[b_rq])
                P.dma("pool", rsq[:], ropeS_d[:, q0:q0 + QG], s_rq, writes=[b_rq])
                for h in range(getattr(cfg, 'dbg_heads', H)):
                    n_it = len(items) * H
                    left = max(1, n_it - bgp["it"] - 2)
                    take = -(-(len(bg_jobs) - bgp["pos"]) // left) if bgp["it"] < n_it - 2 else len(bg_jobs) - bgp["pos"]
                    for job in bg_jobs[bgp["pos"]:bgp["pos"] + take]:
                        run_job(job, bstg, bcvo, b_bstg, b_bcvo, s_bstg, s_bcvo, "sp", "sp", "dve")
                    bgp["pos"] += take
                    bgp["it"] += 1
                    for kb in range(Sk // 512):
                        ci = ct["ck"] % NCK
                        ct["ck"] += 1
                        P.dma("sp", ckc[ci][:], ckT_s[:, :, k0 + kb * 512:k0 + (kb + 1) * 512], s_ckc[ci],
                              reads=[b_ckTs], writes=[b_ckc[ci]])
                        m = mbk()
                        for k in range(NK):
                            P.op("pe", lambda e, k=k, ci=ci, h=h, m=m: e.matmul(
                                banks[m][:], wk[:, k, h * 128:(h + 1) * 128], ckc[ci][:, k, :],
                                start=(k == 0), stop=(k == NK - 1)),
                                reads=[b_aw, b_ckc[ci]], writes=[bb[m]], signal=(k == NK - 1))
                        copy("dve", KT[:, kb * 512:(kb + 1) * 512], banks[m][:], [bb[m]], [b_KT])
                        m = mbk()
                        for j in range(4):
                            for k in range(NK):
                                P.op("pe", lambda e, k=k, j=j, ci=ci, h=h, m=m: e.matmul(
                                    banks[m][:, j * 128:(j + 1) * 128], ckc[ci][:, k, j * 128:(j + 1) * 128],
                                    wv[:, k, h * 128:(h + 1) * 128], start=(k == 0), stop=(k == NK - 1)),
                                    reads=[b_aw, b_ckc[ci]], writes=[bb[m]], signal=(j == 3 and k == NK - 1))
                        copy("act", V[:, kb * 4:(kb + 1) * 4, :], banks[m][:].rearrange("p (a b) -> p a b", a=4),
                             [bb[m]], [b_V])
                    qis = {}
                    for qb in range(QG // 512 if getattr(cfg, 'dbg_p2', 9) >= 2 else 0):
                        qi = ct["q"] % 2
                        ct["q"] += 1
                        qis[qb] = qi
                        qs = slice(qb * 512, (qb + 1) * 512)
                        m = mbk()
                        for k in range(NQ):
                            P.op("pe", lambda e, k=k, h=h, qs=qs, m=m: e.matmul(
                                banks[m][:], wq[:, k, h * 192:h * 192 + 128], cqg[:, k, qs],
                                start=(k == 0), stop=(k == NQ - 1)),
                                reads=[b_aw, b_cqg], writes=[bb[m]], signal=(k == NQ - 1))
                        P.op("act", lambda e, qi=qi, m=m: e.mul(QnT[qi][:], banks[m][:], scale), reads=[bb[m]], writes=[b_Q[qi]])
                        for (coff, dst, tab) in ((h * 192 + 128, qt1, rcq), (c.QW + h * 64, qt2, rsq)):
                            m = mbk()
                            for k in range(NQ):
                                P.op("pe", lambda e, k=k, coff=coff, qs=qs, m=m: e.matmul(
                                    banks[m][0:64, :], wq[:, k, coff:coff + 64], cqg[:, k, qs],
                                    start=(k == 0), stop=(k == NQ - 1)),
                                    reads=[b_aw, b_cqg], writes=[bb[m]], signal=(k == NQ - 1))
                            P.op("dve", lambda e, dst=dst, tab=tab, qs=qs, m=m: e.scalar_tensor_tensor(
                                dst[:], banks[m][0:64, :], scale, tab[:, qs], ALU.mult, ALU.mult),
                                reads=[bb[m], b_rq], writes=[b_qt])
                        P.op("dve", lambda e, qi=qi: e.tensor_tensor(QrT[qi][:], qt1[:], qt2[:], ALU.add),
                             reads=[b_qt], writes=[b_Q[qi]])
                    for qb in range(QG // 512 if getattr(cfg, 'dbg_p2', 9) >= 2 else 0):
                        qi = qis[qb]
                        qs = slice(qb * 512, (qb + 1) * 512)
                        oli = ct["ol"] % 2
                        ct["ol"] += 1
                        bo, bl = 3 + 2 * oli, 4 + 2 * oli
                        nkt = Sk // 128
                        pend = None

                        def ol(kt, pi, bo=bo, bl=bl, nkt=nkt):
                            P.op("pe", lambda e: e.matmul(banks[bo][:], V[:, kt, :], PT[pi][:],
                                                          start=(kt == 0), stop=(kt == nkt - 1)),
                                 reads=[b_V, b_PT[pi]], writes=[bb[bo]], signal=(kt == nkt - 1))
                            P.op("pe", lambda e: e.matmul(banks[bl][:], ones_b[:], PT[pi][:],
                                                          start=(kt == 0), stop=(kt == nkt - 1)),
                                 reads=[b_ident, b_PT[pi]], writes=[bb[bl]], signal=(kt == nkt - 1))

                        if getattr(cfg, 'dbg_p2', 9) < 3:
                            continue
                        for kt in range(nkt):
                            si = ct["s"] % 2
                            ct["s"] += 1
                            P.op("pe", lambda e, kt=kt, si=si, qi=qi: e.matmul(
                                banks[si][:], KT[:, kt * 128:(kt + 1) * 128], QnT[qi][:], start=True, stop=False),
                                reads=[b_KT, b_Q[qi]], writes=[bb[si]], signal=False)
                            P.op("pe", lambda e, kt=kt, si=si, qi=qi: e.matmul(
                                banks[si][:], krT[:, kt * 128:(kt + 1) * 128], QrT[qi][:], start=False, stop=True),
                                reads=[b_krT, b_Q[qi]], writes=[bb[si]], signal=True)
                            pi = ct["pt"] % 3
                            ct["pt"] += 1
                            P.op("act", lambda e, si=si, pi=pi: e.activation(PT[pi][:], banks[si][:], AF.Exp),
                                 reads=[bb[si]], writes=[b_PT[pi]])
                            if pend is not None:
                                ol(*pend)
                            pend = (kt, pi)
                        ol(*pend)
                        P.op("dve", lambda e, bl=bl: e.reciprocal(rl[:], banks[bl][:]), reads=[bb[bl]], writes=[b_rl])
                        P.op("dve", lambda e, bo=bo, h=h, qs=qs: e.tensor_tensor(OT[:, h, qs], banks[bo][:], rl[:], ALU.mult),
                             reads=[bb[bo], b_rl], writes=[b_OT[h]])
                for qb in range(QG // 512 if getattr(cfg, 'dbg_p2', 9) >= 4 else 0):
                    qs = slice(qb * 512, (qb + 1) * 512)
                    m = mbk()
                    for h in range(H):
                        qi2 = ct["sq"] % 2
                        ct["sq"] += 1
                        P.op(alt("dve", "pool"), lambda e, h=h, qi2=qi2, qs=qs: e.tensor_tensor(
                            sqo[qi2][:], OT[:, h, qs], OT[:, h, qs], ALU.mult),
                            reads=[b_OT[h]], writes=[b_sqo[qi2]])
                        P.op("pe", lambda e, h=h, qi2=qi2, m=m: e.matmul(banks[m][:], ones_b[:], sqo[qi2][:],
                                                                         start=(h == 0), stop=(h == H - 1)),
                             reads=[b_ident, b_sqo[qi2]], writes=[bb[m]], signal=True)
                    P.op("act", lambda e, m=m: e.activation(rso[:], banks[m][:], AF.Sqrt, bias=epsT[:, 0:1], scale=1.0 / c.AW),
                         reads=[bb[m], b_ident], writes=[b_rso])
                    P.op("dve", lambda e: e.reciprocal(rso[:], rso[:]), reads=[b_rso], writes=[b_rso])
                    oi = ct["ho"] % 2
                    ct["ho"] += 1
                    for h in range(H):
                        P.op(alt("dve", "pool"), lambda e, h=h, oi=oi, qs=qs: e.tensor_tensor(
                            hoT[oi][:, h, :], OT[:, h, qs], rso[:], ALU.mult),
                            reads=[b_OT[h], b_rso], writes=[b_hoT[oi]])
                    P.dma("pool", hoT_s[:, :, q0 + qb * 512:q0 + (qb + 1) * 512], hoT[oi][:], s_hoT[oi],
                          reads=[b_hoT[oi]], writes=[b_hoTs])
        P.barrier()
        if getattr(cfg, 'stop', None) == 'P2':
            P.emit()
            return nc

        with ExitStack() as ph:
            F = ffn_phase(ph, "c")
            xt, b_xt, s_xt, hT, b_hT = F["xt"], F["b_xt"], F["s_xt"], F["hT"], F["b_hT"]
            slots, b_slot, s_slot = F["slots"], F["b_slot"], F["s_slot"]
            gfin = P.sbuf("gfin", [128, D], F32, ph)
            b_gfin = Buf("gfin")
            s_gfin = P.new_sem("s_gfin")
            P.dma("sp", gfin[:], gfin_d.partition_broadcast(128), s_gfin, writes=[b_gfin])
            yo = [P.sbuf(f"yo{i}", [128, D], F32, ph) for i in range(2)]
            b_yo = [Buf(f"yo{i}") for i in range(2)]
            s_yo = [P.new_sem(f"s_yo{i}") for i in range(2)]
            s_hc = P.new_sem("s_hc")
            yc = 0
            for b in range(c.NOWN // 512):
                tok0 = b * 512
                P.dma("pool", hT[:, 0:H, :], hoT_s[:, :, tok0:tok0 + 512], s_hc, reads=[b_hoTs], writes=b_hT)
                P.dma("pool", hT[:, H:H + G, :], hgT_s[:, :, tok0:tok0 + 512], s_hc, reads=[b_hgTs], writes=b_hT)
                for t in range(4):
                    r0 = tok0 + t * 128
                    P.dma("pool", xt[t][:], x1_s[r0:r0 + 128, :], s_xt[t], reads=[b_x1s], writes=[b_xt[t]])
                for n in range(c.NBD):
                    s_i = F["next_slot"]()
                    ov_ = slots[s_i][:, 0:c.MC * c.WD].rearrange("p (k f) -> p k f", k=c.MC)
                    P.dma("sp", ov_, wo_s[n], s_slot[s_i], reads=[b_wscr], writes=[b_slot[s_i]])
                    for t in range(4):
                        bi = t % 4
                        for k in range(c.MC):
                            P.op("pe", lambda e, k=k, t=t, bi=bi, ov_=ov_: e.matmul(
                                banks[bi][:, 0:c.WD], hT[:, k, t * 128:(t + 1) * 128], ov_[:, k, :],
                                start=(k == 0), stop=(k == c.MC - 1)),
                                reads=[b_hT[t], b_slot[s_i]], writes=[bb[bi]], signal=(k == c.MC - 1))
                        xs = xt[t][:, n * c.WD:(n + 1) * c.WD]
                        P.op("dve", lambda e, bi=bi, xs=xs: e.tensor_tensor(xs, banks[bi][:, 0:c.WD], xs, ALU.add),
                             reads=[bb[bi], b_xt[t]], writes=[b_xt[t]])
                F["ffn"](1)
                for t in range(4):
                    rstd, b_r = F["rstd_of"](xt[t][:], b_xt[t], D)
                    i = yc % 2
                    yc += 1
                    P.op("dve", lambda e, t=t, i=i, rstd=rstd: e.scalar_tensor_tensor(
                        yo[i][:], xt[t][:], rstd, gfin[:], ALU.mult, ALU.mult),
                        reads=[b_xt[t], b_r, b_gfin], writes=[b_yo[i]])
                    r0 = tok0 + t * 128
                    P.dma("pool", y[r0:r0 + 128, :], yo[i][:], s_yo[i], reads=[b_yo[i]], writes=[b_y])
        if getattr(cfg, 'dbg_dump', False):
            s_dd = P.new_sem("s_dd")
            b_dd = Buf("dd")
            for nm, t_, shp, dt_, bsrc in (("d_x1", x1_s, [c.NOWN, D], F32, b_x1s), ("d_h2", h2_s, [c.NTOK, D], BF16, b_h2s),
                                           ("d_cq", cqT_s, [128, NQ, c.NOWN], BF16, b_cqTs), ("d_ck", ckT_s, [128, NK, c.NTOK], BF16, b_ckTs),
                                           ("d_kr", krT_s, [64, c.NTOK], BF16, b_krTs), ("d_hg", hgT_s, [128, G, c.NOWN], BF16, b_hgTs),
                                           ("d_ho", hoT_s, [128, H, c.NOWN], BF16, b_hoTs)):
                od = nc.dram_tensor(nm, shp, dt_, kind="ExternalOutput").ap()
                P.dma("pool", od, t_.ap(), s_dd, reads=[bsrc], writes=[b_dd])
        P.barrier()
        P.emit()
    return nc


def rope_tables_T(pos):
    inv = (1.0 / (ROPE_THETA ** (np.arange(0, 64, 2, dtype=np.float32) / np.float32(64)))).astype(np.float32)
    ang = (pos.astype(np.float32)[:, None] * inv[None, :]).astype(np.float32)
    cos = np.cos(ang).astype(np.float32)
    sin = np.sin(ang).astype(np.float32)
    C2 = np.concatenate([cos, cos], axis=1).T
    S2 = np.concatenate([-sin, sin], axis=1).T
    return np.ascontiguousarray(C2), np.ascontiguousarray(S2)


def fmaj(v):
    v = np.asarray(v, np.float32).reshape(-1)
    return v.reshape(-1, 128).T


def make_in_maps(cfg, inp):
    c = cfg
    f = lambda a: np.ascontiguousarray(np.asarray(a, np.float32))
    gfm = np.concatenate([fmaj(inp["g_ffn1"][0]), fmaj(inp["g_mix"][0]), fmaj(inp["g_q"][0]), fmaj(inp["g_kv"][0]),
                          fmaj(np.concatenate([np.asarray(inp["g_out_attn"][0]), np.asarray(inp["g_out_gmlp"][0])])),
                          fmaj(inp["g_ffn2"][0])], axis=1)
    shared = {
        "w1g": f(inp["w1_gate"][0]), "w1u": f(inp["w1_up"][0]), "w1d": f(inp["w1_down"][0]),
        "w2g": f(inp["w2_gate"][0]), "w2u": f(inp["w2_up"][0]), "w2d": f(inp["w2_down"][0]),
        "win": f(inp["w_in"][0]), "wqb": f(inp["w_q_b"][0]), "wkvb": f(inp["w_kv_b"][0]), "wout": f(inp["w_out"][0]),
        "ws": f(inp["w_s"][0]), "gfm": f(gfm), "gvrow": f(np.asarray(inp["g_v"][0]).reshape(1, -1)),
        "gfinrow": f(np.asarray(inp["g_final"]).reshape(1, -1)), "bsT": f(np.asarray(inp["b_s"][0]).T),
        "ident": np.eye(128, dtype=np.float32),
    }
    xp = np.asarray(inp["x_prompt"], np.float32)[0]
    xs = np.asarray(inp["x_sample"], np.float32)
    maps = []
    for core in range(c.NCORE):
        ppos = (core * c.OWNP + np.arange(c.SP)) % c.SP
        xall = np.concatenate([xs[2 * core], xs[2 * core + 1], xp[ppos]], axis=0)
        pos = np.concatenate([np.arange(c.SA), np.arange(c.SA), ppos])
        C2, S2 = rope_tables_T(pos)
        m = dict(shared)
        m["xall"] = np.ascontiguousarray(xall)
        m["ropeC"] = C2
        m["ropeS"] = S2
        maps.append(m)
    return maps


def assemble(cfg, results):
    c = cfg
    yp = np.zeros((1, c.SP, c.D), np.float32)
    ys = np.zeros((2 * c.NCORE, c.SA, c.D), np.float32)
    for core in range(c.NCORE):
        yy = np.asarray(results[core]["y"], np.float32)
        ys[2 * core] = yy[0:c.SA]
        ys[2 * core + 1] = yy[c.SA:2 * c.SA]
        yp[0, core * c.OWNP:(core + 1) * c.OWNP] = yy[2 * c.SA:]
    return yp, ys


_NC_CACHE = {}


def run_cfg(cfg, inp):
    key = id(cfg)
    if key not in _NC_CACHE:
        _NC_CACHE[key] = build(cfg)
    nc = _NC_CACHE[key]
    maps = make_in_maps(cfg, inp)
    res = run_bass_kernel_spmd(nc, maps, core_ids=list(range(cfg.NCORE)))
    if getattr(cfg, 'dbg_dump', False):
        cfg.dbg_results = res.results
    return assemble(cfg, res.results)


def kernel(**inputs):
    return run_cfg(CFG_FULL, inputs)
```

```python
import math
from contextlib import ExitStack
import numpy as np
import concourse.bass as bass
import concourse.mybir as mybir
from concourse.bass_utils import run_bass_kernel_spmd

AF = mybir.ActivationFunctionType
ALU = mybir.AluOpType
AX = mybir.AxisListType
F32 = mybir.dt.float32
BF16 = mybir.dt.bfloat16
EPS = 1e-6
ROPE_THETA = 10000.0
SEM_LIMIT = 30000


class Sem:
    def __init__(self, handle, name, kind="dma"):
        self.h = handle
        self.name = name
        self.count = 0
        self.kind = kind


class Buf:
    __slots__ = ("name", "w", "r")

    def __init__(self, name):
        self.name = name
        self.w = None
        self.r = {}


class Eng:
    def __init__(self, name, sem):
        self.name = name
        self.sem = sem
        self.ops = []
        self.waited = {}
        self.is_pe = name == "pe"


class Prog:
    def __init__(self, nc, stack):
        self.nc = nc
        self.stack = stack
        self.engs = {}
        self.all_sems = []
        self.uid = 0
        for n in ("pe", "act", "dve", "pool", "sp"):
            self.engs[n] = Eng(n, self.new_sem("c_" + n, "eng"))
        self.nops = 0

    def new_sem(self, name, kind="dma"):
        self.uid += 1
        h = self.stack.enter_context(self.nc.semaphore(f"{name}_{self.uid}"))
        s = Sem(h, name, kind)
        self.all_sems.append(s)
        return s

    def sbuf(self, name, shape, dtype, stack=None):
        st = stack if stack is not None else self.stack
        return st.enter_context(self.nc.sbuf_tensor("sb_" + name, list(shape), dtype))

    def psum(self, name, shape, dtype=F32):
        return self.stack.enter_context(self.nc.psum_tensor(name, list(shape), dtype))

    def _needs(self, eng, reads, writes, is_dma):
        need = {}

        def add(tok):
            s, v = tok
            cur = need.get(s)
            if cur is None or cur < v:
                need[s] = v

        own = None if is_dma else eng.sem
        for b in reads:
            if b.w is not None:
                if b.w[0] is own and eng.is_pe:
                    continue
                add(b.w)
        for b in writes:
            if b.w is not None and b.w[0] is not own:
                add(b.w)
            for tok in b.r.values():
                if tok[0] is not own:
                    add(tok)
        out = []
        for s, v in need.items():
            assert v <= s.count, f"wait on a not-yet-emitted op ({s.name} {v}>{s.count}) on {eng.name}: deadlock risk"
            if s.kind == "dma":
                v = max(v, s.count)
            if eng.waited.get(s, 0) >= v:
                continue
            eng.waited[s] = v
            out.append((s, v))
        return out

    @staticmethod
    def _commit(reads, writes, tok):
        for b in writes:
            b.w = tok
            b.r = {}
        s = tok[0]
        for b in reads:
            cur = b.r.get(s)
            if cur is None or cur[1] < tok[1]:
                b.r[s] = tok

    def op(self, engname, fn, reads=(), writes=(), signal=True):
        eng = self.engs[engname]
        if eng.sem.count >= SEM_LIMIT:
            eng.sem = self.new_sem("c_" + engname, "eng")
        waits = self._needs(eng, reads, writes, False)
        sem = eng.sem
        if signal:
            sem.count += 1
            tok = (sem, sem.count)
        else:
            tok = (sem, sem.count + 1)
        self._commit(reads, writes if signal else (), tok)

        def run(e, fn=fn, waits=waits, signal=signal, sem=sem):
            for s, v in waits:
                e.wait_ge(s.h, v)
            ins = fn(e)
            if signal:
                ins.then_inc(sem.h, 1)

        eng.ops.append(run)
        self.nops += 1

    def dma(self, engname, out_ap, in_ap, sem, reads=(), writes=()):
        eng = self.engs[engname]
        waits = self._needs(eng, reads, writes, True)
        sem.count += 16
        tok = (sem, sem.count)
        self._commit(reads, writes, tok)

        def run(e, waits=waits, sem=sem, out_ap=out_ap, in_ap=in_ap):
            for s, v in waits:
                e.wait_ge(s.h, v)
            e.dma_start(out=out_ap, in_=in_ap).then_inc(sem.h, 16)

        eng.ops.append(run)
        self.nops += 1

    def barrier(self):
        for eng in self.engs.values():
            waits = []
            for s in self.all_sems:
                if s.count > 0 and eng.waited.get(s, 0) < s.count:
                    eng.waited[s] = s.count
                    if s is eng.sem:
                        continue
                    waits.append((s, s.count))

            def run(e, waits=waits):
                for s, v in waits:
                    e.wait_ge(s.h, v)

            eng.ops.append(run)

    def emit(self):
        with self.nc.Block() as block:
            @block.tensor
            def _(e):
                for f in self.engs["pe"].ops:
                    f(e)

            @block.scalar
            def _(e):
                for f in self.engs["act"].ops:
                    f(e)

            @block.vector
            def _(e):
                for f in self.engs["dve"].ops:
                    f(e)

            @block.gpsimd
            def _(e):
                for f in self.engs["pool"].ops:
                    f(e)

            @block.sync
            def _(e):
                for f in self.engs["sp"].ops:
                    f(e)


class Cfg:
    def __init__(self, D=2048, DFF=5632, QL=512, KL=512, H=8, G=8, SA=2048, SP=8192, NCORE=8, QG=1024):
        self.D, self.DFF, self.QL, self.KL, self.H, self.G = D, DFF, QL, KL, H, G
        self.SA, self.SP, self.NCORE, self.QG = SA, SP, NCORE, QG
        self.ND = D // 128
        self.NF = DFF // 128
        self.NQ = QL // 128
        self.NK = KL // 128
        self.GW = G * 128
        self.AW = H * 128
        self.MIXW = self.AW + self.GW
        self.MC = self.MIXW // 128
        self.IN_W = QL + KL + 64 + 2 * self.GW
        self.OWNP = SP // NCORE
        self.NOWN = 2 * SA + self.OWNP
        self.NTOK = 2 * SA + SP
        self.NBD = max(1, D // 512)
        self.WD = D // self.NBD
        self.NFT = self.NF // 4
        self.NQC = self.NF // 4
        self.QW = H * 192
        self.o_ckv = QL
        self.o_kr = QL + KL
        self.o_u = QL + KL + 64
        self.o_v = self.o_u + self.GW
        assert self.NOWN % 512 == 0 and self.NTOK % 512 == 0 and self.MC == self.ND
        assert SA % QG == 0 and self.OWNP % QG == 0 and QG % 512 == 0


CFG_FULL = Cfg()


def _split(start, width, step=512):
    out = []
    o = 0
    while o < width:
        w = min(step, width - o)
        out.append((start + o, w))
        o += w
    return out


def build(cfg):
    c = cfg
    D, ND, NF, NQ, NK, H, G = c.D, c.ND, c.NF, c.NQ, c.NK, c.H, c.G
    nc = bass.Bass("TRN2", target_bir_lowering=False)

    def din(name, shape, dt=F32):
        return nc.dram_tensor(name, list(shape), dt, kind="ExternalInput").ap()

    xall = din("xall", [c.NTOK, D])
    w_g = [din("w1g", [D, c.DFF]), din("w2g", [D, c.DFF])]
    w_u = [din("w1u", [D, c.DFF]), din("w2u", [D, c.DFF])]
    w_d = [din("w1d", [c.DFF, D]), din("w2d", [c.DFF, D])]
    w_in = din("win", [D, c.IN_W])
    w_qb = din("wqb", [c.QL, c.QW])
    w_kvb = din("wkvb", [c.KL, H * 256])
    w_o = din("wout", [c.MIXW, D])
    w_s = din("ws", [G, 128, 128])
    NG = 3 * ND + NQ + NK + c.MC
    gfm_d = din("gfm", [128, NG])
    gv_d = din("gvrow", [1, c.GW])
    gfin_d = din("gfinrow", [1, D])
    bsT_d = din("bsT", [128, G])
    ropeC_d = din("ropeC", [64, c.NTOK])
    ropeS_d = din("ropeS", [64, c.NTOK])
    ident_d = din("ident", [128, 128])
    y = nc.dram_tensor("y", [c.NOWN, D], F32, kind="ExternalOutput").ap()

    wg_s = [nc.dram_tensor(f"wgs{l}", [c.NFT, 128, ND, 512], BF16) for l in range(2)]
    wu_s = [nc.dram_tensor(f"wus{l}", [c.NFT, 128, ND, 512], BF16) for l in range(2)]
    wd_s = [nc.dram_tensor(f"wds{l}", [c.NBD, 4, 128, c.NQC, c.WD], BF16) for l in range(2)]
    win_s = nc.dram_tensor("wins", [128, ND, c.IN_W + 64], BF16)
    wq_s = nc.dram_tensor("wqs", [128, NQ, c.QW + H * 64], BF16)
    wk_s = nc.dram_tensor("wks", [128, NK, c.AW], BF16)
    wv_s = nc.dram_tensor("wvs", [128, NK, c.AW], BF16)
    wo_s = nc.dram_tensor("wos", [c.NBD, 128, c.MC, c.WD], BF16)
    x1_s = nc.dram_tensor("x1s", [c.NOWN, D], F32)
    h2_s = nc.dram_tensor("h2s", [c.NTOK, D], BF16)
    cqT_s = nc.dram_tensor("cqTs", [128, NQ, c.NOWN], BF16)
    ckT_s = nc.dram_tensor("ckTs", [128, NK, c.NTOK], BF16)
    krT_s = nc.dram_tensor("krTs", [64, c.NTOK], BF16)
    hgT_s = nc.dram_tensor("hgTs", [128, G, c.NOWN], BF16)
    hoT_s = nc.dram_tensor("hoTs", [128, H, c.NOWN], BF16)
    b_wscr = Buf("wscr")
    b_x1s, b_h2s, b_cqTs, b_ckTs, b_krTs, b_hgTs, b_hoTs, b_y = (Buf(n) for n in (
        "x1s", "h2s", "cqTs", "ckTs", "krTs", "hgTs", "hoTs", "y"))

    with ExitStack() as st:
        P = Prog(nc, st)
        banks = [P.psum(f"bk{i}", [128, 512], F32) for i in range(8)]
        bb = [Buf(f"bk{i}") for i in range(8)]

        def bf(i):
            return banks[i][:].bitcast(BF16)

        ident_f = P.sbuf("ident_f", [128, 128], F32)
        ident = P.sbuf("ident_b", [128, 128], BF16)
        ones_b = P.sbuf("ones_b", [128, 128], BF16)
        epsT = P.sbuf("epsT", [128, 1], F32)
        gfm = P.sbuf("gfm", [128, NG], F32)
        b_const = Buf("const")
        s_const = P.new_sem("s_const")
        P.dma("sp", ident_f[:], ident_d, s_const, writes=[b_const])
        P.dma("sp", gfm[:], gfm_d, s_const, writes=[b_const])
        b_ident = Buf("ident")
        P.op("dve", lambda e: e.tensor_copy(ident[:], ident_f[:]), reads=[b_const], writes=[b_ident])
        P.op("dve", lambda e: e.memset(ones_b[:], 1.0), writes=[b_ident])
        P.op("dve", lambda e: e.memset(epsT[:], EPS), writes=[b_ident])
        o_g1, o_gm, o_gq, o_gk, o_go, o_g2 = 0, ND, 2 * ND, 2 * ND + NQ, 2 * ND + NQ + NK, 2 * ND + NQ + NK + c.MC

        def copy(engname, out, in_, reads, writes):
            if engname == "act":
                P.op("act", lambda e: e.copy(out, in_), reads=reads, writes=writes)
            else:
                P.op(engname, lambda e: e.tensor_copy(out, in_), reads=reads, writes=writes)

        rr = {"n": 0}

        def alt(*names):
            rr["n"] += 1
            return names[rr["n"] % len(names)]

        STG = 6144
        STGB = 4096
        cur = {"stg": STG}

        def conv_jobs(src, dst, A, B, goff=None):
            a_step = max(1, cur["stg"] // B)
            out = []
            for a0 in range(0, A, a_step):
                an = min(a_step, A - a0)
                out.append((src[:, a0:a0 + an, :], dst[:, a0:a0 + an, :], an, B, None if goff is None else goff + a0))
            return out

        def fm(w, c0, cw):
            return w[:, c0:c0 + cw].rearrange("(k p) f -> p k f", p=128)

        def ffn_jobs(l):
            goff = o_g1 if l == 0 else o_g2
            jobs = []
            for j in range(c.NFT):
                jobs += conv_jobs(fm(w_g[l], j * 512, 512), wg_s[l][j], ND, 512, goff)
                jobs += conv_jobs(fm(w_u[l], j * 512, 512), wu_s[l][j], ND, 512, goff)
            for n in range(c.NBD):
                for q in range(4):
                    src = w_d[l][:, n * c.WD:(n + 1) * c.WD].rearrange("(c p) f -> p c f", p=128)
                    jobs += conv_jobs(src[:, q * c.NQC:(q + 1) * c.NQC, :], wd_s[l][n, q], c.NQC, c.WD)
            return jobs

        def other_jobs():
            jobs = []
            jobs += conv_jobs(fm(w_in, 0, c.IN_W), win_s[:, :, 0:c.IN_W], ND, c.IN_W, o_gm)
            jobs += conv_jobs(fm(w_in, c.o_kr + 32, 32), win_s[:, :, c.IN_W:c.IN_W + 32], ND, 32, o_gm)
            jobs += conv_jobs(fm(w_in, c.o_kr, 32), win_s[:, :, c.IN_W + 32:c.IN_W + 64], ND, 32, o_gm)
            jobs += conv_jobs(fm(w_qb, 0, c.QW), wq_s[:, :, 0:c.QW], NQ, c.QW, o_gq)
            for h in range(H):
                jobs += conv_jobs(fm(w_qb, h * 192 + 160, 32), wq_s[:, :, c.QW + h * 64:c.QW + h * 64 + 32], NQ, 32, o_gq)
                jobs += conv_jobs(fm(w_qb, h * 192 + 128, 32), wq_s[:, :, c.QW + h * 64 + 32:c.QW + h * 64 + 64], NQ, 32, o_gq)
                jobs += conv_jobs(fm(w_kvb, h * 256, 128), wk_s[:, :, h * 128:(h + 1) * 128], NK, 128, o_gk)
                jobs += conv_jobs(fm(w_kvb, h * 256 + 128, 128), wv_s[:, :, h * 128:(h + 1) * 128], NK, 128, o_gk)
            return jobs

        def wout_jobs():
            jobs = []
            for n in range(c.NBD):
                jobs += conv_jobs(fm(w_o, n * c.WD, c.WD), wo_s[n], c.MC, c.WD, o_go)
            return jobs

        def run_job(job, stg_t, cvo_t, b_s, b_c, s_s, s_c, ldq, stq, eng):
            src, dst, an, B, goff = job
            n = an * B
            sv = stg_t[:, 0:n].rearrange("p (a b) -> p a b", a=an)
            ov = cvo_t[:, 0:n].rearrange("p (a b) -> p a b", a=an)
            P.dma(ldq, sv, src, s_s, writes=[b_s])
            if goff is not None:
                gb = gfm[:, goff:goff + an].unsqueeze(2).broadcast_to([128, an, B])
                P.op(eng, lambda e, ov=ov, sv=sv, gb=gb: e.tensor_tensor(ov, sv, gb, ALU.mult),
                     reads=[b_s, b_const], writes=[b_c])
            else:
                P.op(eng, lambda e, ov=ov, sv=sv: e.tensor_copy(ov, sv), reads=[b_s], writes=[b_c])
            P.dma(stq, dst, ov, s_c, reads=[b_c], writes=[b_wscr])

        with ExitStack() as ph:
            NSB = 3
            stg = [P.sbuf(f"stg{i}", [128, STG], F32, ph) for i in range(NSB)]
            cvo = [P.sbuf(f"cvo{i}", [128, STG], BF16, ph) for i in range(NSB)]
            b_stg = [Buf(f"stg{i}") for i in range(NSB)]
            b_cvo = [Buf(f"cvo{i}") for i in range(NSB)]
            s_stg = [P.new_sem(f"s_stg{i}") for i in range(NSB)]
            s_cvo = [P.new_sem(f"s_cvo{i}") for i in range(NSB)]
            fg_jobs = ffn_jobs(0) + other_jobs()
            if getattr(cfg, 'bg_convert', False):
                cur["stg"] = STGB
                bg_jobs = wout_jobs() + ffn_jobs(1)
                cur["stg"] = STG
            else:
                fg_jobs = fg_jobs + wout_jobs() + ffn_jobs(1)
                bg_jobs = []
            for ji, job in enumerate(fg_jobs):
                i = ji % NSB
                run_job(job, stg[i], cvo[i], b_stg[i], b_cvo[i], s_stg[i], s_cvo[i], "sp", "act", "dve")
        P.barrier()
        if getattr(cfg, 'stop', None) == 'P0':
            P.emit()
            return nc

        def ffn_phase(ph, which):
            NS = 4
            slots = [P.sbuf(f"slot{which}{i}", [128, 8192], BF16, ph) for i in range(NS)]
            b_slot = [Buf(f"slot{i}") for i in range(NS)]
            s_slot = [P.new_sem(f"s_slot{i}") for i in range(NS)]
            sl = {"i": 0}
            hT = P.sbuf(f"hT{which}", [128, ND, 512], BF16, ph)
            b_hT = [Buf(f"hT{t}") for t in range(4)]
            aT = P.sbuf(f"aT{which}", [128, NF, 512], BF16, ph)
            b_aT = [Buf(f"aT{f}") for f in range(NF)]
            xt = [P.sbuf(f"xt{which}{t}", [128, D], F32, ph) for t in range(4)]
            b_xt = [Buf(f"xt{t}") for t in range(4)]
            s_xt = [P.new_sem(f"s_xt{t}") for t in range(4)]
            hn = [P.sbuf(f"hn{which}{i}", [128, D], BF16, ph) for i in range(2)]
            b_hn = [Buf(f"hn{i}") for i in range(2)]
            s_hn = [P.new_sem(f"s_hn{i}") for i in range(2)]
            junk = P.sbuf(f"junk{which}", [128, D], BF16, ph)
            b_junk = Buf("junk")
            sg = [P.sbuf(f"sg{which}{i}", [128, 512], F32, ph) for i in range(2)]
            b_sg = [Buf(f"sg{i}") for i in range(2)]
            st_ = [P.sbuf(f"stat{which}{i}", [128, 4], F32, ph) for i in range(4)]
            b_st = [Buf(f"stat{i}") for i in range(4)]
            cnt = {"hn": 0, "st": 0, "sg": 0, "g": 0, "u": 0, "tp": 0}

            def next_slot():
                i = sl["i"] % NS
                sl["i"] += 1
                return i

            def rstd_of(src_ap, src_buf, width):
                i = cnt["st"] % 4
                cnt["st"] += 1
                s = st_[i]
                P.op("act", lambda e: e.activation(junk[:, 0:width], src_ap, AF.Square, accum_out=s[:, 0:1]),
                     reads=[src_buf], writes=[b_junk, b_st[i]])
                P.op("act", lambda e: e.activation(s[:, 1:2], s[:, 0:1], AF.Sqrt, bias=epsT[:, 0:1], scale=1.0 / width),
                     reads=[b_st[i], b_ident], writes=[b_st[i]])
                P.op("dve", lambda e: e.reciprocal(s[:, 2:3], s[:, 1:2]), reads=[b_st[i]], writes=[b_st[i]])
                return s[:, 2:3], b_st[i]

            def norm_tile(t, store_h2=None):
                rstd, b_r = rstd_of(xt[t][:], b_xt[t], D)
                i = cnt["hn"] % 2
                cnt["hn"] += 1
                P.op("dve", lambda e: e.tensor_scalar_mul(hn[i][:], xt[t][:], rstd),
                     reads=[b_xt[t], b_r], writes=[b_hn[i]])
                if store_h2 is not None:
                    P.dma("pool", store_h2, hn[i][:], s_hn[i], reads=[b_hn[i]], writes=[b_h2s])
                    return
                transpose_into(hn[i], b_hn[i], ND, hT, b_hT[t], t)

            def transpose_into(src, b_src, nch, dstT, b_dst, t):
                for k0 in range(0, nch, 4):
                    kn = min(4, nch - k0)
                    bi = 4 + cnt["tp"] % 4
                    cnt["tp"] += 1
                    for j in range(kn):
                        P.op("pe", lambda e, j=j, bi=bi, k0=k0: e.transpose(
                            bf(bi)[:, j * 128:(j + 1) * 128], src[:, (k0 + j) * 128:(k0 + j + 1) * 128], ident[:]),
                            reads=[b_src, b_ident], writes=[bb[bi]], signal=(j == kn - 1))
                    ov = dstT[:, k0:k0 + kn, t * 128:(t + 1) * 128]
                    iv = bf(bi)[:, 0:kn * 128].rearrange("p (a b) -> p a b", a=kn)
                    copy(alt("dve", "act"), ov, iv, [bb[bi]], [b_dst])

            def ffn(l):
                for t in range(4):
                    norm_tile(t)
                for j in range(c.NFT):
                    sg_i = next_slot()
                    su_i = next_slot()
                    gv_ = slots[sg_i][:, 0:ND * 512].rearrange("p (k f) -> p k f", k=ND)
                    uv_ = slots[su_i][:, 0:ND * 512].rearrange("p (k f) -> p k f", k=ND)
                    P.dma("sp", gv_, wg_s[l][j], s_slot[sg_i], reads=[b_wscr], writes=[b_slot[sg_i]])
                    P.dma("sp", uv_, wu_s[l][j], s_slot[su_i], reads=[b_wscr], writes=[b_slot[su_i]])
                    for cc in range(4):
                        f = j * 4 + cc
                        bg = cnt["g"] % 2
                        cnt["g"] += 1
                        bu = 2 + cnt["u"] % 2
                        cnt["u"] += 1
                        for k in range(ND):
                            P.op("pe", lambda e, k=k, cc=cc, bg=bg, gv_=gv_: e.matmul(
                                banks[bg][:], gv_[:, k, cc * 128:(cc + 1) * 128], hT[:, k, :],
                                start=(k == 0), stop=(k == ND - 1)),
                                reads=[b_slot[sg_i]] + b_hT, writes=[bb[bg]], signal=(k == ND - 1))
                        for k in range(ND):
                            P.op("pe", lambda e, k=k, cc=cc, bu=bu, uv_=uv_: e.matmul(
                                banks[bu][:], uv_[:, k, cc * 128:(cc + 1) * 128], hT[:, k, :],
                                start=(k == 0), stop=(k == ND - 1)),
                                reads=[b_slot[su_i]] + b_hT, writes=[bb[bu]], signal=(k == ND - 1))
                        si = cnt["sg"] % 2
                        cnt["sg"] += 1
                        P.op("act", lambda e, si=si, bg=bg: e.activation(sg[si][:], banks[bg][:], AF.Silu),
                             reads=[bb[bg]], writes=[b_sg[si]])
                        P.op("dve", lambda e, si=si, bu=bu, f=f: e.tensor_tensor(
                            aT[:, f, :], banks[bu][:], sg[si][:], ALU.mult),
                            reads=[bb[bu], b_sg[si]], writes=[b_aT[f]])
                for n in range(c.NBD):
                    for q in range(4):
                        s_i = next_slot()
                        dv_ = slots[s_i][:, 0:c.NQC * c.WD].rearrange("p (c f) -> p c f", c=c.NQC)
                        P.dma("sp", dv_, wd_s[l][n, q], s_slot[s_i], reads=[b_wscr], writes=[b_slot[s_i]])
                        for t in range(4):
                            bi = 4 + t
                            for cc in range(c.NQC):
                                f = q * c.NQC + cc
                                first = (q == 0 and cc == 0)
                                last = (q == 3 and cc == c.NQC - 1)
                                P.op("pe", lambda e, t=t, cc=cc, f=f, bi=bi, first=first, last=last, dv_=dv_: e.matmul(
                                    banks[bi][:, 0:c.WD], aT[:, f, t * 128:(t + 1) * 128], dv_[:, cc, :],
                                    start=first, stop=last),
                                    reads=[b_slot[s_i], b_aT[f]], writes=[bb[bi]], signal=(last or cc == c.NQC - 1))
                    for t in range(4):
                        xs = xt[t][:, n * c.WD:(n + 1) * c.WD]
                        P.op("dve", lambda e, t=t, xs=xs: e.scalar_tensor_tensor(
                            xs, banks[4 + t][:, 0:c.WD], 0.5, xs, ALU.mult, ALU.add),
                            reads=[bb[4 + t], b_xt[t]], writes=[b_xt[t]])

            return dict(slots=slots, b_slot=b_slot, s_slot=s_slot, next_slot=next_slot, hT=hT, b_hT=b_hT,
                        xt=xt, b_xt=b_xt, s_xt=s_xt, norm_tile=norm_tile, ffn=ffn, rstd_of=rstd_of,
                        transpose_into=transpose_into, cnt=cnt)

        with ExitStack() as ph:
            F = ffn_phase(ph, "a")
            xt, b_xt, s_xt = F["xt"], F["b_xt"], F["s_xt"]
            if False:
                bstg = P.sbuf("bstg", [128, STGB], F32, ph)
                bcvo = P.sbuf("bcvo", [128, STGB], BF16, ph)
                b_bstg, b_bcvo = Buf("bstg"), Buf("bcvo")
                s_bstg, s_bcvo = P.new_sem("s_bstg"), P.new_sem("s_bcvo")
            nblk = c.NTOK // 512
            jpos = 0
            for b in range(nblk):
                tok0 = b * 512
                own = tok0 < c.NOWN
                for t in range(4):
                    P.dma("pool", xt[t][:], xall[tok0 + t * 128:tok0 + (t + 1) * 128, :], s_xt[t], writes=[b_xt[t]])
                nb_left = max(1, nblk - 1 - b)
                take = len(bg_jobs) - jpos if b >= nblk - 2 else -(-(len(bg_jobs) - jpos) // nb_left)
                take = 0
                F["ffn"](0)
                for t in range(4):
                    r0 = tok0 + t * 128
                    if own:
                        P.dma("pool", x1_s[r0:r0 + 128, :], xt[t][:], s_xt[t], reads=[b_xt[t]], writes=[b_x1s])
                    F["norm_tile"](t, store_h2=h2_s[r0:r0 + 128, :])
        P.barrier()
        if getattr(cfg, 'stop', None) == 'P1a':
            P.emit()
            return nc

        with ExitStack() as ph:
            winT = P.sbuf("winT", [128, ND, c.IN_W + 64], BF16, ph)
            b_win = Buf("winT")
            s_win = P.new_sem("s_win")
            for k0 in range(0, ND, 4):
                P.dma("sp", winT[:, k0:k0 + 4, :], win_s[:, k0:k0 + 4, :], s_win, reads=[b_wscr], writes=[b_win])
            sqv = P.sbuf("sqv", [128, c.GW], F32, ph)
            b_sqv = Buf("sqv")
            wsf = sqv[:].rearrange("p (g c) -> p g c", g=G)
            wsb = P.sbuf("wsb", [128, G, 128], BF16, ph)
            wsT = P.sbuf("wsT", [128, G, 128], BF16, ph)
            gvB = P.sbuf("gvB", [128, c.GW], F32, ph)
            bsB = P.sbuf("bsB", [128, c.GW], F32, ph)
            bsT = P.sbuf("bsTt", [128, G], F32, ph)
            b_gc = Buf("gconst")
            s_gc = P.new_sem("s_gc")
            P.dma("sp", wsf, w_s.rearrange("g i j -> i g j"), s_gc, writes=[b_gc, b_sqv])
            P.dma("sp", gvB[:], gv_d.partition_broadcast(128), s_gc, writes=[b_gc])
            P.dma("sp", bsT[:], bsT_d, s_gc, writes=[b_gc])
            b_gc2 = Buf("gconst2")
            P.op("dve", lambda e: e.tensor_copy(wsb[:], wsf), reads=[b_gc, b_sqv], writes=[b_gc2])
            P.op("dve", lambda e: e.tensor_copy(bsB[:].rearrange("p (g c) -> p g c", g=G),
                                                bsT[:].unsqueeze(2).broadcast_to([128, G, 128])),
                 reads=[b_gc], writes=[b_gc2])
            b_wsT = Buf("wsT")
            for g in range(G):
                bi = 6 + g % 2
                P.op("pe", lambda e, g=g, bi=bi: e.transpose(bf(bi)[:, 0:128], wsb[:, g, :], ident[:]),
                     reads=[b_gc2, b_ident], writes=[bb[bi]])
                copy("dve", wsT[:, g, :], bf(bi)[:, 0:128], [bb[bi]], [b_wsT])

            h2tm = [P.sbuf(f"h2tm{i}", [128, D], BF16, ph) for i in range(2)]
            b_h2tm = [Buf(f"h2tm{i}") for i in range(2)]
            s_h2tm = [P.new_sem(f"s_h2tm{i}") for i in range(2)]
            h2T = [P.sbuf(f"h2T{i}", [128, ND, 512], BF16, ph) for i in range(1)]
            b_h2T = [[Buf(f"h2T{i}_{t}") for t in range(4)] for i in range(1)]
            junk = P.sbuf("junkb", [128, 1024], BF16, ph)
            b_junk = Buf("junkb")
            stt = [P.sbuf(f"sttb{i}", [128, 4], F32, ph) for i in range(8)]
            b_stt = [Buf(f"sttb{i}") for i in range(8)]
            ln = [P.sbuf(f"lnb{i}", [128, 512], BF16, ph) for i in range(4)]
            b_ln = [Buf(f"lnb{i}") for i in range(4)]
            ckT = [P.sbuf(f"ckTb{i}", [128, NK, 512], BF16, ph) for i in range(2)]
            b_ckT = [Buf(f"ckTb{i}") for i in range(2)]
            s_ckT = [P.new_sem(f"s_ckTb{i}") for i in range(2)]
            cqT = [P.sbuf(f"cqTb{i}", [128, NQ, 512], BF16, ph) for i in range(1)] * 2
            b_cqT = [Buf(f"cqTb{i}") for i in range(1)] * 2
            s_cqT = [P.new_sem(f"s_cqTb{i}") for i in range(1)] * 2
            hgT = [P.sbuf(f"hgTb{i}", [128, G, 512], BF16, ph) for i in range(1)] * 2
            b_hgT = [Buf(f"hgTb{i}") for i in range(1)] * 2
            s_hgT = [P.new_sem(f"s_hgTb{i}") for i in range(1)] * 2
            rc = [P.sbuf(f"rcb{i}", [64, 512], F32, ph) for i in range(1)] * 2
            rs = [P.sbuf(f"rsb{i}", [64, 512], F32, ph) for i in range(1)] * 2
            b_rt = [Buf(f"rtb{i}") for i in range(1)] * 2
            s_rt = [P.new_sem(f"s_rtb{i}") for i in range(1)] * 2
            kt1 = P.sbuf("kt1", [64, 512], F32, ph)
            kt2 = P.sbuf("kt2", [64, 512], F32, ph)
            b_kt = Buf("kt")
            krb = [P.sbuf(f"krb{i}", [64, 512], BF16, ph) for i in range(2)]
            b_krb = [Buf(f"krb{i}") for i in range(2)]
            s_krb = [P.new_sem(f"s_krb{i}") for i in range(2)]
            gu = [P.sbuf(f"gu{i}", [128, c.GW], F32, ph) for i in range(2)]
            gvv = [P.sbuf(f"gvv{i}", [128, c.GW], F32, ph) for i in range(2)]
            b_gu = [Buf(f"gu{i}") for i in range(2)]
            b_gvv = [Buf(f"gvv{i}") for i in range(2)]
            ssg = [P.sbuf(f"ssg{i}", [128, 3 * G], F32, ph) for i in range(2)]
            b_ssg = [Buf(f"ssg{i}") for i in range(2)]
            vn = [P.sbuf(f"vn{i}", [128, c.GW], BF16, ph) for i in range(2)]
            b_vn = [Buf(f"vn{i}") for i in range(2)]
            og = [P.sbuf(f"og{i}", [128, c.GW], F32, ph) for i in range(1)] * 2
            b_og = [Buf(f"og{i}") for i in range(1)] * 2
            hgn = [P.sbuf(f"hgn{i}", [128, c.GW], BF16, ph) for i in range(2)]
            b_hgn = [Buf(f"hgn{i}") for i in range(2)]
            cn = {"bk": 0, "tp": 0, "st": 0, "ln": 0}

            def nbank():
                i = cn["bk"] % 6
                cn["bk"] += 1
                return i

            def tbank():
                i = 6 + cn["tp"] % 2
                cn["tp"] += 1
                return i

            def transp(src, b_src, nch, dstT, b_dst, t):
                for k0 in range(0, nch, 4):
                    kn = min(4, nch - k0)
                    bi = tbank()
                    for j in range(kn):
                        P.op("pe", lambda e, j=j, bi=bi, k0=k0: e.transpose(
                            bf(bi)[:, j * 128:(j + 1) * 128], src[:, (k0 + j) * 128:(k0 + j + 1) * 128], ident[:]),
                            reads=[b_src, b_ident], writes=[bb[bi]], signal=(j == kn - 1))
                    ov = dstT[:, k0:k0 + kn, t * 128:(t + 1) * 128]
                    iv = bf(bi)[:, 0:kn * 128].rearrange("p (a b) -> p a b", a=kn)
                    copy(alt("dve", "act"), ov, iv, [bb[bi]], [b_dst])

            def latent(hi, t, col0, width, nch, dstT, b_dst):
                bi = nbank()
                for k in range(ND):
                    P.op("pe", lambda e, k=k, bi=bi: e.matmul(
                        banks[bi][:, 0:width], h2T[hi][:, k, t * 128:(t + 1) * 128], winT[:, k, col0:col0 + width],
                        start=(k == 0), stop=(k == ND - 1)),
                        reads=[b_h2T[hi][t], b_win], writes=[bb[bi]], signal=(k == ND - 1))
                si = cn["st"] % 8
                cn["st"] += 1
                s = stt[si]
                P.op("act", lambda e: e.activation(junk[:, 0:width], banks[bi][:, 0:width], AF.Square, accum_out=s[:, 0:1]),
                     reads=[bb[bi]], writes=[b_junk, b_stt[si]])
                P.op("act", lambda e: e.activation(s[:, 1:2], s[:, 0:1], AF.Sqrt, bias=epsT[:, 0:1], scale=1.0 / width),
                     reads=[b_stt[si], b_ident], writes=[b_stt[si]])
                P.op("dve", lambda e: e.reciprocal(s[:, 2:3], s[:, 1:2]), reads=[b_stt[si]], writes=[b_stt[si]])
                li = cn["ln"] % 4
                cn["ln"] += 1
                P.op("dve", lambda e: e.tensor_scalar_mul(ln[li][:, 0:width], banks[bi][:, 0:width], s[:, 2:3]),
                     reads=[bb[bi], b_stt[si]], writes=[b_ln[li]])
                return lambda: transp(ln[li], b_ln[li], nch, dstT, b_dst, t)

            def gmlp_stage1(hi, t, gi):
                for (col0, w) in _split(c.o_u, c.GW):
                    bi = nbank()
                    for k in range(ND):
                        P.op("pe", lambda e, k=k, bi=bi, col0=col0, w=w: e.matmul(
                            banks[bi][:, 0:w], h2T[hi][:, k, t * 128:(t + 1) * 128], winT[:, k, col0:col0 + w],
                            start=(k == 0), stop=(k == ND - 1)),
                            reads=[b_h2T[hi][t], b_win], writes=[bb[bi]], signal=(k == ND - 1))
                    o0 = col0 - c.o_u
                    P.op("act", lambda e, bi=bi, o0=o0, w=w: e.activation(gu[gi][:, o0:o0 + w], banks[bi][:, 0:w], AF.Gelu_apprx_tanh),
                         reads=[bb[bi]], writes=[b_gu[gi]])
                for (col0, w) in _split(c.o_v, c.GW):
                    bi = nbank()
                    for k in range(ND):
                        P.op("pe", lambda e, k=k, bi=bi, col0=col0, w=w: e.matmul(
                            banks[bi][:, 0:w], h2T[hi][:, k, t * 128:(t + 1) * 128], winT[:, k, col0:col0 + w],
                            start=(k == 0), stop=(k == ND - 1)),
                            reads=[b_h2T[hi][t], b_win], writes=[bb[bi]], signal=(k == ND - 1))
                    o0 = col0 - c.o_v
                    P.op("act", lambda e, bi=bi, o0=o0, w=w: e.activation(gvv[gi][:, o0:o0 + w], banks[bi][:, 0:w], AF.Gelu_apprx_tanh),
                         reads=[bb[bi]], writes=[b_gvv[gi]])
                s = ssg[gi]
                P.op("dve", lambda e: e.tensor_tensor(sqv[:], gvv[gi][:], gvv[gi][:], ALU.mult),
                     reads=[b_gvv[gi]], writes=[b_sqv])
                P.op("dve", lambda e: e.tensor_reduce(s[:, 0:G], sqv[:].rearrange("p (g c) -> p g c", g=G), AX.X, ALU.add),
                     reads=[b_sqv], writes=[b_ssg[gi]])
                P.op("act", lambda e: e.activation(s[:, G:2 * G], s[:, 0:G], AF.Sqrt, bias=epsT[:, 0:1], scale=1.0 / 128),
                     reads=[b_ssg[gi], b_ident], writes=[b_ssg[gi]])
                P.op("dve", lambda e: e.reciprocal(s[:, 2 * G:3 * G], s[:, G:2 * G]), reads=[b_ssg[gi]], writes=[b_ssg[gi]])
                P.op("dve", lambda e: e.tensor_tensor(
                    vn[gi][:].rearrange("p (g c) -> p g c", g=G), gvv[gi][:].rearrange("p (g c) -> p g c", g=G),
                    s[:, 2 * G:3 * G].unsqueeze(2).broadcast_to([128, G, 128]), ALU.mult),
                    reads=[b_gvv[gi], b_ssg[gi]], writes=[b_vn[gi]])

            def gmlp_stage2(t, gi, oi):
                halves = _split(0, c.GW)
                for (c0, w) in halves:
                    bi = nbank()
                    for g in range(c0 // 128, (c0 + w) // 128):
                        P.op("pe", lambda e, g=g, bi=bi, c0=c0: e.matmul(
                            banks[bi][:, g * 128 - c0:(g + 1) * 128 - c0], wsT[:, g, :], vn[gi][:, g * 128:(g + 1) * 128],
                            start=True, stop=True),
                            reads=[b_wsT, b_vn[gi]], writes=[bb[bi]], signal=(g == (c0 + w) // 128 - 1))
                    P.op("dve", lambda e, bi=bi, c0=c0, w=w: e.tensor_tensor(
                        og[gi][:, c0:c0 + w], banks[bi][:, 0:w], gvB[:, c0:c0 + w], ALU.mult),
                        reads=[bb[bi], b_gc], writes=[b_og[gi]])
                P.op("pool", lambda e: e.tensor_tensor(og[gi][:], og[gi][:], bsB[:], ALU.add),
                     reads=[b_og[gi], b_gc2], writes=[b_og[gi]])
                P.op("pool", lambda e: e.tensor_tensor(og[gi][:], og[gi][:], gu[gi][:], ALU.mult),
                     reads=[b_og[gi], b_gu[gi]], writes=[b_og[gi]])
                si = cn["st"] % 8
                cn["st"] += 1
                s = stt[si]
                P.op("act", lambda e: e.activation(junk[:, 0:c.GW], og[gi][:], AF.Square, accum_out=s[:, 0:1]),
                     reads=[b_og[gi]], writes=[b_junk, b_stt[si]])
                P.op("act", lambda e: e.activation(s[:, 1:2], s[:, 0:1], AF.Sqrt, bias=epsT[:, 0:1], scale=1.0 / c.GW),
                     reads=[b_stt[si], b_ident], writes=[b_stt[si]])
                P.op("dve", lambda e: e.reciprocal(s[:, 2:3], s[:, 1:2]), reads=[b_stt[si]], writes=[b_stt[si]])
                P.op("dve", lambda e: e.tensor_scalar_mul(hgn[gi][:], og[gi][:], s[:, 2:3]),
                     reads=[b_og[gi], b_stt[si]], writes=[b_hgn[gi]])
                return lambda: transp(hgn[gi], b_hgn[gi], G, hgT[oi], b_hgT[oi], t)

            gcount = 0
            for b in range(c.NTOK // 512):
                tok0 = b * 512
                own = tok0 < c.NOWN
                hi = 0
                oi = b % 2
                for t in range(2):
                    r0 = tok0 + t * 128
                    P.dma("pool", h2tm[t][:], h2_s[r0:r0 + 128, :], s_h2tm[t], reads=[b_h2s], writes=[b_h2tm[t]])
                P.dma("pool", rc[oi][:], ropeC_d[:, tok0:tok0 + 512], s_rt[oi], writes=[b_rt[oi]])
                P.dma("pool", rs[oi][:], ropeS_d[:, tok0:tok0 + 512], s_rt[oi], writes=[b_rt[oi]])
                pend = []
                for t in range(4):
                    transp(h2tm[t % 2], b_h2tm[t % 2], ND, h2T[hi], b_h2T[hi][t], t)
                    if t + 2 < 4:
                        r0 = tok0 + (t + 2) * 128
                        P.dma("pool", h2tm[t % 2][:], h2_s[r0:r0 + 128, :], s_h2tm[t % 2], reads=[b_h2s], writes=[b_h2tm[t % 2]])
                    nxt = []
                    nxt.append(latent(hi, t, c.o_ckv, c.KL, NK, ckT[oi], b_ckT[oi]))
                    if own:
                        nxt.append(latent(hi, t, 0, c.QL, NQ, cqT[oi], b_cqT[oi]))
                        gi = gcount % 2
                        gcount += 1
                        gmlp_stage1(hi, t, gi)
                        nxt.append(("g", t, gi))
                    for it in pend:
                        if isinstance(it, tuple):
                            pend2 = gmlp_stage2(it[1], it[2], oi)
                            nxt.append(pend2)
                        else:
                            it()
                    pend = nxt
                while pend:
                    nxt = []
                    for it in pend:
                        if isinstance(it, tuple):
                            nxt.append(gmlp_stage2(it[1], it[2], oi))
                        else:
                            it()
                    pend = nxt
                ba, bbk = nbank(), nbank()
                for (bi, coff) in ((ba, c.o_kr), (bbk, c.IN_W)):
                    for k in range(ND):
                        P.op("pe", lambda e, k=k, bi=bi, coff=coff, hi=hi: e.matmul(
                            banks[bi][0:64, :], winT[:, k, coff:coff + 64], h2T[hi][:, k, :],
                            start=(k == 0), stop=(k == ND - 1)),
                            reads=b_h2T[hi] + [b_win], writes=[bb[bi]], signal=(k == ND - 1))
                P.op("dve", lambda e, ba=ba, oi=oi: e.tensor_tensor(kt1[:], banks[ba][0:64, :], rc[oi][:], ALU.mult),
                     reads=[bb[ba], b_rt[oi]], writes=[b_kt])
                P.op("dve", lambda e, bbk=bbk, oi=oi: e.tensor_tensor(kt2[:], banks[bbk][0:64, :], rs[oi][:], ALU.mult),
                     reads=[bb[bbk], b_rt[oi]], writes=[b_kt])
                P.op("dve", lambda e, oi=oi: e.tensor_tensor(krb[oi][:], kt1[:], kt2[:], ALU.add),
                     reads=[b_kt], writes=[b_krb[oi]])
                P.dma("sp", krT_s[:, tok0:tok0 + 512], krb[oi][:], s_krb[oi], reads=[b_krb[oi]], writes=[b_krTs])
                P.dma("sp", ckT_s[:, :, tok0:tok0 + 512], ckT[oi][:], s_ckT[oi], reads=[b_ckT[oi]], writes=[b_ckTs])
                if own:
                    P.dma("sp", cqT_s[:, :, tok0:tok0 + 512], cqT[oi][:], s_cqT[oi], reads=[b_cqT[oi]], writes=[b_cqTs])
                    P.dma("sp", hgT_s[:, :, tok0:tok0 + 512], hgT[oi][:], s_hgT[oi], reads=[b_hgT[oi]], writes=[b_hgTs])
        P.barrier()
        if getattr(cfg, 'stop', None) == 'P1b':
            P.emit()
            return nc

        with ExitStack() as ph:
            wq = P.sbuf("wq", [128, NQ, c.QW + H * 64], BF16, ph)
            wk = P.sbuf("wk", [128, NK, c.AW], BF16, ph)
            wv = P.sbuf("wv", [128, NK, c.AW], BF16, ph)
            b_aw = Buf("attw")
            s_aw = P.new_sem("s_aw")
            P.dma("sp", wq[:], wq_s[:, :, :], s_aw, reads=[b_wscr], writes=[b_aw])
            P.dma("sp", wk[:], wk_s[:, :, :], s_aw, reads=[b_wscr], writes=[b_aw])
            P.dma("sp", wv[:], wv_s[:, :, :], s_aw, reads=[b_wscr], writes=[b_aw])
            SKMAX = max(c.SA, c.SP)
            QG = c.QG
            krT = P.sbuf("krT", [64, SKMAX], BF16, ph)
            b_krT = Buf("krT")
            s_krT = P.new_sem("s_krT")
            cqg = P.sbuf("cqg", [128, NQ, QG], BF16, ph)
            b_cqg = Buf("cqg")
            s_cqg = P.new_sem("s_cqg")
            rcq = P.sbuf("rcq", [64, QG], F32, ph)
            rsq = P.sbuf("rsq", [64, QG], F32, ph)
            b_rq = Buf("rq")
            s_rq = P.new_sem("s_rq")
            KT = P.sbuf("KT", [128, SKMAX], BF16, ph)
            b_KT = Buf("KT")
            V = P.sbuf("V", [128, SKMAX // 128, 128], BF16, ph)
            b_V = Buf("V")
            NCK = 4
            ckc = [P.sbuf(f"ckc{i}", [128, NK, 512], BF16, ph) for i in range(NCK)]
            b_ckc = [Buf(f"ckc{i}") for i in range(NCK)]
            s_ckc = [P.new_sem(f"s_ckc{i}") for i in range(NCK)]
            QnT = [P.sbuf(f"QnT{i}", [128, 512], BF16, ph) for i in range(2)]
            QrT = [P.sbuf(f"QrT{i}", [64, 512], BF16, ph) for i in range(2)]
            b_Q = [Buf(f"Q{i}") for i in range(2)]
            qt1 = P.sbuf("qt1", [64, 512], F32, ph)
            qt2 = P.sbuf("qt2", [64, 512], F32, ph)
            b_qt = Buf("qt")
            NPT = 4
            PT = [P.sbuf(f"PT{i}", [128, 512], BF16, ph) for i in range(NPT)]
            b_PT = [Buf(f"PT{i}") for i in range(NPT)]
            PS = [P.sbuf(f"PS{i}", [128, 512], BF16, ph) for i in range(2)]
            b_PS = [Buf(f"PS{i}") for i in range(2)]
            OT = P.sbuf("OT", [128, H, QG], F32, ph)
            b_OT = [Buf(f"OT{h}") for h in range(H)]
            rl = P.sbuf("rl", [128, 512], F32, ph)
            b_rl = Buf("rl")
            sqo = [P.sbuf(f"sqo{i}", [128, 512], BF16, ph) for i in range(2)]
            b_sqo = [Buf(f"sqo{i}") for i in range(2)]
            rso = P.sbuf("rso", [128, 512], F32, ph)
            b_rso = Buf("rso")
            hoT = [P.sbuf(f"hoT{i}", [128, H, 512], BF16, ph) for i in range(2)]
            b_hoT = [Buf(f"hoT{i}") for i in range(2)]
            s_hoT = [P.new_sem(f"s_hoT{i}") for i in range(2)]
            scale = 1.0 / math.sqrt(192.0)
            ct = {"ck": 0, "q": 0, "s": 0, "pt": 0, "ol": 0, "ho": 0, "sq": 0, "mb": 0}
            if bg_jobs:
                bstg = P.sbuf("bstg2", [128, STGB], F32, ph)
                bcvo = P.sbuf("bcvo2", [128, STGB], BF16, ph)
                b_bstg, b_bcvo = Buf("bstg2"), Buf("bcvo2")
                s_bstg, s_bcvo = P.new_sem("s_bstg2"), P.new_sem("s_bcvo2")
            bgp = {"pos": 0, "it": 0}

            def mbk():
                ct["mb"] += 1
                return (2, 7)[ct["mb"] % 2]

            items = []
            for s_ in range(2):
                for qg in range(c.SA // QG):
                    items.append((s_ * c.SA, c.SA, s_ * c.SA + qg * QG))
            for qg in range(c.OWNP // QG):
                items.append((2 * c.SA, c.SP, 2 * c.SA + qg * QG))
            prev_k0 = None
            for (k0, Sk, q0) in items[:getattr(cfg, 'dbg_items', 99)]:
                if k0 != prev_k0:
                    P.dma("pool", krT[:, 0:Sk], krT_s[:, k0:k0 + Sk], s_krT, reads=[b_krTs], writes=[b_krT])
                    prev_k0 = k0
                P.dma("pool", cqg[:], cqT_s[:, :, q0:q0 + QG], s_cqg, reads=[b_cqTs], writes=[b_cqg])
                P.dma("pool", rcq[:], ropeC_d[:, q0:q0 + QG], s_rq, writes=[b_rq])
                P.dma("pool", rsq[:], ropeS_d[:, q0:q0 + QG], s_rq, writes=[b_rq])
                for h in range(getattr(cfg, 'dbg_heads', H)):
                    n_it = len(items) * H
                    left = max(1, n_it - bgp["it"] - 2)
                    take = -(-(len(bg_jobs) - bgp["pos"]) // left) if bgp["it"] < n_it - 2 else len(bg_jobs) - bgp["pos"]
                    for job in bg_jobs[bgp["pos"]:bgp["pos"] + take]:
                        run_job(job, bstg, bcvo, b_bstg, b_bcvo, s_bstg, s_bcvo, "sp", "sp", "dve")
                    bgp["pos"] += take
                    bgp["it"] += 1
                    for kb in range(Sk // 512):
                        ci = ct["ck"] % NCK
                        ct["ck"] += 1
                        P.dma("sp", ckc[ci][:], ckT_s[:, :, k0 + kb * 512:k0 + (kb + 1) * 512], s_ckc[ci],
                              reads=[b_ckTs], writes=[b_ckc[ci]])
                        m = mbk()
                        for k in range(NK):
                            P.op("pe", lambda e, k=k, ci=ci, h=h, m=m: e.matmul(
                                banks[m][:], wk[:, k, h * 128:(h + 1) * 128], ckc[ci][:, k, :],
                                start=(k == 0), stop=(k == NK - 1)),
                                reads=[b_aw, b_ckc[ci]], writes=[bb[m]], signal=(k == NK - 1))
                        copy("dve", KT[:, kb * 512:(kb + 1) * 512], banks[m][:], [bb[m]], [b_KT])
                        m = mbk()
                        for j in range(4):
                            for k in range(NK):
                                P.op("pe", lambda e, k=k, j=j, ci=ci, h=h, m=m: e.matmul(
                                    banks[m][:, j * 128:(j + 1) * 128], ckc[ci][:, k, j * 128:(j + 1) * 128],
                                    wv[:, k, h * 128:(h + 1) * 128], start=(k == 0), stop=(k == NK - 1)),
                                    reads=[b_aw, b_ckc[ci]], writes=[bb[m]], signal=(j == 3 and k == NK - 1))
                        copy("act", V[:, kb * 4:(kb + 1) * 4, :], banks[m][:].rearrange("p (a b) -> p a b", a=4),
                             [bb[m]], [b_V])
                    qis = {}
                    for qb in range(QG // 512 if getattr(cfg, 'dbg_p2', 9) >= 2 else 0):
                        qi = ct["q"] % 2
                        ct["q"] += 1
                        qis[qb] = qi
                        qs = slice(qb * 512, (qb + 1) * 512)
                        m = mbk()
                        for k in range(NQ):
                            P.op("pe", lambda e, k=k, h=h, qs=qs, m=m: e.matmul(
                                banks[m][:], wq[:, k, h * 192:h * 192 + 128], cqg[:, k, qs],
                                start=(k == 0), stop=(k == NQ - 1)),
                                reads=[b_aw, b_cqg], writes=[bb[m]], signal=(k == NQ - 1))
                        P.op("act", lambda e, qi=qi, m=m: e.mul(QnT[qi][:], banks[m][:], scale), reads=[bb[m]], writes=[b_Q[qi]])
                        for (coff, dst, tab) in ((h * 192 + 128, qt1, rcq), (c.QW + h * 64, qt2, rsq)):
                            m = mbk()
                            for k in range(NQ):
                                P.op("pe", lambda e, k=k, coff=coff, qs=qs, m=m: e.matmul(
                                    banks[m][0:64, :], wq[:, k, coff:coff + 64], cqg[:, k, qs],
                                    start=(k == 0), stop=(k == NQ - 1)),
                                    reads=[b_aw, b_cqg], writes=[bb[m]], signal=(k == NQ - 1))
                            P.op("dve", lambda e, dst=dst, tab=tab, qs=qs, m=m: e.scalar_tensor_tensor(
                                dst[:], banks[m][0:64, :], scale, tab[:, qs], ALU.mult, ALU.mult),
                                reads=[bb[m], b_rq], writes=[b_qt])
                        P.op("dve", lambda e, qi=qi: e.tensor_tensor(QrT[qi][:], qt1[:], qt2[:], ALU.add),
                             reads=[b_qt], writes=[b_Q[qi]])
                    for qb in range(QG // 512 if getattr(cfg, 'dbg_p2', 9) >= 2 else 0):
                        qi = qis[qb]
                        qs = slice(qb * 512, (qb + 1) * 512)
                        oli = ct["ol"] % 2
                        ct["ol"] += 1
                        bo, bl = 3 + 2 * oli, 4 + 2 * oli
                        nkt = Sk // 128
                        pend = None

                        def ol(kt, pi, bo=bo, bl=bl, nkt=nkt):
                            P.op("pe", lambda e: e.matmul(banks[bo][:], V[:, kt, :], PT[pi][:],
                                                          start=(kt == 0), stop=(kt == nkt - 1)),
                                 reads=[b_V, b_PT[pi]], writes=[bb[bo]], signal=(kt == nkt - 1))
                            if kt % 2 == 1:
                                pj = (pi - 1) % NPT
                                psi = (kt // 2) % 2
                                P.op("dve", lambda e: e.tensor_tensor(PS[psi][:], PT[pj][:], PT[pi][:], ALU.add),
                                     reads=[b_PT[pj], b_PT[pi]], writes=[b_PS[psi]])
                                P.op("pe", lambda e: e.matmul(banks[bl][:], ones_b[:], PS[psi][:],
                                                              start=(kt == 1), stop=(kt == nkt - 1)),
                                     reads=[b_ident, b_PS[psi]], writes=[bb[bl]], signal=True)

                        if getattr(cfg, 'dbg_p2', 9) < 3:
                            continue
                        for kt in range(nkt):
                            si = ct["s"] % 2
                            ct["s"] += 1
                            P.op("pe", lambda e, kt=kt, si=si, qi=qi: e.matmul(
                                banks[si][:], KT[:, kt * 128:(kt + 1) * 128], QnT[qi][:], start=True, stop=False),
                                reads=[b_KT, b_Q[qi]], writes=[bb[si]], signal=False)
                            P.op("pe", lambda e, kt=kt, si=si, qi=qi: e.matmul(
                                banks[si][:], krT[:, kt * 128:(kt + 1) * 128], QrT[qi][:], start=False, stop=True),
                                reads=[b_krT, b_Q[qi]], writes=[bb[si]], signal=True)
                            pi = ct["pt"] % NPT
                            ct["pt"] += 1
                            P.op("act", lambda e, si=si, pi=pi: e.activation(PT[pi][:], banks[si][:], AF.Exp),
                                 reads=[bb[si]], writes=[b_PT[pi]])
                            if pend is not None:
                                ol(*pend)
                            pend = (kt, pi)
                        ol(*pend)
                        P.op("dve", lambda e, bl=bl: e.reciprocal(rl[:], banks[bl][:]), reads=[bb[bl]], writes=[b_rl])
                        P.op("dve", lambda e, bo=bo, h=h, qs=qs: e.tensor_tensor(OT[:, h, qs], banks[bo][:], rl[:], ALU.mult),
                             reads=[bb[bo], b_rl], writes=[b_OT[h]])
                for qb in range(QG // 512 if getattr(cfg, 'dbg_p2', 9) >= 4 else 0):
                    qs = slice(qb * 512, (qb + 1) * 512)
                    m = mbk()
                    for h in range(H):
                        qi2 = ct["sq"] % 2
                        ct["sq"] += 1
                        P.op(alt("dve", "pool"), lambda e, h=h, qi2=qi2, qs=qs: e.tensor_tensor(
                            sqo[qi2][:], OT[:, h, qs], OT[:, h, qs], ALU.mult),
                            reads=[b_OT[h]], writes=[b_sqo[qi2]])
                        P.op("pe", lambda e, h=h, qi2=qi2, m=m: e.matmul(banks[m][:], ones_b[:], sqo[qi2][:],
                                                                         start=(h == 0), stop=(h == H - 1)),
                             reads=[b_ident, b_sqo[qi2]], writes=[bb[m]], signal=True)
                    P.op("act", lambda e, m=m: e.activation(rso[:], banks[m][:], AF.Sqrt, bias=epsT[:, 0:1], scale=1.0 / c.AW),
                         reads=[bb[m], b_ident], writes=[b_rso])
                    P.op("dve", lambda e: e.reciprocal(rso[:], rso[:]), reads=[b_rso], writes=[b_rso])
                    oi = ct["ho"] % 2
                    ct["ho"] += 1
                    for h in range(H):
                        P.op(alt("dve", "pool"), lambda e, h=h, oi=oi, qs=qs: e.tensor_tensor(
                            hoT[oi][:, h, :], OT[:, h, qs], rso[:], ALU.mult),
                            reads=[b_OT[h], b_rso], writes=[b_hoT[oi]])
                    P.dma("pool", hoT_s[:, :, q0 + qb * 512:q0 + (qb + 1) * 512], hoT[oi][:], s_hoT[oi],
                          reads=[b_hoT[oi]], writes=[b_hoTs])
        P.barrier()
        if getattr(cfg, 'stop', None) == 'P2':
            P.emit()
            return nc

        with ExitStack() as ph:
            F = ffn_phase(ph, "c")
            xt, b_xt, s_xt, hT, b_hT = F["xt"], F["b_xt"], F["s_xt"], F["hT"], F["b_hT"]
            slots, b_slot, s_slot = F["slots"], F["b_slot"], F["s_slot"]
            gfin = P.sbuf("gfin", [128, D], F32, ph)
            b_gfin = Buf("gfin")
            s_gfin = P.new_sem("s_gfin")
            P.dma("sp", gfin[:], gfin_d.partition_broadcast(128), s_gfin, writes=[b_gfin])
            yo = [P.sbuf(f"yo{i}", [128, D], F32, ph) for i in range(2)]
            b_yo = [Buf(f"yo{i}") for i in range(2)]
            s_yo = [P.new_sem(f"s_yo{i}") for i in range(2)]
            s_hc = P.new_sem("s_hc")
            yc = 0
            for b in range(c.NOWN // 512):
                tok0 = b * 512
                P.dma("pool", hT[:, 0:H, :], hoT_s[:, :, tok0:tok0 + 512], s_hc, reads=[b_hoTs], writes=b_hT)
                P.dma("pool", hT[:, H:H + G, :], hgT_s[:, :, tok0:tok0 + 512], s_hc, reads=[b_hgTs], writes=b_hT)
                for t in range(4):
                    r0 = tok0 + t * 128
                    P.dma("pool", xt[t][:], x1_s[r0:r0 + 128, :], s_xt[t], reads=[b_x1s], writes=[b_xt[t]])
                for n in range(c.NBD):
                    s_i = F["next_slot"]()
                    ov_ = slots[s_i][:, 0:c.MC * c.WD].rearrange("p (k f) -> p k f", k=c.MC)
                    P.dma("sp", ov_, wo_s[n], s_slot[s_i], reads=[b_wscr], writes=[b_slot[s_i]])
                    for t in range(4):
                        bi = t % 4
                        for k in range(c.MC):
                            P.op("pe", lambda e, k=k, t=t, bi=bi, ov_=ov_: e.matmul(
                                banks[bi][:, 0:c.WD], hT[:, k, t * 128:(t + 1) * 128], ov_[:, k, :],
                                start=(k == 0), stop=(k == c.MC - 1)),
                                reads=[b_hT[t], b_slot[s_i]], writes=[bb[bi]], signal=(k == c.MC - 1))
                        xs = xt[t][:, n * c.WD:(n + 1) * c.WD]
                        P.op("dve", lambda e, bi=bi, xs=xs: e.tensor_tensor(xs, banks[bi][:, 0:c.WD], xs, ALU.add),
                             reads=[bb[bi], b_xt[t]], writes=[b_xt[t]])
                F["ffn"](1)
                for t in range(4):
                    rstd, b_r = F["rstd_of"](xt[t][:], b_xt[t], D)
                    i = yc % 2
                    yc += 1
                    P.op("dve", lambda e, t=t, i=i, rstd=rstd: e.scalar_tensor_tensor(
                        yo[i][:], xt[t][:], rstd, gfin[:], ALU.mult, ALU.mult),
                        reads=[b_xt[t], b_r, b_gfin], writes=[b_yo[i]])
                    r0 = tok0 + t * 128
                    P.dma("pool", y[r0:r0 + 128, :], yo[i][:], s_yo[i], reads=[b_yo[i]], writes=[b_y])
        if getattr(cfg, 'dbg_dump', False):
            s_dd = P.new_sem("s_dd")
            b_dd = Buf("dd")
            for nm, t_, shp, dt_, bsrc in (("d_x1", x1_s, [c.NOWN, D], F32, b_x1s), ("d_h2", h2_s, [c.NTOK, D], BF16, b_h2s),
                                           ("d_cq", cqT_s, [128, NQ, c.NOWN], BF16, b_cqTs), ("d_ck", ckT_s, [128, NK, c.NTOK], BF16, b_ckTs),
                                           ("d_kr", krT_s, [64, c.NTOK], BF16, b_krTs), ("d_hg", hgT_s, [128, G, c.NOWN], BF16, b_hgTs),
                                           ("d_ho", hoT_s, [128, H, c.NOWN], BF16, b_hoTs)):
                od = nc.dram_tensor(nm, shp, dt_, kind="ExternalOutput").ap()
                P.dma("pool", od, t_.ap(), s_dd, reads=[bsrc], writes=[b_dd])
        P.barrier()
        P.emit()
    return nc


def rope_tables_T(pos):
    inv = (1.0 / (ROPE_THETA ** (np.arange(0, 64, 2, dtype=np.float32) / np.float32(64)))).astype(np.float32)
    ang = (pos.astype(np.float32)[:, None] * inv[None, :]).astype(np.float32)
    cos = np.cos(ang).astype(np.float32)
    sin = np.sin(ang).astype(np.float32)
    C2 = np.concatenate([cos, cos], axis=1).T
    S2 = np.concatenate([-sin, sin], axis=1).T
    return np.ascontiguousarray(C2), np.ascontiguousarray(S2)


def fmaj(v):
    v = np.asarray(v, np.float32).reshape(-1)
    return v.reshape(-1, 128).T


def make_in_maps(cfg, inp):
    c = cfg
    f = lambda a: np.ascontiguousarray(np.asarray(a, np.float32))
    gfm = np.concatenate([fmaj(inp["g_ffn1"][0]), fmaj(inp["g_mix"][0]), fmaj(inp["g_q"][0]), fmaj(inp["g_kv"][0]),
                          fmaj(np.concatenate([np.asarray(inp["g_out_attn"][0]), np.asarray(inp["g_out_gmlp"][0])])),
                          fmaj(inp["g_ffn2"][0])], axis=1)
    shared = {
        "w1g": f(inp["w1_gate"][0]), "w1u": f(inp["w1_up"][0]), "w1d": f(inp["w1_down"][0]),
        "w2g": f(inp["w2_gate"][0]), "w2u": f(inp["w2_up"][0]), "w2d": f(inp["w2_down"][0]),
        "win": f(inp["w_in"][0]), "wqb": f(inp["w_q_b"][0]), "wkvb": f(inp["w_kv_b"][0]), "wout": f(inp["w_out"][0]),
        "ws": f(inp["w_s"][0]), "gfm": f(gfm), "gvrow": f(np.asarray(inp["g_v"][0]).reshape(1, -1)),
        "gfinrow": f(np.asarray(inp["g_final"]).reshape(1, -1)), "bsT": f(np.asarray(inp["b_s"][0]).T),
        "ident": np.eye(128, dtype=np.float32),
    }
    xp = np.asarray(inp["x_prompt"], np.float32)[0]
    xs = np.asarray(inp["x_sample"], np.float32)
    maps = []
    for core in range(c.NCORE):
        ppos = (core * c.OWNP + np.arange(c.SP)) % c.SP
        xall = np.concatenate([xs[2 * core], xs[2 * core + 1], xp[ppos]], axis=0)
        pos = np.concatenate([np.arange(c.SA), np.arange(c.SA), ppos])
        C2, S2 = rope_tables_T(pos)
        m = dict(shared)
        m["xall"] = np.ascontiguousarray(xall)
        m["ropeC"] = C2
        m["ropeS"] = S2
        maps.append(m)
    return maps


def assemble(cfg, results):
    c = cfg
    yp = np.zeros((1, c.SP, c.D), np.float32)
    ys = np.zeros((2 * c.NCORE, c.SA, c.D), np.float32)
    for core in range(c.NCORE):
        yy = np.asarray(results[core]["y"], np.float32)
        ys[2 * core] = yy[0:c.SA]
        ys[2 * core + 1] = yy[c.SA:2 * c.SA]
        yp[0, core * c.OWNP:(core + 1) * c.OWNP] = yy[2 * c.SA:]
    return yp, ys


_NC_CACHE = {}


def run_cfg(cfg, inp):
    key = id(cfg)
    if key not in _NC_CACHE:
        _NC_CACHE[key] = build(cfg)
    nc = _NC_CACHE[key]
    maps = make_in_maps(cfg, inp)
    res = run_bass_kernel_spmd(nc, maps, core_ids=list(range(cfg.NCORE)))
    if getattr(cfg, 'dbg_dump', False):
        cfg.dbg_results = res.results
    return assemble(cfg, res.results)


def kernel(**inputs):
    return run_cfg(CFG_FULL, inputs)
```
